# Optimizing a Trainium2 kernel written in Bass

```python
import jax, jax.numpy as jnp
from jax import lax
import numpy as np

D_MODEL = 1024
BATCH = 8
SEQ = 4096
DEPTH = 1
DEC_BATCH = 32
DEC_SEQ = 1
PAST_LEN = 16384
PAGE_SIZE = 128

MIX_WIDTH = D_MODEL
DN_HEADS = 4
DN_DK = 128
DN_DV = 128
DN_QK = DN_HEADS * DN_DK
DN_V = DN_HEADS * DN_DV
CONV_CH = 2 * DN_QK + DN_V
CONV_W = 4
DN_CHUNK = 64
AT_HEADS = 8
AT_HEAD_DIM = 64
AT_WIDTH = AT_HEADS * AT_HEAD_DIM
BRANCHES = ((128, 1), (512, 4), (2048, 16))
WINDOW = 2048
D_FF = 4 * D_MODEL
IN_SIZES = (CONV_CH, DN_V, DN_HEADS, DN_HEADS, 3 * AT_WIDTH)
IN_DIM = CONV_CH + DN_V + 2 * DN_HEADS + 3 * AT_WIDTH
EPS = 1e-6

kernel_name = 'hymba_gdn_dilated_alibi_step'


def rmsnorm(x, w):
    xf = x.astype(jnp.float32)
    y = xf * lax.rsqrt(jnp.mean(xf * xf, axis=-1, keepdims=True) + EPS) * w.astype(jnp.float32)
    return y.astype(x.dtype)


def l2norm(x):
    return x * lax.rsqrt(jnp.sum(x * x, axis=-1, keepdims=True) + EPS)


def alibi_slopes(n):
    return 2.0 ** (-8.0 * jnp.arange(1, n + 1, dtype=jnp.float32) / n)


def _to_chunks(a, nc):
    N, Tp, H = a.shape[:3]
    a = a.reshape((N, nc, DN_CHUNK, H) + a.shape[3:])
    return jnp.moveaxis(a, (1, 3), (0, 2))


def gated_delta(q, k, v, beta, g, state0):
    N, T, H, Dk = q.shape
    Dv = v.shape[-1]
    nc = -(-T // DN_CHUNK)
    Tp = nc * DN_CHUNK
    p4 = ((0, 0), (0, Tp - T), (0, 0), (0, 0))
    p3 = ((0, 0), (0, Tp - T), (0, 0))
    q = _to_chunks(jnp.pad(q, p4), nc)
    k = _to_chunks(jnp.pad(k, p4), nc)
    v = _to_chunks(jnp.pad(v, p4), nc)
    beta = _to_chunks(jnp.pad(beta, p3), nc)
    g = _to_chunks(jnp.pad(g, p3), nc)
    gc = jnp.cumsum(g, axis=-1)
    ci = jnp.arange(DN_CHUNK)
    incl = ci[:, None] >= ci[None, :]
    strict = ci[:, None] > ci[None, :]
    decay = jnp.exp(jnp.where(incl, gc[..., :, None] - gc[..., None, :], -jnp.inf))
    kb = k * beta[..., None]
    m = jnp.where(strict, jnp.einsum('...ik,...jk->...ij', kb, k) * decay, 0.0)
    a = m + jnp.eye(DN_CHUNK, dtype=m.dtype)
    rhs = jnp.concatenate([v * beta[..., None], kb * jnp.exp(gc)[..., None]], axis=-1)
    sol = lax.linalg.triangular_solve(a, rhs, left_side=True, lower=True, unit_diagonal=True)
    u, w = sol[..., :Dv], sol[..., Dv:]
    qk = jnp.where(incl, jnp.einsum('...ik,...jk->...ij', q, k) * decay, 0.0)

    def step(S, inp):
        qc, kc, uc, wc, qkc, gcc = inp
        v_new = uc - jnp.einsum('nhck,nhkv->nhcv', wc, S)
        o = (jnp.einsum('nhck,nhkv->nhcv', qc * jnp.exp(gcc)[..., None], S)
             + jnp.einsum('nhij,nhjv->nhiv', qkc, v_new))
        gl = gcc[..., -1]
        S = (S * jnp.exp(gl)[..., None, None]
             + jnp.einsum('nhck,nhcv->nhkv', kc * jnp.exp(gl[..., None] - gcc)[..., None], v_new))
        return S, o

    state, o = lax.scan(step, state0, (q, k, u, w, qk, gc))
    o = jnp.moveaxis(o, (0, 2), (1, 3)).reshape(N, Tp, H, Dv)[:, :T]
    return o, state


def banded_attention(q, k, v, slopes, dil, band):
    N, L, H, Dh = q.shape
    nb = L // band
    qb = q.reshape(N, nb, band, H, Dh)
    pad = ((0, 0), (band, 0), (0, 0), (0, 0))
    kp = jnp.pad(k, pad).reshape(N, nb + 1, band, H, Dh)
    vp = jnp.pad(v, pad).reshape(N, nb + 1, band, H, Dh)
    kw = jnp.concatenate([kp[:, :-1], kp[:, 1:]], axis=2)
    vw = jnp.concatenate([vp[:, :-1], vp[:, 1:]], axis=2)
    s = jnp.einsum('nbqhd,nbkhd->nbhqk', qb, kw).astype(jnp.float32) * (AT_HEAD_DIM ** -0.5)
    qi = jnp.arange(band)[:, None] + band
    kj = jnp.arange(2 * band)[None, :]
    dist = qi - kj
    blk = jnp.arange(nb)[:, None, None]
    mask = (dist >= 0)[None] & (dist <= band)[None] & (blk * band + kj[None] - band >= 0)
    bias = -slopes[:, None, None] * (dil * dist).astype(jnp.float32)[None]
    s = jnp.where(mask[None, :, None], s + bias[None, None], -jnp.inf)
    lse = jax.nn.logsumexp(s, axis=-1)
    p = jnp.exp(s - lse[..., None])
    o = jnp.einsum('nbhqk,nbkhd->nbqhd', p, vw.astype(jnp.float32))
    return o.reshape(N, L, H, Dh), jnp.swapaxes(lse, 2, 3).reshape(N, L, H)


def combine_branches(outs, lses):
    wts = jax.nn.softmax(jnp.stack(lses, axis=0), axis=0)
    return jnp.einsum('gnth,gnthd->nthd', wts, jnp.stack(outs, axis=0))


def dilated_attn_prompt(q, k, v, slopes):
    N, T, H, Dh = q.shape
    outs, lses = [], []
    for win, dil in BRANCHES:
        band = win // dil
        span = dil * band
        Tp = -(-T // span) * span
        Q = Tp // dil

        def to_res(a):
            a = jnp.pad(a, ((0, 0), (0, Tp - T), (0, 0), (0, 0)))
            return a.reshape(N, Q, dil, H, Dh).transpose(0, 2, 1, 3, 4).reshape(N * dil, Q, H, Dh)

        o, lse = banded_attention(to_res(q), to_res(k), to_res(v), slopes, dil, band)
        o = o.reshape(N, dil, Q, H, Dh).transpose(0, 2, 1, 3, 4).reshape(N, Tp, H, Dh)[:, :T]
        lse = lse.reshape(N, dil, Q, H).transpose(0, 2, 1, 3).reshape(N, Tp, H)[:, :T]
        outs.append(o)
        lses.append(lse)
    return combine_branches(outs, lses)


def dilated_attn_sample(q, kc, vc, L, slopes):
    T = q.shape[1]
    outs, lses = [], []
    for win, dil in BRANCHES:
        band = win // dil
        r = jnp.arange(band + 1)
        idx = L + jnp.arange(T)[:, None] - r[None, :] * dil
        valid = idx >= 0
        idxc = jnp.maximum(idx, 0)
        kg = kc[:, idxc]
        vg = vc[:, idxc]
        s = jnp.einsum('nthd,ntrhd->nthr', q, kg).astype(jnp.float32) * (AT_HEAD_DIM ** -0.5)
        s = s - slopes[:, None] * (r * dil).astype(jnp.float32)[None, :]
        s = jnp.where(valid[None, :, None, :], s, -jnp.inf)
        lse = jax.nn.logsumexp(s, axis=-1)
        p = jnp.exp(s - lse[..., None])
        outs.append(jnp.einsum('nthr,ntrhd->nthd', p, vg.astype(jnp.float32)))
        lses.append(lse)
    return combine_branches(outs, lses)


def mixer(h, conv_prev, ssm_prev, kbuf, vbuf, w_in, conv_w, a_log, dt_bias, dn_norm, w_out, prompt):
    N, T, _ = h.shape
    proj = h @ w_in
    offs = [int(o) for o in np.cumsum(IN_SIZES)[:-1]]
    dn_qkv, dn_z, dn_b, dn_a, at_qkv = jnp.split(proj, offs, axis=-1)

    xc = jnp.concatenate([conv_prev.astype(dn_qkv.dtype), dn_qkv], axis=1)
    y = xc[:, 0:T] * conv_w[0]
    for i in range(1, CONV_W):
        y = y + xc[:, i:i + T] * conv_w[i]
    y = jax.nn.silu(y).astype(jnp.float32)
    conv_new = xc[:, T:]
    qd, kd, vd = jnp.split(y, [DN_QK, 2 * DN_QK], axis=-1)
    qd = l2norm(qd.reshape(N, T, DN_HEADS, DN_DK)) * (DN_DK ** -0.5)
    kd = l2norm(kd.reshape(N, T, DN_HEADS, DN_DK))
    vd = vd.reshape(N, T, DN_HEADS, DN_DV)
    beta = jax.nn.sigmoid(dn_b.astype(jnp.float32))
    g = -jnp.exp(a_log.astype(jnp.float32)) * jax.nn.softplus(dn_a.astype(jnp.float32) + dt_bias.astype(jnp.float32))
    od, ssm_new = gated_delta(qd, kd, vd, beta, g, ssm_prev.astype(jnp.float32))
    od = od * lax.rsqrt(jnp.mean(od * od, axis=-1, keepdims=True) + EPS) * dn_norm.astype(jnp.float32)
    od = od * jax.nn.silu(dn_z.astype(jnp.float32).reshape(N, T, DN_HEADS, DN_DV))
    od = od.reshape(N, T, DN_V).astype(h.dtype)

    qa, ka, va = [a.reshape(N, T, AT_HEADS, AT_HEAD_DIM) for a in jnp.split(at_qkv, 3, axis=-1)]
    slopes = alibi_slopes(AT_HEADS)
    if prompt:
        oa = dilated_attn_prompt(qa, ka, va, slopes)
        keep = min(WINDOW, T)
        k_new, v_new = ka[:, T - keep:], va[:, T - keep:]
    else:
        L = kbuf.shape[1]
        kc = jnp.concatenate([kbuf.astype(ka.dtype), ka], axis=1)
        vc = jnp.concatenate([vbuf.astype(va.dtype), va], axis=1)
        oa = dilated_attn_sample(qa, kc, vc, L, slopes)
        keep = min(WINDOW, L + T)
        k_new, v_new = kc[:, L + T - keep:], vc[:, L + T - keep:]
    oa = oa.reshape(N, T, AT_WIDTH).astype(h.dtype)

    out = jnp.concatenate([od, oa], axis=-1) @ w_out
    return out, conv_new, ssm_new.astype(h.dtype), k_new, v_new


def forward_group(x, conv_st, ssm_st, k_st, v_st, ln_mix, w_in, dn_conv_w, dn_a_log, dn_dt_bias,
                  dn_norm, w_out, ln_ffn, w_ffn_up, w_ffn_down, ln_final, prompt):
    N = x.shape[0]
    convs, ssms, ks, vs = [], [], [], []
    for l in range(DEPTH):
        if prompt:
            conv_prev = jnp.zeros((N, CONV_W - 1, CONV_CH), x.dtype)
            ssm_prev = jnp.zeros((N, DN_HEADS, DN_DK, DN_DV), jnp.float32)
            kb = None
            vb = None
        else:
            conv_prev, ssm_prev, kb, vb = conv_st[l], ssm_st[l], k_st[l], v_st[l]
        h = rmsnorm(x, ln_mix[l])
        mix, c_new, s_new, k_new, v_new = mixer(h, conv_prev, ssm_prev, kb, vb, w_in[l], dn_conv_w[l],
                                                dn_a_log[l], dn_dt_bias[l], dn_norm[l], w_out[l], prompt)
        x = x + mix
        h = rmsnorm(x, ln_ffn[l])
        x = x + jnp.square(jax.nn.relu(h @ w_ffn_up[l])) @ w_ffn_down[l]
        convs.append(c_new)
        ssms.append(s_new)
        ks.append(k_new)
        vs.append(v_new)
    y = rmsnorm(x, ln_final)
    return y, jnp.stack(convs), jnp.stack(ssms), jnp.stack(ks), jnp.stack(vs)


def setup_inputs(seed: int = 0) -> dict:
    key = jax.random.key(seed)
    ks = jax.random.split(key, 18)
    f32 = jnp.float32
    nrm = jax.random.normal
    buf = min(WINDOW, PAST_LEN)
    dt = jnp.exp(jax.random.uniform(ks[10], (DEPTH, DN_HEADS), f32, minval=np.log(1e-3), maxval=np.log(1e-1)))
    return {
        'x_prompt': nrm(ks[0], (BATCH, SEQ, D_MODEL), f32),
        'x_sample': nrm(ks[1], (DEC_BATCH, DEC_SEQ, D_MODEL), f32),
        'state_conv': nrm(ks[2], (DEPTH, DEC_BATCH, CONV_W - 1, CONV_CH), f32),
        'state_ssm': 0.05 * nrm(ks[3], (DEPTH, DEC_BATCH, DN_HEADS, DN_DK, DN_DV), f32),
        'cache_win_k': nrm(ks[4], (DEPTH, DEC_BATCH, buf, AT_HEADS, AT_HEAD_DIM), f32),
        'cache_win_v': nrm(ks[5], (DEPTH, DEC_BATCH, buf, AT_HEADS, AT_HEAD_DIM), f32),
        'ln_mix': 1.0 + 0.02 * nrm(ks[6], (DEPTH, D_MODEL), f32),
        'w_in': nrm(ks[7], (DEPTH, D_MODEL, IN_DIM), f32) * D_MODEL ** -0.5,
        'dn_conv_w': nrm(ks[8], (DEPTH, CONV_W, CONV_CH), f32) * CONV_W ** -0.5,
        'dn_a_log': jnp.log(jax.random.uniform(ks[9], (DEPTH, DN_HEADS), f32, minval=1.0, maxval=16.0)),
        'dn_dt_bias': dt + jnp.log(-jnp.expm1(-dt)),
        'dn_norm': 1.0 + 0.02 * nrm(ks[11], (DEPTH, DN_DV), f32),
        'w_out': nrm(ks[12], (DEPTH, MIX_WIDTH, D_MODEL), f32) * MIX_WIDTH ** -0.5,
        'ln_ffn': 1.0 + 0.02 * nrm(ks[13], (DEPTH, D_MODEL), f32),
        'w_ffn_up': nrm(ks[14], (DEPTH, D_MODEL, D_FF), f32) * D_MODEL ** -0.5,
        'w_ffn_down': nrm(ks[15], (DEPTH, D_FF, D_MODEL), f32) * D_FF ** -0.5,
        'ln_final': 1.0 + 0.02 * nrm(ks[16], (D_MODEL,), f32),
    }


def reference(x_prompt, x_sample, state_conv, state_ssm, cache_win_k, cache_win_v, ln_mix, w_in,
              dn_conv_w, dn_a_log, dn_dt_bias, dn_norm, w_out, ln_ffn, w_ffn_up, w_ffn_down, ln_final):
    y_prompt, p_conv, p_ssm, p_win_k, p_win_v = forward_group(
        x_prompt, None, None, None, None, ln_mix, w_in, dn_conv_w, dn_a_log, dn_dt_bias, dn_norm,
        w_out, ln_ffn, w_ffn_up, w_ffn_down, ln_final, True)
    y_sample, s_conv, s_ssm, s_win_k, s_win_v = forward_group(
        x_sample, state_conv, state_ssm, cache_win_k, cache_win_v, ln_mix, w_in, dn_conv_w, dn_a_log,
        dn_dt_bias, dn_norm, w_out, ln_ffn, w_ffn_up, w_ffn_down, ln_final, False)
    return (y_prompt, y_sample, p_conv, p_ssm, p_win_k, p_win_v, s_conv, s_ssm, s_win_k, s_win_v)
```

```python
from contextlib import ExitStack
import numpy as np
import concourse.bass as bass
import concourse.mybir as mybir
from concourse.bass_utils import run_bass_kernel_spmd

F32 = mybir.dt.float32
BF16 = mybir.dt.bfloat16
ALU = mybir.AluOpType
AF = mybir.ActivationFunctionType
AX = mybir.AxisListType

D = 1024
KC = 8
IN_DIM = 3592
DFF = 4096
EPS = 1e-6
BIGM = 30000.0


class _Op:
    __slots__ = ("eng", "fn", "waits", "seq", "needed", "incval", "is_dma", "key", "dmaval", "clock", "idx", "seg",
                 "preds", "opreds", "dur", "lat", "start", "finish", "nsucc", "succs", "npend", "bundle", "unit")


def _free_elems(ap):
    n = 1
    for s_ in list(ap.shape)[1:]:
        n *= int(s_)
    return n


class Prog:
    def __init__(self, nc):
        self.nc = nc
        self.stack = ExitStack()
        self.all = []
        self.last_w = {}
        self.readers = {}
        self.seg = 0
        self.cur_bundle = None
        self.nbundle = 0

    def bundle(self):
        prog = self

        class _B:
            def __enter__(self_):
                prog.nbundle += 1
                prog.cur_bundle = prog.nbundle

            def __exit__(self_, *a):
                prog.cur_bundle = None
        return _B()

    def sb(self, name, shape, dtype):
        return self.stack.enter_context(self.nc.sbuf_tensor(name, list(shape), dtype))

    def ps(self, name, shape, dtype):
        return self.stack.enter_context(self.nc.psum_tensor(name, list(shape), dtype))

    def _record(self, op, reads, writes):
        pr = [r for r in reads if r.startswith("psb")]
        if pr:
            reads = [r for r in reads if not r.startswith("psb")]
            writes = list(writes) + pr
        preds = {}
        for r in reads:
            d = self.last_w.get(r)
            if d is not None and d is not op:
                preds[id(d)] = d
        for w in writes:
            d = self.last_w.get(w)
            if d is not None and d is not op:
                preds[id(d)] = d
            for d in self.readers.get(w, ()):
                if d is not op:
                    preds[id(d)] = d
        op.preds = list(preds.values())
        op.opreds = []
        op.idx = len(self.all)
        op.seg = self.seg
        op.bundle = self.cur_bundle
        for r in reads:
            self.readers.setdefault(r, []).append(op)
        for w in writes:
            self.last_w[w] = op
            self.readers[w] = []
        self.all.append(op)

    def op(self, eng, fn, reads=(), writes=(), dur=0.5):
        o = _Op()
        o.eng = eng; o.fn = fn; o.waits = []; o.needed = False; o.is_dma = False
        o.incval = None; o.key = None; o.dmaval = None; o.dur = dur; o.lat = 0.0
        self._record(o, list(reads), list(writes))
        return o

    def dma(self, queue, out, in_, key, reads=(), writes=(), **kw):
        o = _Op()
        o.eng = queue; o.fn = (lambda e: e.dma_start(out=out, in_=in_, **kw))
        o.waits = []; o.needed = True; o.is_dma = True
        o.incval = None; o.key = key; o.dmaval = None
        nbytes = 1
        for s_ in list(out.shape):
            nbytes *= int(s_)
        nbytes *= 4 if out.dtype == F32 else 2
        o.dur = 1.5 if queue == "pool" else 0.12
        o.lat = 2.0 + nbytes / 120e3
        self._record(o, list(reads), list(writes))
        return o

    def group(self, ops):
        last = ops[-1]
        members = set(id(o) for o in ops)
        for res, o in list(self.last_w.items()):
            if id(o) in members:
                self.last_w[res] = last
        for res, lst in self.readers.items():
            if any(id(o) in members for o in lst):
                self.readers[res] = [o for o in lst if id(o) not in members] + [last]
        for a_, b_ in zip(ops[:-1], ops[1:]):
            b_.opreds.append(a_)

    def barrier(self):
        self.seg += 1

    def mm(self, out, lhsT, rhs, start, stop, reads, writes):
        n = _free_elems(out)
        d = 0.02 + n * 0.0003
        if lhsT.dtype == F32:
            d *= 4
        return self.op("pe", lambda e: e.matmul(out, lhsT=lhsT, rhs=rhs, start=start, stop=stop), reads, writes, d)

    def tr(self, out, in_, ident, reads, writes):
        d = 0.02 + _free_elems(out) * 0.0003
        if in_.dtype == F32:
            d *= 4
        return self.op("pe", lambda e: e.transpose(out, in_, ident), reads, writes, d)

    def act(self, out, in_, func, reads, writes, scale=None, bias=None, accum_out=None, eng="act"):
        kw = {}
        if scale is not None:
            kw["scale"] = scale
        if bias is not None:
            kw["bias"] = bias
        if accum_out is not None:
            kw["accum_out"] = accum_out
        d = 0.2 + _free_elems(out) * 0.00075
        return self.op(eng, lambda e: e.activation(out=out, in_=in_, func=func, **kw), reads, writes, d)

    def _dur(self, eng, out):
        n = _free_elems(out)
        if eng == "pool":
            return 0.25 + n * 0.002
        if eng == "act":
            return 0.2 + n * 0.00075
        return 0.12 + n * 0.00105

    def tt(self, eng, out, in0, in1, op, reads, writes):
        return self.op(eng, lambda e: e.tensor_tensor(out=out, in0=in0, in1=in1, op=op), reads, writes, self._dur(eng, out))

    def ts(self, eng, out, in0, s1, op0, reads, writes, s2=None, op1=None, accum_out=None):
        kw = {}
        if op1 is not None:
            kw["op1"] = op1
        if accum_out is not None:
            kw["accum_out"] = accum_out
        return self.op(eng, lambda e: e.tensor_scalar(out=out, in0=in0, scalar1=s1, scalar2=s2, op0=op0, **kw), reads, writes,
                       self._dur(eng, out))

    def stt(self, out, in0, scalar, in1, op0, op1, reads, writes):
        return self.op("dve", lambda e: e.scalar_tensor_tensor(out=out, in0=in0, scalar=scalar, in1=in1, op0=op0, op1=op1),
                       reads, writes, self._dur("dve", out))

    def cp(self, eng, out, in_, reads, writes):
        if eng == "act":
            return self.op("act", lambda e: e.activation(out=out, in_=in_, func=AF.Copy), reads, writes, self._dur("act", out))
        return self.op(eng, lambda e: e.tensor_copy(out=out, in_=in_), reads, writes, self._dur(eng, out))

    def memset(self, eng, out, val, writes):
        return self.op(eng, lambda e: e.memset(out, val), (), writes, self._dur(eng, out))

    def redsum(self, out, in_, reads, writes):
        return self.op("dve", lambda e: e.tensor_reduce(out=out, in_=in_, axis=AX.X, op=ALU.add), reads, writes,
                       self._dur("dve", in_))

    def recip(self, out, in_, reads, writes):
        return self.op("dve", lambda e: e.reciprocal(out=out, in_=in_), reads, writes, 0.2 + _free_elems(out) * 0.004)

    def _schedule(self, ops, t0):
        import heapq
        engs = ("pe", "act", "dve", "pool", "sp")
        inseg = set(id(o) for o in ops)
        units = []
        bmap = {}
        for o in ops:
            if o.bundle is not None and not o.is_dma:
                k = (o.bundle, o.eng)
                u = bmap.get(k)
                if u is None:
                    u = {"ops": [], "eng": o.eng, "idx": o.idx, "npend": 0, "succs": [], "preds": {}}
                    bmap[k] = u
                    units.append(u)
            else:
                u = {"ops": [], "eng": o.eng, "idx": o.idx, "npend": 0, "succs": [], "preds": {}}
                units.append(u)
            u["ops"].append(o)
            o.unit = u
        for u in units:
            for o in u["ops"]:
                for p in o.preds + o.opreds:
                    if id(p) in inseg and p.unit is not u:
                        u["preds"][id(p.unit)] = p.unit
        for u in units:
            u["npend"] = len(u["preds"])
            for pu in u["preds"].values():
                pu["succs"].append(u)
        for u in reversed(units):
            own = sum(o.dur + o.lat for o in u["ops"])
            u["cp"] = own + max([su["cp"] for su in u["succs"]] + [0.0])
        cand = {e: [] for e in engs}
        cnt = [0]
        for u in units:
            if u["npend"] == 0:
                heapq.heappush(cand[u["eng"]], (u["idx"], id(u), u))
        te = {e: t0 for e in engs}
        order = {e: [] for e in engs}
        left = len(units)
        XLAT = 0.15
        CPW = 8

        def ready(u):
            r = t0
            for o in u["ops"]:
                for p in o.preds:
                    if id(p) in inseg and p.unit is not u:
                        f = p.finish + (XLAT if p.eng != o.eng else 0.05)
                        if f > r:
                            r = f
            return r

        while left:
            best = None
            for e in engs:
                if cand[e] and (best is None or te[e] < te[best]):
                    best = e
            e = best
            t = te[e]
            look = heapq.nsmallest(48, cand[e])
            pick = None; pick_r = None; rdy = []
            for (ix, _, u) in look:
                r = ready(u)
                if r <= t:
                    rdy.append((u, r))
                    if len(rdy) >= CPW:
                        break
                elif not rdy and (pick_r is None or r < pick_r):
                    pick = u; pick_r = r
            if rdy:
                base = rdy[0][0]["idx"]
                pick, pick_r = max(rdy, key=lambda ur: (ur[0]["cp"], -ur[0]["idx"]))
            u = pick
            cand[e] = [c_ for c_ in cand[e] if c_[2] is not u]
            heapq.heapify(cand[e])
            tcur = max(t, pick_r)
            for o in u["ops"]:
                o.start = tcur
                if o.is_dma:
                    o.finish = tcur + o.dur + o.lat
                    tcur = tcur + o.dur
                else:
                    o.finish = tcur + o.dur
                    tcur = o.finish
                order[e].append(o)
            te[e] = tcur
            left -= 1
            for su in u["succs"]:
                su["npend"] -= 1
                if su["npend"] == 0:
                    heapq.heappush(cand[su["eng"]], (su["idx"], id(su), su))
        tend = max([t0] + [o.finish for o in ops])
        return order, tend

    def emit(self, verbose=True):
        nc = self.nc
        st = self.stack
        engs = ("pe", "act", "dve", "pool", "sp")
        nseg = self.seg + 1
        final = {e: [] for e in engs}
        t0 = 0.0
        seg_first = []
        for sg in range(nseg):
            ops = [o for o in self.all if o.seg == sg]
            order, t1 = self._schedule(ops, t0)
            seg_first.append({e: (order[e][0] if order[e] else None) for e in engs})
            for e in engs:
                final[e].extend(order[e])
            if verbose:
                print("  segment %d: %d ops, est %.0f us" % (sg, len(ops), t1 - t0))
            t0 = t1
        if verbose:
            print("  estimated total %.0f us" % t0)
        dma_cnt = {}
        for e in engs:
            for i, o in enumerate(final[e]):
                o.seq = i
                if o.is_dma:
                    dma_cnt[o.key] = dma_cnt.get(o.key, 0) + 16
                    o.dmaval = dma_cnt[o.key]
        firsts = {}
        for sg in range(1, nseg):
            lasts = {}
            dl = {}
            for e in engs:
                for o in final[e]:
                    if o.seg >= sg:
                        break
                    if o.is_dma:
                        dl[o.key] = o
                    else:
                        lasts[e] = o
            bar = list(lasts.values()) + list(dl.values())
            for e in engs:
                f = seg_first[sg][e]
                if f is not None:
                    firsts[id(f)] = bar
        clock = {e: {} for e in engs}
        glob = sorted(self.all, key=lambda o: (o.seg, o.start, o.idx))
        for o in glob:
            clk = clock[o.eng]
            deps = list(o.preds) + firsts.get(id(o), [])
            for d in deps:
                if d.is_dma:
                    if clk.get(d.key, 0) >= d.dmaval:
                        continue
                else:
                    if d.eng == o.eng and o.eng == "pe":
                        continue
                    if clk.get(d.eng, -1) >= d.seq:
                        continue
                o.waits.append(d)
                d.needed = True
                for k, v in d.clock.items():
                    if clk.get(k, -1) < v:
                        clk[k] = v
            c = dict(clk)
            if o.is_dma:
                c[o.key] = o.dmaval
            else:
                c[o.eng] = o.seq
            o.clock = c
        esem = {e: st.enter_context(nc.semaphore("s_" + e)) for e in engs}
        dsem = {k: st.enter_context(nc.semaphore("d_%d" % i)) for i, k in enumerate(dma_cnt)}
        for e in engs:
            cum = 0
            for o in final[e]:
                if not o.is_dma and o.needed:
                    cum += 1
                    o.incval = cum
        engh = {"pe": "tensor", "act": "scalar", "dve": "vector", "pool": "gpsimd", "sp": "sync"}
        block = st.enter_context(nc.Block())

        def mk(ename):
            lst = final[ename]

            def body(e):
                fin = {}
                for o in lst:
                    for d in o.waits:
                        if d.is_dma:
                            e.wait_ge(dsem[d.key], d.dmaval)
                        else:
                            e.wait_ge(esem[d.eng], d.incval)
                    ins = o.fn(e)
                    if o.is_dma:
                        ins.then_inc(dsem[o.key], 16)
                        fin[o.key] = o.dmaval
                    elif o.needed:
                        ins.then_inc(esem[ename], 1)
                for k, v in fin.items():
                    e.wait_ge(dsem[k], v)
            return body

        for ename in engs:
            if final[ename]:
                getattr(block, engh[ename])(mk(ename))
        st.close()


class Arena:
    def __init__(self, P, nbytes):
        self.t = P.sb("arena", [128, nbytes // 4], F32)
        self.n = nbytes
        self.lo = 0
        self.hi = nbytes
        self.peak = 0

    def alloc(self, shape, dt, persistent=False):
        n = 1
        for s_ in shape[1:]:
            n *= s_
        nb = (n * (4 if dt == F32 else 2) + 31) // 32 * 32
        if persistent:
            off = self.lo
            self.lo += nb
        else:
            self.hi -= nb
            off = self.hi
        assert self.lo <= self.hi, "SBUF arena overflow: lo=%d hi=%d" % (self.lo, self.hi)
        self.peak = max(self.peak, self.lo + (self.n - self.hi))
        v = self.t[0:shape[0], off // 4:(off + nb) // 4]
        if dt != F32:
            v = v.bitcast(dt)
        v = v[:, 0:n]
        if len(shape) == 3:
            v = v.rearrange("p (a b) -> p a b", a=shape[1])
        elif len(shape) == 4:
            v = v.rearrange("p (a b c) -> p a b c", a=shape[1], b=shape[2])
        return v

    def phase_reset(self):
        self.hi = self.n


class Ring:
    def __init__(self, A, name, n, shape, dtype, persistent=False):
        self.t = [A.alloc(shape, dtype, persistent) for i in range(n)]
        self.k = ["%s%d" % (name, i) for i in range(n)]
        self.i = 0

    def next(self):
        j = self.i % len(self.t)
        self.i += 1
        return self.t[j], self.k[j]


class PsPool:
    def __init__(self, tens, keys):
        self.t = tens; self.k = keys; self.i = 0

    def next(self):
        j = self.i % len(self.t)
        self.i += 1
        return self.t[j], self.k[j]


def run_pipeline(gens, max_active=1000):
    active = []
    gens = list(gens)
    gi = 0
    while gi < len(gens) or active:
        if gi < len(gens) and len(active) < max_active:
            active.append(gens[gi]); gi += 1
        nxt = []
        for g in active:
            try:
                next(g)
                nxt.append(g)
            except StopIteration:
                pass
        active = nxt


def alibi_slope(h):
    return 2.0 ** (-8.0 * (h + 1) / 8)


def host_consts(branches):
    c = {}
    c["ident"] = np.eye(128, dtype=np.float32)
    p = np.arange(128)[:, None]
    f = np.arange(128)[None, :]
    c["umat"] = (p <= f).astype(np.float32)
    c["maskA"] = np.where(f >= p, BIGM, 0.0).astype(np.float32)
    c["maskC"] = np.where(f < p, -BIGM, 0.0).astype(np.float32)
    bd32 = ((p // 32) == (f // 32)).astype(np.float32)
    bd64 = ((p // 64) == (f // 64)).astype(np.float32)
    c["bmask"] = np.concatenate([bd32, bd64 - bd32, 1.0 - bd64], axis=1)
    NEG = -1.0e5
    j = p; i = f
    dprev = np.where(i <= j, (i - j + 128).astype(np.float32), np.nan)
    dcur = np.where(i >= j, (i - j).astype(np.float32), np.nan)
    tab = np.zeros((8, len(branches), 128, 256), np.float32)
    for h in range(8):
        for g, (win, dil) in enumerate(branches):
            cc = -alibi_slope(h) * dil * 8.0
            a = np.where(np.isnan(dprev), NEG, cc * np.nan_to_num(dprev))
            b = np.where(np.isnan(dcur), NEG, cc * np.nan_to_num(dcur))
            tab[h, g, :, 0:128] = a
            tab[h, g, :, 128:256] = b
    sm_ = np.zeros((8, 512), np.float32)
    for h in range(8):
        sm_[h, h * 64:(h + 1) * 64] = 1.0
    c["smask"] = sm_
    sb_ = np.zeros((128, len(branches), 8), np.float32)
    for g, (win, dil) in enumerate(branches):
        for h in range(8):
            sb_[:, g, h] = -alibi_slope(h) * dil * (128 - np.arange(128))
    c["sbias"] = sb_.reshape(128, len(branches) * 8)
    c["dist"] = np.ascontiguousarray(tab.transpose(2, 0, 1, 3).reshape(128, 8 * len(branches) * 256))
    return c


WIN_GROUPS = [(0, 512), (512, 512), (1024, 512), (1536, 512), (2048, 8), (2056, 512), (2568, 512), (3080, 512)]
G_QD, G_KD, G_VD, G_Z, G_BA, G_QA, G_KA, G_VA = range(8)
ARENA_BYTES = 196 * 1024


def build(T=4096, branches=((128, 1), (512, 4), (2048, 16)), NS=4, LC=2048, phases=("p1", "p2", "p3", "p4", "ps"), dbg=()):
    nc = bass.Bass("TRN2", target_bir_lowering=False)
    P = Prog(nc)
    A = Arena(P, ARENA_BYTES)
    NBR = len(branches)
    WINDOW = max(w for w, _ in branches)
    KEEP = min(WINDOW, T)
    NT = T // 512

    def din(name, shape, dt=F32):
        return nc.dram_tensor(name, list(shape), dt, kind="ExternalInput").ap()

    def dout(name, shape, dt=F32):
        return nc.dram_tensor(name, list(shape), dt, kind="ExternalOutput").ap()

    def dscr(name, shape, dt=BF16):
        return nc.dram_tensor(name, list(shape), dt, kind="Internal").ap()

    def pa(shape, dt):
        return A.alloc(shape, dt, True)

    def ta(shape, dt):
        return A.alloc(shape, dt, False)

    def new_phase():
        P.barrier()
        A.phase_reset()

    x = din("x", [T, D]); xs_in = din("xs", [NS, D])
    sconv = din("sconv", [NS, 3, 1536]); sssm = din("sssm", [NS, 4, 128, 128])
    ck = din("ck", [NS, LC, 512]); cv = din("cv", [NS, LC, 512])
    ln_mix = din("ln_mix", [1, D]); w_in = din("w_in", [D, IN_DIM]); conv_w = din("conv_w", [4, 1536])
    a_log = din("a_log", [1, 4]); dt_bias = din("dt_bias", [1, 4]); dn_norm = din("dn_norm", [1, 128])
    w_out = din("w_out", [D, D]); ln_ffn = din("ln_ffn", [1, D]); w_up = din("w_up", [D, DFF])
    w_down = din("w_down", [DFF, D]); ln_final = din("ln_final", [1, D])
    c_ident = din("c_ident", [128, 128]); c_umat = din("c_umat", [128, 128])
    c_maskA = din("c_maskA", [128, 128]); c_maskC = din("c_maskC", [128, 128])
    c_dist = din("c_dist", [128, 8 * NBR * 256])
    c_bmask = din("c_bmask", [128, 384])
    c_smask = din("c_smask", [8, 512]); c_sbias = din("c_sbias", [128, NBR * 8])

    y = dout("y", [T, D]); ys_out = dout("ys", [NS, D])
    p_conv = dout("p_conv", [3, 1536]); p_ssm = dout("p_ssm", [4, 128, 128])
    p_k = dout("p_k", [KEEP, 512]); p_v = dout("p_v", [KEEP, 512])
    s_conv = dout("s_conv", [NS, 3, 1536]); s_ssm = dout("s_ssm", [NS, 4, 128, 128])
    s_k = dout("s_k", [NS, LC, 512]); s_v = dout("s_v", [NS, LC, 512])
    dbgout = {}
    dbgin = {}
    for name, shape in dbg:
        if name.startswith("in_"):
            dbgin[name] = din("dbg_" + name, shape)
        else:
            dbgout[name] = dout("dbg_" + name, shape)

    win_s = dscr("win_s", [8, 128, KC, 512])
    wout_s = dscr("wout_s", [2, 128, KC, 512])
    wup_s = dscr("wup_s", [8, 128, KC, 512])
    wdn_s = dscr("wdn_s", [8, 128, KC, 512])
    TC = T + 128
    cat_s = dscr("cat_s", [KC, 128, TC])
    cat_smp = dscr("cat_smp", [128, KC, NS])

    identb = pa([128, 128], BF16)
    identf = pa([128, 128], F32)
    umat = pa([128, 128], F32)
    maskA = pa([128, 128], F32)
    maskC = pa([128, 128], F32)
    onesb = pa([128, 128], BF16)
    onesf = pa([128, 128], F32)
    negonesf = pa([128, 128], F32)
    negonesb = pa([128, 128], BF16)
    lnm_bc = pa([128, D], F32)
    epsc = pa([128, 4], F32)
    bmask = pa([128, 3, 128], BF16)

    pst = [P.ps("psb%d" % i, [128, 512], F32) for i in range(8)]
    psk = ["psb%d" % i for i in range(8)]

    wring = Ring(A, "wr", 4, [128, KC, 512], BF16, True)
    stat = Ring(A, "stat", 6, [128, 4], F32, True)
    trp = PsPool([pst[7]], [psk[7]])
    H = {}

    def phase_helpers():
        H["xring"] = Ring(A, "xt", 2, [128, D], F32)
        H["hbring"] = Ring(A, "hb", 2, [128, D], BF16)
        H["junk"] = ta([128, D], BF16)

    g0 = [P.dma("sp", identf, c_ident, "C0", writes=["identf"]),
          P.dma("sp", umat, c_umat, "C0", writes=["umat"]),
          P.dma("sp", maskA, c_maskA, "C0", writes=["maskA"]),
          P.dma("sp", maskC, c_maskC, "C0", writes=["maskC"]),
          P.dma("sp", lnm_bc, ln_mix.partition_broadcast(128), "C0", writes=["lnm_bc"])]
    P.group(g0)
    g0 = [P.dma("pool", identb, c_ident, "C1", writes=["identb"]),
          P.dma("pool", bmask, c_bmask.rearrange("p (a f) -> p a f", a=3), "C1", writes=["bmask"])]
    P.group(g0)
    P.memset("dve", onesb, 1.0, ["onesb"])
    P.memset("dve", onesf, 1.0, ["onesf"])
    P.memset("dve", negonesf, -1.0, ["negonesf"])
    P.memset("dve", negonesb, -1.0, ["negonesb"])
    P.memset("dve", epsc, EPS, ["epsc"])

    cast_jobs = []
    wres = {}

    def add_cast(pieces, name):
        wres[name] = "%s_%d" % (name, len(pieces) - 1)
        cast_jobs.append((name, pieces))

    wv = w_in.rearrange("(kc p) c -> p kc c", p=128)
    for t in (G_QA, G_KA, G_VA, G_QD, G_KD, G_VD, G_Z, G_BA):
        off, n = WIN_GROUPS[t]
        add_cast([(win_s[t, :, kc, 0:n], wv[:, kc, off:off + n]) for kc in range(KC)], "win_s%d" % t)
    wv = w_out.rearrange("(kc p) c -> p kc c", p=128)
    for t in range(2):
        add_cast([(wout_s[t, :, kc, :], wv[:, kc, t * 512:(t + 1) * 512]) for kc in range(KC)], "wout_s%d" % t)
    wv = w_up.rearrange("(kc p) c -> p kc c", p=128)
    for t in range(8):
        add_cast([(wup_s[t, :, kc, :], wv[:, kc, t * 512:(t + 1) * 512]) for kc in range(KC)], "wup_s%d" % t)
    wv = w_down.rearrange("(fc p) c -> p fc c", p=128)
    for t in range(8):
        half, g = t // 4, t % 4
        add_cast([(wdn_s[t, :, fl, :], wv[:, g * 8 + fl, half * 512:(half + 1) * 512]) for fl in range(8)], "wdn_s%d" % t)

    cast_prev = [None]

    def cast_some(n):
        for _ in range(n):
            if not cast_jobs:
                return
            name, pieces = cast_jobs.pop(0)
            ops = []
            for i, (dst, src) in enumerate(pieces):
                rd = [cast_prev[0]] if (i == 0 and cast_prev[0]) else []
                ops.append(P.dma("pool", dst, src, "CAST", reads=rd, writes=["%s_%d" % (name, i)], max_dma_last_dim=2048))
            P.group(ops)
            cast_prev[0] = "%s_%d" % (name, len(pieces) - 1)

    def wload(scr, t, name, ncols=512):
        while any(j[0] == name for j in cast_jobs):
            cast_some(1)
        wt, wk = wring.next()
        P.dma("sp", wt[:, :, 0:ncols], scr[t, :, :, 0:ncols], "L" + wk, reads=[wres[name]], writes=[wk])
        return wt, wk

    cast_some(3)

    def bulk_copies():
        nel = (LC - 1) * 512
        for n in range(NS):
            for (src, dst) in ((ck, s_k), (cv, s_v)):
                sv = src[n].rearrange("l c -> (l c)")[512:512 + nel].rearrange("(a b) -> a b", a=16)
                dv = dst[n].rearrange("l c -> (l c)")[0:nel].rearrange("(a b) -> a b", a=16)
                P.dma("act", dv, sv, "OUTps")
        P.dma("act", s_conv[:, 0:2, :], sconv[:, 1:3, :], "OUTps")

    def rstd_act(st, sk, nt, scale):
        P.act(st[:nt, 1:2], st[:nt, 0:1], AF.Ln, [sk, "epsc"], [sk], scale=scale, bias=epsc[:nt, 0:1])
        P.act(st[:nt, 2:3], st[:nt, 1:2], AF.Exp, [sk], [sk], scale=-0.5)

    def norm_T(xt, xk, nt, lnw, lnk, dst, dstk):
        st, sk = stat.next()
        hb, hk = H["hbring"].next()
        junk = H["junk"]
        P.act(junk[:nt], xt[:nt], AF.Square, [xk], ["junk", sk], accum_out=st[:nt, 0:1])
        rstd_act(st, sk, nt, 1.0 / D)
        P.stt(hb[:nt], xt[:nt], st[:nt, 2:3], lnw[:nt], ALU.mult, ALU.mult, [xk, sk, lnk], [hk])
        pt, pk = trp.next()
        pb = pt[:].bitcast(BF16).rearrange("p (k n) -> p k n", k=8)
        for kc in range(KC):
            P.tr(pb[:, kc, 0:nt], hb[:nt, kc * 128:(kc + 1) * 128], identb[:nt, :nt], [hk, "identb"], [pk])
        P.cp("act", dst, pb[:, :, 0:nt], [pk], [dstk])

    def dbg_store(name, src_ap, reskey, dst_ap=None, key=None):
        if name in dbgout:
            P.dma("pool", dbgout[name] if dst_ap is None else dst_ap, src_ap, key or ("DBG" + name), reads=[reskey],
                  max_dma_last_dim=2048)

    if "p1" in phases:
        QKVT = [ta([128, 4, T], BF16) for i in range(3)]
        distb = ta([128, 8 * NBR * 256], BF16)
        P.dma("pool", distb, c_dist, "c8", writes=["distb"], max_dma_last_dim=2048)
        a_mark = A.hi
        phase_helpers()
        xring = H["xring"]
        hTr = Ring(A, "hT", 2, [128, KC, 512], BF16)
        kvst = Ring(A, "kvst", 2, [128, 512], F32)
        mmp = PsPool(pst[0:4], psk[0:4])
        ev = [0]

        def p1_tile(j):
            hT, hTk = hTr.next()
            for sub in range(4):
                xt, xk = xring.next()
                r0 = j * 512 + sub * 128
                P.dma("sp", xt, x[r0:r0 + 128, :], "L" + xk, writes=[xk])
                norm_T(xt, xk, 128, lnm_bc, "lnm_bc", hT[:, :, sub * 128:(sub + 1) * 128], hTk)
            yield
            for which in range(3):
                wt, wk = wload(win_s, G_QA + which, "win_s%d" % (G_QA + which))
                for c4 in range(4):
                    pt, pk = mmp.next()
                    for kc in range(KC):
                        P.mm(pt[:, 0:512], wt[:, kc, c4 * 128:(c4 + 1) * 128], hT[:, kc, :], kc == 0, kc == KC - 1,
                             [wk, hTk], [pk])
                    dstT = QKVT[which][:, c4, j * 512:(j + 1) * 512]
                    P.cp(("act", "dve")[ev[0] % 2], dstT, pt[:, 0:512], [pk], ["qkvt%d" % which]); ev[0] += 1
                if which >= 1:
                    for sub in range(4):
                        tok0 = j * 512 + sub * 128
                        if tok0 < T - KEEP:
                            continue
                        pt, pk = mmp.next()
                        for kc in range(KC):
                            P.mm(pt[:, 0:512], hT[:, kc, sub * 128:(sub + 1) * 128], wt[:, kc, :], kc == 0, kc == KC - 1,
                                 [wk, hTk], [pk])
                        st_, stk = kvst.next()
                        P.cp(("act", "dve")[ev[0] % 2], st_, pt[:, 0:512], [pk], [stk]); ev[0] += 1
                        o0 = tok0 - (T - KEEP)
                        P.dma("sp", (p_k, p_v)[which - 1][o0:o0 + 128, :], st_, "S" + stk, reads=[stk])
                if which < 2:
                    yield
            cast_some(1)

        run_pipeline([p1_tile(j) for j in range(NT)], max_active=2)
        for i in range(3):
            dbg_store("qkvt%d" % i, QKVT[i], "qkvt%d" % i)

    if "p2" in phases:
        P.barrier()
        if "ps" in phases:
            bulk_copies()
        A.hi = a_mark
        QT, KT, VT = QKVT
        oacc = ta([128, 2, T], F32)
        oast = Ring(A, "oast", 2, [128, 1024], BF16)
        ptr = Ring(A, "ptr", 6, [128, 256], BF16)
        vtr = Ring(A, "vtok", 4, [128, 128], BF16)
        stA = PsPool([pst[0], pst[2]], [psk[0], psk[2]])
        stB = PsPool([pst[1], pst[3]], [psk[1], psk[3]])
        pvp = PsPool([pst[4], pst[5], pst[7]], [psk[4], psk[5], psk[7]])
        vtp = PsPool([pst[6]], [psk[6]])
        vslots = {}

        def blk(tens, hp, d, r, b):
            return tens[:, hp, :].rearrange("p (n d) -> p n d", d=d)[:, b * 128:(b + 1) * 128, r]

        def unit(hp, g, d, r, b):
            vt, vk = vtr.next()
            vslots[(hp, g, r, b)] = (vt, vk)
            pv_, pvk = vtp.next()
            pvb = pv_[:].bitcast(BF16)
            bctx = P.bundle(); bctx.__enter__()
            P.tr(pvb[:, 0:128], blk(VT, hp, d, r, b), identb, ["qkvt2", "identb"], [pvk])
            banks = []
            qb = blk(QT, hp, d, r, b)
            for h in range(2):
                pt, pk = (stA, stB)[h].next()
                banks.append((pt, pk))
                doff = ((hp * 2 + h) * NBR + g) * 256
                P.mm(pt[:, 0:256], identb, distb[:, doff:doff + 256], True, False, ["identb", "distb"], [pk])
                rows = slice(h * 64, (h + 1) * 64)
                if b > 0:
                    P.mm(pt[:, 0:128], blk(KT, hp, d, r, b - 1)[rows], qb[rows], False, False, ["qkvt0", "qkvt1"], [pk])
                P.mm(pt[:, 128:256], blk(KT, hp, d, r, b)[rows], qb[rows], False, True, ["qkvt0", "qkvt1"], [pk])
            bctx.__exit__()
            yield
            P.cp("dve", vt, pvb[:, 0:128], [pvk], [vk])
            pts = []
            for h in range(2):
                pt, pk = banks[h]
                e_, ek = ptr.next()
                P.act(e_, pt[:, 0:256], AF.Exp, [pk], [ek], scale=0.125)
                pts.append((e_, ek))
            yield
            pc, pck = pvp.next()
            halves = ([(0, vslots[(hp, g, r, b - 1)])] if b > 0 else []) + [(1, (vt, vk))]
            bctx = P.bundle(); bctx.__enter__()
            for h in range(2):
                rows = slice(h * 64, (h + 1) * 64)
                e_, ek = pts[h]
                for grp in range(2):
                    for i, (half, (vv, vvk)) in enumerate(halves):
                        lhs = vv[:, h * 64:(h + 1) * 64] if grp == 0 else onesb[:, 0:64]
                        P.mm(pc[rows, grp * 128:(grp + 1) * 128], lhs, e_[:, half * 128:(half + 1) * 128],
                             i == 0, i == len(halves) - 1, [vvk, ek, "onesb"], [pck])
            bctx.__exit__()
            yield
            ov = oacc.rearrange("p a (n d) -> p a n d", d=d)[:, :, b * 128:(b + 1) * 128, r]
            pcv = pc[:, 0:256].rearrange("p (a q) -> p a q", a=2)
            if g == 0:
                P.cp("dve", ov, pcv, [pck], ["oacc"])
            else:
                P.tt("dve", ov, pcv, ov, ALU.add, [pck, "oacc"], ["oacc"])

        for hp in range(4):
            gens = []
            for g, (win, d) in enumerate(branches):
                nb = T // (128 * d)
                for r in range(d):
                    for b in range(nb):
                        gens.append(unit(hp, g, d, r, b))
            run_pipeline(gens)
            for c0 in range(0, T, 1024):
                n = min(1024, T - c0)
                P.recip(oacc[:, 1, c0:c0 + n], oacc[:, 1, c0:c0 + n], ["oacc"], ["oacc"])
                o_, ok_ = oast.next()
                P.tt("dve", o_[:, 0:n], oacc[:, 0, c0:c0 + n], oacc[:, 1, c0:c0 + n], ALU.mult, ["oacc"], [ok_])
                P.dma("sp", cat_s[4 + hp, :, c0:c0 + n], o_[:, 0:n], "S" + ok_, reads=[ok_],
                      writes=["cat%d_%d" % (4 + hp, tt_) for tt_ in range(c0 // 512, (c0 + n) // 512)])
                dbg_store("oaT", o_[:, 0:n], ok_, dbgout.get("oaT", [None] * 8)[hp, :, c0:c0 + n] if "oaT" in dbgout else None,
                          "DBGoaT%d_%d" % (hp, c0))
            cast_some(2)

    if "ps" in phases:
        new_phase()
        if "p2" not in phases:
            bulk_copies()
        phase_helpers()
        xring = H["xring"]
        pp = PsPool(pst[0:6], psk[0:6])
        trp.t = [pst[5]]; trp.k = [psk[5]]
        psNum, kNum = pst[6], psk[6]
        psDen, kDen = pst[7], psk[7]
        n4 = NS
        S_all = ta([128, NS * 4, 128], F32)
        Snew = ta([128, NS * 4, 128], F32)
        g0 = [P.dma("sp", S_all[:, n * 4:(n + 1) * 4, :], sssm[n].rearrange("h k v -> k h v"), "Lsall", writes=["S_all"])
              for n in range(NS)]
        P.group(g0)
        proj4 = ta([n4, IN_DIM], F32)
        hTs = ta([128, KC, n4], BF16)
        acc = ta([n4, 1536], F32)
        tmpc = ta([n4, 1536], F32)
        scr_ = Ring(A, "scr", 2, [n4, 1536], F32)
        cwr = Ring(A, "cwr", 2, [n4, 1536], F32)
        I4bc = ta([128, 4, 4], F32)
        a4 = ta([n4, 4], F32); dtb4 = ta([n4, 4], F32); nrm4 = ta([n4, 128], F32)
        sml = ta([n4, 16, 8], F32)
        smask = ta([8, 512], F32); sbias = ta([128, NBR * 8], F32)
        QKcol = ta([128, 8, 4], F32)
        Sel = ta([128, 8, 4, 4], F32)
        qS4 = ta([n4, 4, 128], F32); kS4 = ta([n4, 4, 128], F32)
        vnew4 = ta([n4, 4, 128], F32); o4 = ta([n4, 4, 128], F32); t4a = ta([n4, 4, 128], F32)
        ksel = Ring(A, "ksel", 2, [n4, 4, 128], F32)
        EgSel = ta([n4, 4, 4], F32); EgB = ta([128, 16], F32)
        RowSel = ta([n4, 4, 128], F32)
        z4 = ta([n4, 512], F32)
        cat4 = ta([n4, 1024], F32)
        kgr = Ring(A, "kg", 2, [128, 512], F32); vgr = Ring(A, "vg", 2, [128, 512], F32)
        Qbs = ta([128, 512], F32); prod = ta([128, 512], F32)
        sc8 = Ring(A, "sc8", 2, [128, 8], F32)
        Xm = Ring(A, "Xm", 2, [8, 512], F32)
        num4 = ta([n4, 512], F32); den4 = ta([n4, 8], F32)
        catb = ta([128, KC, n4], BF16)
        I4 = identf[0:n4, 0:n4]

        g0 = [P.dma("sp", a4, a_log.partition_broadcast(n4), "C2", writes=["a4"]),
              P.dma("sp", dtb4, dt_bias.partition_broadcast(n4), "C2", writes=["dtb4"]),
              P.dma("sp", nrm4, dn_norm.partition_broadcast(n4), "C2", writes=["nrm4"]),
              P.dma("sp", smask, c_smask, "C2", writes=["smask"]),
              P.dma("sp", sbias, c_sbias, "C2", writes=["sbias"])]
        P.group(g0)
        P.act(a4, a4, AF.Exp, ["a4"], ["a4"])
        P.ts("dve", a4, a4, -1.0, ALU.mult, ["a4"], ["a4"])
        P.memset("dve", I4bc, 0.0, ["I4bc"])
        for n in range(4):
            P.memset("dve", I4bc[:, n, n:n + 1], 1.0, ["I4bc"])
        P.tt("dve", RowSel, onesf[0:n4, :].unsqueeze(1).to_broadcast([n4, 4, 128]), I4.unsqueeze(2).to_broadcast([n4, 4, 128]),
             ALU.mult, ["onesf", "identf"], ["RowSel"])

        xt, xk = xring.next()
        P.dma("sp", xt[:n4], xs_in, "L" + xk, writes=[xk])
        norm_T(xt, xk, n4, lnm_bc, "lnm_bc", hTs, "hTs")
        for grp in range(8):
            off, n = WIN_GROUPS[grp]
            wt, wk = wload(win_s, grp, "win_s%d" % grp, ncols=n)
            pt, pk = pp.next()
            for kc in range(KC):
                P.mm(pt[0:n4, 0:n], hTs[:, kc, :], wt[:, kc, 0:n], kc == 0, kc == KC - 1, [wk, "hTs"], [pk])
            P.cp("act", proj4[:, off:off + n], pt[0:n4, 0:n], [pk], ["proj4"])
        g0 = [P.dma("sp", s_conv[:, 2, :], proj4[:, 0:1536], "OUTp2", reads=["proj4"]),
              P.dma("sp", s_k[:, LC - 1, :], proj4[:, 2568:3080], "OUTp2", reads=["proj4"]),
              P.dma("sp", s_v[:, LC - 1, :], proj4[:, 3080:3592], "OUTp2", reads=["proj4"])]
        P.group(g0)
        q4a = proj4[:, 2056:2568]; k4a = proj4[:, 2568:3080]; v4a = proj4[:, 3080:3592]

        cw_, cwk = cwr.next()
        P.dma("sp", cw_, conv_w[3:4, :].partition_broadcast(n4), "L" + cwk, writes=[cwk])
        P.tt("dve", acc, proj4[:, 0:1536], cw_, ALU.mult, ["proj4", cwk], ["acc"])
        for i in range(3):
            cw_, cwk = cwr.next(); sc_, sck = scr_.next()
            P.dma("sp", cw_, conv_w[i:i + 1, :].partition_broadcast(n4), "L" + cwk, writes=[cwk])
            P.dma("sp", sc_, sconv[:, i, :], "L" + sck, writes=[sck])
            P.tt("dve", tmpc, sc_, cw_, ALU.mult, [sck, cwk], ["tmpc"])
            P.tt("dve", acc, acc, tmpc, ALU.add, ["acc", "tmpc"], ["acc"])
        P.act(acc, acc, AF.Silu, ["acc"], ["acc"])
        P.act(z4, proj4[:, 1536:2048], AF.Silu, ["proj4"], ["z4"])
        qk = acc[:, 0:1024].rearrange("p (i d) -> p i d", i=8)
        v4 = acc[:, 1024:1536].rearrange("p (h d) -> p h d", h=4)
        P.tt("dve", tmpc[:, 0:1024], acc[:, 0:1024], acc[:, 0:1024], ALU.mult, ["acc"], ["tmpc"])
        P.redsum(sml[:, 0, :], tmpc[:, 0:1024].rearrange("p (i d) -> p i d", i=8), ["tmpc"], ["sml"])
        P.act(sml[:, 1, :], sml[:, 0, :], AF.Ln, ["sml", "epsc"], ["sml"], bias=epsc[0:n4, 0:1])
        P.act(sml[:, 1, :], sml[:, 1, :], AF.Exp, ["sml"], ["sml"], scale=-0.5)
        P.ts("dve", sml[:, 1, 0:4], sml[:, 1, 0:4], float(128.0 ** -0.5), ALU.mult, ["sml"], ["sml"])
        P.tt("dve", qk, qk, sml[:, 1, :].unsqueeze(2).to_broadcast([n4, 8, 128]), ALU.mult, ["acc", "sml"], ["acc"])
        beta4 = sml[:, 2, 0:4]; g4 = sml[:, 3, 0:4]; eg4 = sml[:, 4, 0:4]
        P.act(sml[:, 5, 0:4], proj4[:, 2048:2052], AF.Exp, ["proj4"], ["sml"], scale=-1.0)
        P.ts("dve", sml[:, 5, 0:4], sml[:, 5, 0:4], 1.0, ALU.add, ["sml"], ["sml"])
        P.recip(beta4, sml[:, 5, 0:4], ["sml"], ["sml"])
        P.tt("dve", g4, proj4[:, 2052:2056], dtb4, ALU.add, ["proj4", "dtb4"], ["sml"])
        P.act(g4, g4, AF.Exp, ["sml"], ["sml"])
        P.act(g4, g4, AF.Ln, ["sml", "onesf"], ["sml"], bias=onesf[0:n4, 0:1])
        P.tt("dve", g4, g4, a4, ALU.mult, ["sml", "a4"], ["sml"])
        P.act(eg4, g4, AF.Exp, ["sml"], ["sml"])

        def b4(ap):
            return ap.unsqueeze(2).to_broadcast([n4, 4, 128])

        pt, pk = pp.next()
        ptv = pt[:, 0:32].rearrange("p (i m) -> p i m", i=8)
        for i in range(8):
            P.tr(ptv[:, i, :], qk[:, i, :], I4, ["acc", "identf"], [pk])
        P.cp("dve", QKcol, ptv, [pk], ["QKcol"])
        P.tt("dve", Sel, QKcol.unsqueeze(2).to_broadcast([128, 8, 4, 4]), I4bc.unsqueeze(1).to_broadcast([128, 8, 4, 4]), ALU.mult,
             ["QKcol", "I4bc"], ["Sel"])
        for (i0, dstt, dk_) in ((0, qS4, "qS4"), (4, kS4, "kS4")):
            pt, pk = pp.next()
            for h in range(4):
                for n in range(NS):
                    P.mm(pt[0:n4, h * 128:(h + 1) * 128], Sel[:, i0 + h, n, :], S_all[:, n * 4 + h, :], n == 0, n == NS - 1,
                         ["Sel", "S_all"], [pk])
            P.cp("act", dstt, pt[0:n4, 0:512].rearrange("p (h d) -> p h d", h=4), [pk], [dk_])
        q4 = qk[:, 0:4, :]; k4 = qk[:, 4:8, :]
        P.tt("dve", t4a, kS4, b4(eg4), ALU.mult, ["kS4", "sml"], ["t4a"])
        P.tt("dve", t4a, v4, t4a, ALU.subtract, ["acc", "t4a"], ["t4a"])
        P.tt("dve", vnew4, t4a, b4(beta4), ALU.mult, ["t4a", "sml"], ["vnew4"])
        P.tt("dve", t4a, q4, k4, ALU.mult, ["acc"], ["t4a"])
        P.redsum(sml[:, 6, 0:4], t4a, ["t4a"], ["sml"])
        P.tt("dve", o4, qS4, b4(eg4), ALU.mult, ["qS4", "sml"], ["o4"])
        P.tt("dve", t4a, vnew4, b4(sml[:, 6, 0:4]), ALU.mult, ["vnew4", "sml"], ["t4a"])
        P.tt("dve", o4, o4, t4a, ALU.add, ["o4", "t4a"], ["o4"])
        P.tt("dve", EgSel, eg4.unsqueeze(1).to_broadcast([n4, 4, 4]), I4.unsqueeze(2).to_broadcast([n4, 4, 4]), ALU.mult,
             ["sml", "identf"], ["EgSel"])
        pt, pk = pp.next()
        P.mm(pt[:, 0:16], onesf[0:n4, :], EgSel.rearrange("p n h -> p (n h)"), True, True, ["onesf", "EgSel"], [pk])
        P.cp("dve", EgB, pt[:, 0:16], [pk], ["EgB"])
        for n in range(NS):
            ks_, ksk = ksel.next()
            P.ts("dve", ks_, k4, I4[:, n:n + 1], ALU.mult, ["acc", "identf"], [ksk])
            pt, pk = pp.next()
            for h in range(4):
                P.mm(pt[:, h * 128:(h + 1) * 128], ks_[:, h, :], vnew4[:, h, :], True, True, [ksk, "vnew4"], [pk])
            for h in range(4):
                nh = n * 4 + h
                P.stt(Snew[:, nh, :], S_all[:, nh, :], EgB[:, nh:nh + 1], pt[:, h * 128:(h + 1) * 128], ALU.mult, ALU.add,
                      ["S_all", "EgB", pk], ["Snew"])
        g0 = [P.dma("sp", s_ssm[n].rearrange("h k v -> k h v"), Snew[:, n * 4:(n + 1) * 4, :], "Sssm", reads=["Snew"])
              for n in range(NS)]
        P.group(g0)
        P.tt("dve", t4a, o4, o4, ALU.mult, ["o4"], ["t4a"])
        P.redsum(sml[:, 7, 0:4], t4a, ["t4a"], ["sml"])
        P.act(sml[:, 8, 0:4], sml[:, 7, 0:4], AF.Ln, ["sml", "epsc"], ["sml"], scale=1.0 / 128, bias=epsc[0:n4, 0:1])
        P.act(sml[:, 8, 0:4], sml[:, 8, 0:4], AF.Exp, ["sml"], ["sml"], scale=-0.5)
        od4 = cat4[:, 0:512].rearrange("p (h d) -> p h d", h=4)
        P.tt("dve", o4, o4, b4(sml[:, 8, 0:4]), ALU.mult, ["o4", "sml"], ["o4"])
        P.tt("dve", o4, o4, nrm4.unsqueeze(1).to_broadcast([n4, 4, 128]), ALU.mult, ["o4", "nrm4"], ["o4"])
        P.tt("dve", od4, o4, z4.rearrange("p (h d) -> p h d", h=4), ALU.mult, ["o4", "z4"], ["cat4"])

        P.tt("dve", tmpc[:, 0:512], q4a, k4a, ALU.mult, ["proj4"], ["tmpc"])
        P.redsum(sml[:, 9, :], tmpc[:, 0:512].rearrange("p (h d) -> p h d", h=8), ["tmpc"], ["sml"])
        P.act(sml[:, 10, :], sml[:, 9, :], AF.Exp, ["sml"], ["sml"], scale=0.125)
        P.ts("dve", sml[:, 10, :], sml[:, 10, :], float(NBR), ALU.mult, ["sml"], ["sml"])
        first = True
        for n in range(NS):
            pt, pk = pp.next()
            P.mm(pt[:, 0:512], RowSel[:, n, :], q4a, True, True, ["RowSel", "proj4"], [pk])
            P.cp("act", Qbs, pt[:, 0:512], [pk], ["Qbs"])
            for g, (win, d) in enumerate(branches):
                kg, kgk = kgr.next(); vg, vgk = vgr.next()
                rows = ck[n].rearrange("(a d) c -> a d c", d=d)[LC // d - 128:LC // d, 0, :]
                rows_v = cv[n].rearrange("(a d) c -> a d c", d=d)[LC // d - 128:LC // d, 0, :]
                P.dma("sp", kg, rows, "L" + kgk, writes=[kgk])
                P.dma("sp", vg, rows_v, "L" + vgk, writes=[vgk])
                P.tt("dve", prod, kg, Qbs, ALU.mult, [kgk, "Qbs"], ["prod"])
                s8, s8k = sc8.next()
                P.redsum(s8, prod.rearrange("p (h d) -> p h d", h=8), ["prod"], [s8k])
                P.stt(s8, s8, 0.125, sbias[:, g * 8:(g + 1) * 8], ALU.mult, ALU.add, [s8k, "sbias"], [s8k])
                P.act(s8, s8, AF.Exp, [s8k], [s8k])
                pt, pk = pp.next()
                P.mm(pt[0:8, 0:512], s8, vg, True, True, [s8k, vgk], [pk])
                xm, xmk = Xm.next()
                P.tt("dve", xm, pt[0:8, 0:512], smask, ALU.mult, [pk, "smask"], [xmk])
                last = (n == NS - 1 and g == NBR - 1)
                P.mm(psNum[0:n4, 0:512], I4bc[0:8, n, :], xm, first, last, ["I4bc", xmk], [kNum])
                P.mm(psDen[0:n4, 0:8], I4bc[:, n, :], s8, first, last, ["I4bc", s8k], [kDen])
                first = False
        P.cp("dve", num4, psNum[0:n4, 0:512], [kNum], ["num4"])
        P.cp("dve", den4, psDen[0:n4, 0:8], [kDen], ["den4"])
        e0b = sml[:, 10, :].unsqueeze(2).to_broadcast([n4, 8, 64])
        P.tt("dve", tmpc[:, 0:512].rearrange("p (h d) -> p h d", h=8), v4a.rearrange("p (h d) -> p h d", h=8), e0b, ALU.mult,
             ["proj4", "sml"], ["tmpc"])
        P.tt("dve", num4, num4, tmpc[:, 0:512], ALU.add, ["num4", "tmpc"], ["num4"])
        P.tt("dve", den4, den4, sml[:, 10, :], ALU.add, ["den4", "sml"], ["den4"])
        P.recip(den4, den4, ["den4"], ["den4"])
        P.tt("dve", cat4[:, 512:1024].rearrange("p (h d) -> p h d", h=8), num4.rearrange("p (h d) -> p h d", h=8),
             den4.unsqueeze(2).to_broadcast([n4, 8, 64]), ALU.mult, ["num4", "den4"], ["cat4"])
        pt, pk = pp.next()
        ptc = pt[:, 0:32].rearrange("p (c m) -> p c m", c=8)
        for c in range(8):
            P.tr(ptc[:, c, :], cat4[:, c * 128:(c + 1) * 128], I4, ["cat4", "identf"], [pk])
        P.cp("dve", catb, ptc, [pk], ["catb"])
        P.dma("sp", cat_smp, catb, "Scats", reads=["catb"], writes=["cats%d" % c for c in range(8)])
        dbg_store("cat4", cat4, "cat4")

    if "p3" in phases:
        new_phase()
        phase_helpers()
        xring = H["xring"]
        hTr = Ring(A, "hT3", 2, [128, KC, 512], BF16)
        stg = Ring(A, "cstg", 2, [128, 516], F32)
        halo = ta([128, 12, 4], F32)
        ybr = Ring(A, "yb", 2, [128, 512], F32)
        ysil = ta([128, 8, 512], F32)
        sqr = Ring(A, "sq", 2, [128, 512], BF16)
        lrs = Ring(A, "lrs", 2, [128, 512], F32)
        QKr = Ring(A, "QKn", 2, [128, 8, 512], BF16)
        VTr = Ring(A, "VTn", 2, [128, 4, 512], BF16)
        nwzr = Ring(A, "nwz", 2, [128, 4, 512], BF16)
        zsr = Ring(A, "zs", 2, [128, 512], F32)
        bar_ = Ring(A, "ba", 2, [128, 4, 8], F32)
        bgr = Ring(A, "bg", 2, [128, 3, 4, 4], F32)
        odTr = Ring(A, "odT", 2, [128, 4, 512], BF16)
        smr = Ring(A, "sm", 8, [128, 8, 4], F32)
        S32 = ta([128, 4, 128], F32)
        Sb = ta([128, 4, 128], BF16)
        cw = ta([128, 12, 4], F32)
        cwt = ysil[0:4, 0:3, :].rearrange("p a b -> p (a b)")
        nrm_bc = ta([128, 128], F32)
        nA_bc = ta([128, 4], F32)
        dtb_bc = ta([128, 4], F32)
        qsc = ta([128, 1], F32)
        wba = ta([128, KC, 8], BF16)
        pp = PsPool(pst[6:8], psk[6:8])
        trp.t = pp.t; trp.k = pp.k

        g0 = [P.dma("sp", cwt, conv_w, "C3", writes=["ysil"]),
              P.dma("sp", nrm_bc, dn_norm.partition_broadcast(128), "C3", writes=["nrm_bc"]),
              P.dma("sp", nA_bc, a_log.partition_broadcast(128), "C3", writes=["nA_bc"]),
              P.dma("sp", dtb_bc, dt_bias.partition_broadcast(128), "C3", writes=["dtb_bc"])]
        P.group(g0)
        P.act(nA_bc, nA_bc, AF.Exp, ["nA_bc"], ["nA_bc"])
        P.ts("dve", nA_bc, nA_bc, -1.0, ALU.mult, ["nA_bc"], ["nA_bc"])
        P.memset("dve", qsc, float(np.log(128.0 ** -0.5)), ["qsc"])
        P.memset("dve", halo, 0.0, ["halo"])
        P.memset("dve", S32, 0.0, ["S32"])
        P.memset("dve", Sb, 0.0, ["Sb"])
        pt, pk = pp.next()
        ptv = pt[:, 0:48].rearrange("p (c i) -> p c i", i=4)
        for cc in range(12):
            P.tr(ptv[:, cc, :], cwt[:, cc * 128:(cc + 1) * 128], identf[0:4, 0:4], ["ysil", "identf"], [pk])
        P.cp("dve", cw, ptv, [pk], ["cw"])
        while any(j_[0] == "win_s%d" % G_BA for j_ in cast_jobs):
            cast_some(1)
        P.dma("sp", wba, win_s[G_BA, :, :, 0:8], "c14", reads=[wres["win_s%d" % G_BA]], writes=["wba"])

        def bc_h(ap):
            return ap.unsqueeze(2).to_broadcast([128, 4, 128])

        def bc_m(ap):
            return ap.unsqueeze(1).to_broadcast([128, 4, 128])

        tiles = {}

        def prep(j):
            hT, hTk = hTr.next()
            QK, QKk = QKr.next(); VTn, VTk = VTr.next(); nwz, nwzk = nwzr.next()
            ba, bak = bar_.next(); bg, bgk = bgr.next()
            tiles[j] = (QK, QKk, VTn, VTk, nwz, nwzk, bg, bgk)
            for sub in range(4):
                xt, xk = xring.next()
                r0 = j * 512 + sub * 128
                P.dma("sp", xt, x[r0:r0 + 128, :], "L" + xk, writes=[xk])
                norm_T(xt, xk, 128, lnm_bc, "lnm_bc", hT[:, :, sub * 128:(sub + 1) * 128], hTk)
            yield
            for grp in range(3):
                wt, wk = wload(win_s, G_QD + grp, "win_s%d" % (G_QD + grp))
                for c4 in range(4):
                    cc = grp * 4 + c4
                    pt, pk = pp.next()
                    for kc in range(KC):
                        P.mm(pt[:, 0:512], wt[:, kc, c4 * 128:(c4 + 1) * 128], hT[:, kc, :], kc == 0, kc == KC - 1, [wk, hTk], [pk])
                    sg, sgk = stg.next()
                    P.cp("dve", sg[:, 0:3], halo[:, cc, 0:3], ["halo"], [sgk])
                    P.cp("act", sg[:, 3:515], pt[:, 0:512], [pk], [sgk])
                    P.cp("dve", halo[:, cc, 0:3], sg[:, 512:515], [sgk], ["halo"])
                    yb, ybk = ybr.next()
                    P.act(yb, sg[:, 0:512], AF.Identity, [sgk, "cw"], [ybk], scale=cw[:, cc, 0:1])
                    for i in range(1, 4):
                        P.stt(yb, sg[:, i:i + 512], cw[:, cc, i:i + 1], yb, ALU.mult, ALU.add, [sgk, "cw", ybk], [ybk])
                    if grp == 2:
                        P.act(VTn[:, c4, :], yb, AF.Silu, [ybk], [VTk])
                    else:
                        P.act(ysil[:, cc, :], yb, AF.Silu, [ybk], ["ysil"])
                yield
            if j == NT - 1:
                for grp in range(3):
                    pt, pk = pp.next()
                    for c4 in range(4):
                        P.tr(pt[0:3, c4 * 128:(c4 + 1) * 128], halo[:, grp * 4 + c4, 0:3], identf, ["halo", "identf"], [pk])
                    pz, pzk = zsr.next()
                    P.cp("dve", pz[0:3, :], pt[0:3, 0:512], [pk], [pzk])
                    P.dma("sp", p_conv[:, grp * 512:(grp + 1) * 512], pz[0:3, :], "S" + pzk, reads=[pzk])
            wt, wk = wload(win_s, G_Z, "win_s%d" % G_Z)
            for sub in range(4):
                pt, pk = pp.next()
                ts_ = slice(sub * 128, (sub + 1) * 128)
                for kc in range(KC):
                    P.mm(pt[:, 0:512], hT[:, kc, ts_], wt[:, kc, :], kc == 0, kc == KC - 1, [wk, hTk], [pk])
                zs, zsk = zsr.next()
                P.act(zs, pt[:, 0:512], AF.Silu, [pk], [zsk])
                P.tt("pool", nwz[:, :, ts_].rearrange("p h t -> p h t"), zs.rearrange("p (h v) -> p h v", h=4), bc_m(nrm_bc), ALU.mult,
                     [zsk, "nrm_bc"], [nwzk])
                pt2, pk2 = pp.next()
                for kc in range(KC):
                    P.mm(pt2[:, 0:8], hT[:, kc, ts_], wba[:, kc, :], kc == 0, kc == KC - 1, ["wba", hTk], [pk2])
                P.cp("dve", ba[:, sub, :], pt2[:, 0:8], [pk2], [bak])
            yield
            for cc in range(8):
                sq, sqk = sqr.next()
                P.tt("pool", sq, ysil[:, cc, :], ysil[:, cc, :], ALU.mult, ["ysil"], [sqk])
                pt, pk = pp.next()
                P.mm(pt[:, 0:512], onesb, sq, True, True, ["onesb", sqk], [pk])
                lr, lrk = lrs.next()
                P.act(lr, pt[:, 0:512], AF.Ln, [pk, "epsc"], [lrk], bias=epsc[:, 0:1])
                if cc < 4:
                    P.act(lr, lr, AF.Exp, [lrk, "qsc"], [lrk], scale=-0.5, bias=qsc[:, 0:1])
                else:
                    P.act(lr, lr, AF.Exp, [lrk], [lrk], scale=-0.5)
                P.tt("dve", QK[:, cc, :], ysil[:, cc, :], lr, ALU.mult, ["ysil", lrk], [QKk])
                if cc == 3:
                    yield
            yield
            P.act(bg[:, 2], ba[:, :, 0:4], AF.Exp, [bak], [bgk], scale=-1.0)
            P.ts("dve", bg[:, 2], bg[:, 2], 1.0, ALU.add, [bgk], [bgk])
            P.recip(bg[:, 0], bg[:, 2], [bgk], [bgk])
            P.tt("dve", bg[:, 1], ba[:, :, 4:8], dtb_bc.unsqueeze(1).to_broadcast([128, 4, 4]), ALU.add, [bak, "dtb_bc"], [bgk])
            P.act(bg[:, 1], bg[:, 1], AF.Exp, [bgk], [bgk])
            P.act(bg[:, 1], bg[:, 1], AF.Ln, [bgk, "onesf"], [bgk], bias=onesf[:, 0:1])
            P.tt("dve", bg[:, 1], bg[:, 1], nA_bc.unsqueeze(1).to_broadcast([128, 4, 4]), ALU.mult, [bgk, "nA_bc"], [bgk])

        KCTX = 2
        ctxs = []
        for ci in range(KCTX):
            cx = {}
            for nm in ("Mm", "Mo64", "Mo128", "Nn", "Qa", "Pa", "R0", "R1", "QKm", "QsT", "Kt", "rK", "vb"):
                cx[nm] = (ta([128, 4, 128], BF16), "cx%d_%s" % (ci, nm))
            for nm in ("gU", "aL", "aU", "eg"):
                cx[nm] = (ta([128, 4, 128], F32), "cx%d_%s" % (ci, nm))
            ctxs.append(cx)

        def v4(pt):
            return pt[:, 0:512].rearrange("p (h f) -> p h f", h=4)

        seq_done = {}

        def chunk(c):
            j, sub = c // 4, c % 4
            cs = slice(sub * 128, (sub + 1) * 128)
            ci = c % KCTX
            cx = ctxs[ci]
            if False:
                yield
            bA, bB, bC = pst[3 * ci], pst[3 * ci + 1], pst[3 * ci + 2]
            kA, kB, kC = psk[3 * ci], psk[3 * ci + 1], psk[3 * ci + 2]
            QK, QKk, VTn, VTk, nwz, nwzk, bg, bgk = tiles[j]
            qT = QK[:, 0:4, cs]; kT = QK[:, 4:8, cs]; vT = VTn[:, :, cs]
            beta = bg[:, 0, sub, :]; g = bg[:, 1, sub, :]
            sm, smk = smr.next()
            CbK = bC[:].bitcast(BF16)[:, 0:512].rearrange("p (h f) -> p h f", h=4)
            Gc = bC[:, 256:260]
            gU, gUk = cx["gU"]
            ghi, ghik = cx["Nn"]; glo, glok = cx["Pa"]
            P.tt("pool", gU, bc_m(umat), bc_h(g), ALU.mult, ["umat", bgk], [gUk])
            P.cp("pool", ghi, gU, [gUk], [ghik])
            P.tt("pool", glo, gU, ghi, ALU.subtract, [gUk, ghik], [glok])
            G2 = v4(bA); KKp = v4(bB)
            for h in range(4):
                P.mm(G2[:, h, :], onesb, ghi[:, h, :], True, False, ["onesb", ghik], [kA])
                P.mm(G2[:, h, :], onesb, glo[:, h, :], False, False, ["onesb", glok], [kA])
                P.mm(G2[:, h, :], ghi[:, h, :], negonesb, False, False, ["negonesb", ghik], [kA])
                P.mm(G2[:, h, :], glo[:, h, :], negonesb, False, True, ["negonesb", glok], [kA])
            for h in range(4):
                P.mm(KKp[:, h, :], kT[:, h, :], kT[:, h, :], True, True, [QKk], [kB])
            for h in range(4):
                P.tr(CbK[:, h, :], kT[:, h, :], identb, [QKk, "identb"], [kC])
            P.mm(Gc, umat, g, True, True, ["umat", bgk], [kC])
            yield
            P.cp("dve", sm[:, 0, :], Gc, [kC], [smk])
            aL, aLk = cx["aL"]; aU, aUk = cx["aU"]; eg, egk = cx["eg"]
            P.tt("dve", aL, G2, bc_m(maskA), ALU.add, [kA, "maskA"], [aLk])
            P.tt("dve", aU, G2, bc_m(maskC), ALU.add, [kA, "maskC"], [aUk])
            P.cp("dve", sm[:, 1, :], G2[:, :, 127], [kA], [smk])
            for h in range(4):
                P.act(eg[:, h, :], G2[:, h, :], AF.Exp, [kA, smk], [egk], bias=sm[:, 0, h:h + 1])
            P.act(aL, aL, AF.Exp, [aLk], [aLk], scale=-1.0)
            P.act(aU, aU, AF.Exp, [aUk], [aUk])
            P.act(sm[:, 2, :], sm[:, 1, :], AF.Exp, [smk], [smk])
            P.act(sm[:, 3, :], sm[:, 0, :], AF.Exp, [smk], [smk])
            P.tt("dve", sm[:, 4, :], sm[:, 3, :], beta, ALU.mult, [smk, bgk], [smk])
            P.tt("dve", sm[:, 5, :], sm[:, 1, :], sm[:, 0, :], ALU.add, [smk], [smk])
            P.act(sm[:, 5, :], sm[:, 5, :], AF.Exp, [smk], [smk])
            yield
            P.tt("pool", aL, aL, bc_h(beta), ALU.mult, [aLk, bgk], [aLk])
            Mm, Mk = cx["Mm"]; QKm, QKmk = cx["QKm"]; QsT, QsTk = cx["QsT"]
            rK, rKk = cx["rK"]; Kt, Ktk = cx["Kt"]; vb, vbk = cx["vb"]
            Mf, Mfk = cx["Qa"]
            Mo64, Mo64k = cx["Mo64"]; Mo128, Mo128k = cx["Mo128"]
            P.tt("dve", Mf, KKp, aL, ALU.mult, [kB, aLk], [Mfk])
            P.tt("pool", Mm, Mf, bc_m(bmask[:, 0, :]), ALU.mult, [Mfk, "bmask"], [Mk])
            P.tt("pool", Mo64, Mf, bc_m(bmask[:, 1, :]), ALU.mult, [Mfk, "bmask"], [Mo64k])
            P.tt("pool", Mo128, Mf, bc_m(bmask[:, 2, :]), ALU.mult, [Mfk, "bmask"], [Mo128k])
            ktk_, ktkk = cx["Pa"]
            P.cp("act", ktk_, CbK, [kC], [ktkk])
            P.tt("pool", rK, ktk_, bc_h(sm[:, 4, :]), ALU.mult, [ktkk, smk], [rKk])
            P.tt("pool", Kt, ktk_, bc_h(sm[:, 2, :]), ALU.mult, [ktkk, smk], [Ktk])
            P.tt("pool", QsT, qT, eg, ALU.mult, [QKk, egk], [QsTk])
            if c == 0:
                dbg_store("M", Mf, Mfk)
            yield
            Nb = bA[:].bitcast(BF16)[:, 0:512].rearrange("p (h f) -> p h f", h=4)
            QKp = v4(bB)
            for h in range(4):
                P.tr(Nb[:, h, :], Mm[:, h, :], identb, [Mk, "identb"], [kA])
            for h in range(4):
                P.mm(QKp[:, h, :], kT[:, h, :], qT[:, h, :], True, True, [QKk], [kB])
            for h in range(4):
                P.tr(CbK[:, h, :], vT[:, h, :], identb, [VTk, "identb"], [kC])
            yield
            Nn, Nk = cx["Nn"]
            Rbuf = [cx["R0"], cx["R1"]]
            Rr, Rk = Rbuf[0]
            P.cp("act", Nn, Nb, [kA], [Nk])
            P.tt("pool", Rr, bc_m(identb), Nn, ALU.subtract, ["identb", Nk], [Rk])
            P.tt("dve", QKm, QKp, aU, ALU.mult, [kB, aUk], [QKmk])
            P.tt("dve", vb, CbK, bc_h(beta), ALU.mult, [kC, bgk], [vbk])
            if c == 0:
                dbg_store("aL", aL, aLk); dbg_store("aU", aU, aUk); dbg_store("eg", eg, egk)
                dbg_store("QKm", QKm, QKmk); dbg_store("rK", rK, rKk); dbg_store("Kt", Kt, Ktk); dbg_store("vb", vb, vbk)
                dbg_store("qT", qT, QKk); dbg_store("kT", kT, QKk)
                dbg_store("sm", sm[:, 0:6, :], smk, dbgout["sm"][:, 0:6, :] if "sm" in dbgout else None); dbg_store("bg", bg, bgk)
            yield
            Pbuf = [cx["Nn"], cx["Pa"]]; Qbuf = [cx["Mm"], cx["Qa"]]
            Pc, Pck = Pbuf[0]; Qc, Qck = Qbuf[0]
            Qp = v4(bA); Pp = v4(bB); Rp = v4(bC)
            pend = None
            NLEV = 4
            for lev in range(1, NLEV + 2):
                if lev <= NLEV:
                    if lev < NLEV:
                        for h in range(4):
                            P.mm(Pp[:, h, :], Qc[:, h, :], Pc[:, h, :], True, True, [Qck, Pck], [kB])
                    for h in range(4):
                        P.mm(Qp[:, h, :], Pc[:, h, :], Qc[:, h, :], True, True, [Qck, Pck], [kA])
                if pend is not None:
                    for h in range(4):
                        P.mm(Rp[:, h, :], pend[0][:, h, :], Rr[:, h, :], True, True, [pend[1], Rk], [kC])
                yield
                if pend is not None:
                    Rn, Rnk = Rbuf[(lev - 1) % 2]
                    P.tt("dve", Rn, Rp, Rr, ALU.add, [kC, Rk], [Rnk])
                    Rr, Rk = Rn, Rnk
                    pend = None
                if lev <= NLEV:
                    Qn, Qnk = Qbuf[lev % 2]
                    P.cp("dve", Qn, Qp, [kA], [Qnk])
                    if lev < NLEV:
                        Pn, Pnk = Pbuf[lev % 2]
                        P.cp("act", Pn, Pp, [kB], [Pnk])
                        Pc, Pck = Pn, Pnk
                    Qc, Qck = Qn, Qnk
                    pend = (Qn, Qnk)
                yield
            RTb = bA[:].bitcast(BF16)[:, 0:512].rearrange("p (h f) -> p h f", h=4)
            Yp = v4(bB); Xp = v4(bC)
            for (Mo, Mok) in ((Mo64, Mo64k), (Mo128, Mo128k)):
                for h in range(4):
                    P.tr(RTb[:, h, :], Rr[:, h, :], identb, [Rk, "identb"], [kA])
                for h in range(4):
                    P.mm(Yp[:, h, :], Mo[:, h, :], Rr[:, h, :], True, True, [Mok, Rk], [kB])
                yield
                Rm, Rmk = cx["Nn"]; Yb, Ybk = cx["Pa"]
                P.cp("act", Rm, RTb, [kA], [Rmk])
                P.cp("act", Yb, Yp, [kB], [Ybk])
                yield
                for h in range(4):
                    P.mm(Xp[:, h, :], Rm[:, h, :], Yb[:, h, :], True, True, [Rmk, Ybk], [kC])
                yield
                Rn, Rnk = Rbuf[1] if Rr is Rbuf[0][0] else Rbuf[0]
                P.stt(Rn, Xp, -1.0, Rr, ALU.mult, ALU.add, [kC, Rk], [Rnk])
                Rr, Rk = Rn, Rnk
                yield
            Up = v4(bA); Wp = v4(bB)
            for h in range(4):
                P.mm(Up[:, h, :], Rr[:, h, :], vb[:, h, :], True, True, [Rk, vbk], [kA])
            for h in range(4):
                P.mm(Wp[:, h, :], rK[:, h, :], Rr[:, h, :], True, True, [Rk, rKk], [kB])
            yield
            us, usk = cx["gU"]; WT, WTk = cx["Qa"]
            P.cp("act", us, Up, [kA], [usk])
            P.cp("dve", WT, Wp, [kB], [WTk])
            if c == 0:
                dbg_store("R", Rr, Rk); dbg_store("us", us, usk); dbg_store("WT", WT, WTk)
            yield
            while c > 0 and not seq_done.get(c - 1):
                yield
            WSp = v4(bC)
            for h in range(4):
                P.mm(WSp[:, h, :], WT[:, h, :], Sb[:, h, :], True, True, [WTk, "Sb"], [kC])
            yield
            vn, vnk = cx["Nn"]
            P.stt(vn, WSp, -1.0, us, ALU.mult, ALU.add, [kC, usk], [vnk])
            yield
            Op = v4(bA); Snp = v4(bB)
            for h in range(4):
                P.mm(Op[:, h, :], QsT[:, h, :], Sb[:, h, :], True, False, [QsTk, "Sb"], [kA])
                P.mm(Op[:, h, :], QKm[:, h, :], vn[:, h, :], False, True, [QKmk, vnk], [kA])
            for h in range(4):
                P.mm(Snp[:, h, :], Kt[:, h, :], vn[:, h, :], True, True, [Ktk, vnk], [kB])
            yield
            P.tt("pool", S32, S32, bc_h(sm[:, 5, :]), ALU.mult, ["S32", smk], ["S32"])
            P.tt("dve", S32, Snp, S32, ALU.add, [kB, "S32"], ["S32"])
            P.cp("act", Sb, S32, ["S32"], ["Sb"])
            seq_done[c] = True
            junk = H["junk"]
            for h in range(4):
                P.act(junk[:, h * 128:(h + 1) * 128], Op[:, h, :], AF.Square, [kA], ["junk", smk], accum_out=sm[:, 6, h:h + 1])
            P.act(sm[:, 7, :], sm[:, 6, :], AF.Ln, [smk, "epsc"], [smk], scale=1.0 / 128, bias=epsc[:, 0:1])
            P.act(sm[:, 7, :], sm[:, 7, :], AF.Exp, [smk], [smk], scale=-0.5)
            t1, t1k = cx["aL"]
            P.tt("dve", t1, Op, bc_h(sm[:, 7, :]), ALU.mult, [kA, smk], [t1k])
            od, odk = cx["Pa"]
            P.tt("pool", od, t1, nwz[:, :, cs], ALU.mult, [t1k, nwzk], [odk])
            yield
            Tb = bC[:].bitcast(BF16)[:, 0:512].rearrange("p (h f) -> p h f", h=4)
            for h in range(4):
                P.tr(Tb[:, h, :], od[:, h, :], identb, [odk, "identb"], [kC])
            yield
            if sub == 0:
                tiles[("odT", j)] = odTr.next()
            oT, oTk = tiles[("odT", j)]
            P.cp("act", oT[:, :, cs], Tb, [kC], [oTk])
            if sub == 3:
                P.dma("sp", cat_s.rearrange("c p t -> p c t")[:, 0:4, j * 512:(j + 1) * 512], oT, "S" + oTk, reads=[oTk],
                      writes=["cat%d_%d" % (h, j) for h in range(4)])
                for h in range(4):
                    dbg_store("odT", oT[:, h, :], oTk, dbgout["odT"][h, :, j * 512:(j + 1) * 512] if "odT" in dbgout else None,
                              "DBGodT%d_%d" % (h, j))
            if c == T // 128 - 1:
                P.dma("sp", p_ssm.rearrange("h k v -> k h v"), S32, "Spssm", reads=["S32"])

        NCH = T // 128
        PSTEP = 5
        next_prep = 0; prep_gen = None; cur_prep = -1; prep_done = set()
        chunk_next = 0; active = []; done_chunks = set(); step = 0
        while len(done_chunks) < NCH:
            if prep_gen is None and next_prep < NT and all(cc_ in done_chunks for cc_ in range(0, 4 * (next_prep - 1))):
                prep_gen = prep(next_prep); cur_prep = next_prep; next_prep += 1
            while (len(active) < KCTX and chunk_next < NCH and (chunk_next // 4) in prep_done
                   and (chunk_next < KCTX or (chunk_next - KCTX) in done_chunks)):
                active.append((chunk_next, chunk(chunk_next))); chunk_next += 1
            nxt = []
            for (cid, g_) in active:
                try:
                    next(g_)
                    nxt.append((cid, g_))
                except StopIteration:
                    done_chunks.add(cid)
            active = nxt
            if prep_gen is not None and (not active or step % PSTEP == 0):
                try:
                    next(prep_gen)
                except StopIteration:
                    prep_done.add(cur_prep); prep_gen = None
            step += 1

    if "p4" in phases:
        new_phase()
        phase_helpers()
        xring = H["xring"]; junk = H["junk"]
        lnf_bc = ta([128, D], F32)
        lnz_bc = ta([128, D], F32)
        g0 = [P.dma("sp", lnf_bc, ln_ffn.partition_broadcast(128), "C4", writes=["lnf_bc"]),
              P.dma("sp", lnz_bc, ln_final.partition_broadcast(128), "C4", writes=["lnz_bc"])]
        P.group(g0)
        if "in_cat" in dbgin:
            for c in range(8):
                P.dma("pool", cat_s[c], dbgin["in_cat"][c], "DBGcat%d" % c,
                      writes=["cat%d_%d" % (c, j) for j in range(NT)], max_dma_last_dim=2048)
                P.dma("pool", cat_smp[:, c, :], dbgin["in_cat"][c, :, T:T + NS], "DBGcats%d" % c, writes=["cats%d" % c])
        for i_ in range(2):
            wring.t.append(ta([128, KC, 512], BF16)); wring.k.append("wrx%d" % i_)
        catr = Ring(A, "cat", 2, [128, KC, 512], BF16)
        x1r = Ring(A, "x1", 2, [128, 4, D], F32)
        h2r = Ring(A, "h2T", 2, [128, KC, 512], BF16)
        rr = Ring(A, "relu", 3, [128, 512], F32)
        aT = ta([128, 32, 512], BF16)
        mixp = PsPool(pst[0:3], psk[0:3])
        cat_v = cat_s.rearrange("c p t -> p c t")

        def p4_tile(tok0, ntok, xsrc, ydst, catkeys):
            subs = [(s0, min(128, ntok - s0)) for s0 in range(0, ntok, 128)]
            ct, ctk = catr.next(); x1, x1k = x1r.next(); h2, h2k = h2r.next()
            if tok0 >= T:
                P.dma("sp", ct[:, :, 0:ntok], cat_smp, "L" + ctk, reads=catkeys, writes=[ctk])
            else:
                P.dma("sp", ct[:, :, 0:ntok], cat_v[:, :, tok0:tok0 + ntok], "L" + ctk, reads=catkeys, writes=[ctk])
            wos = [wload(wout_s, half, "wout_s%d" % half) for half in range(2)]
            for si, (s0, nt) in enumerate(subs):
                xt, xk = xring.next()
                P.dma("sp", xt[:nt], xsrc[s0:s0 + nt, :], "L" + xk, writes=[xk])
                for half in range(2):
                    wt, wk = wos[half]
                    pt, pk = mixp.next()
                    for kc in range(KC):
                        P.mm(pt[:nt, 0:512], ct[:, kc, s0:s0 + nt], wt[:, kc, :], kc == 0, kc == KC - 1, [ctk, wk], [pk])
                    cs = slice(half * 512, (half + 1) * 512)
                    P.tt("dve", x1[:nt, si, cs], pt[:nt, 0:512], xt[:nt, cs], ALU.add, [pk, xk], [x1k + "_%d" % si])
                norm_T(x1[:, si, :], x1k + "_%d" % si, nt, lnf_bc, "lnf_bc", h2[:, :, s0:s0 + nt], h2k)
            yield
            for g in range(8):
                wt, wk = wload(wup_s, g, "wup_s%d" % g)
                for fl in range(4):
                    pt, pk = mixp.next()
                    for kc in range(KC):
                        P.mm(pt[:, 0:ntok], wt[:, kc, fl * 128:(fl + 1) * 128], h2[:, kc, 0:ntok], kc == 0, kc == KC - 1,
                             [wk, h2k], [pk])
                    r_, rk = rr.next()
                    P.act(r_[:, 0:ntok], pt[:, 0:ntok], AF.Relu, [pk], [rk])
                    P.tt("pool", aT[:, g * 4 + fl, 0:ntok], r_[:, 0:ntok], r_[:, 0:ntok], ALU.mult, [rk], ["aT"])
            for half in range(2):
                cs = slice(half * 512, (half + 1) * 512)
                for g in range(4):
                    wt, wk = wload(wdn_s, half * 4 + g, "wdn_s%d" % (half * 4 + g))
                    for fl in range(8):
                        fc = g * 8 + fl
                        for si, (s0, nt) in enumerate(subs):
                            P.mm(pst[3 + si][:nt, 0:512], aT[:, fc, s0:s0 + nt], wt[:, fl, :], fc == 0, fc == 31,
                                 [wk, "aT"], [psk[3 + si]])
                for si, (s0, nt) in enumerate(subs):
                    P.tt("dve", x1[:nt, si, cs], pst[3 + si][:nt, 0:512], x1[:nt, si, cs], ALU.add,
                         [psk[3 + si], x1k + "_%d" % si], [x1k + "_%d" % si])
            for si, (s0, nt) in enumerate(subs):
                st, sk = stat.next()
                k1 = x1k + "_%d" % si
                P.act(junk[:nt], x1[:nt, si, :], AF.Square, [k1], ["junk", sk], accum_out=st[:nt, 0:1])
                rstd_act(st, sk, nt, 1.0 / D)
                P.stt(x1[:nt, si, :], x1[:nt, si, :], st[:nt, 2:3], lnz_bc[:nt], ALU.mult, ALU.mult, [k1, sk, "lnz_bc"], [k1])
            yo = [P.dma("sp", ydst[s0:s0 + nt, :], x1[:nt, si, :], "S" + x1k, reads=[x1k + "_%d" % si])
                  for si, (s0, nt) in enumerate(subs)]
            P.group(yo)

        gens = []
        for j in range(NT):
            gens.append(p4_tile(j * 512, 512, x[j * 512:(j + 1) * 512, :], y[j * 512:(j + 1) * 512, :],
                                ["cat%d_%d" % (c, j) for c in range(8)]))
        if "ps" in phases or "p4s" in phases:
            gens.append(p4_tile(T, NS, xs_in, ys_out, ["cats%d" % c for c in range(8)]))
        run_pipeline(gens)

    print("SBUF peak bytes/partition:", A.peak, "ops:", len(P.all))
    P.emit()
    return nc


_BRANCHES = ((128, 1), (512, 4), (2048, 16))
_NC_CACHE = {}


def kernel(x_prompt, x_sample, state_conv, state_ssm, cache_win_k, cache_win_v, ln_mix, w_in, dn_conv_w, dn_a_log,
           dn_dt_bias, dn_norm, w_out, ln_ffn, w_up=None, w_down=None, ln_final=None, w_ffn_up=None, w_ffn_down=None,
           _phases=("p1", "p2", "p3", "p4", "ps")):
    if w_up is None:
        w_up = w_ffn_up
    if w_down is None:
        w_down = w_ffn_down
    f = lambda a: np.ascontiguousarray(np.asarray(a, dtype=np.float32))
    x_prompt = f(x_prompt); x_sample = f(x_sample)
    B, T, _ = x_prompt.shape
    NSALL = x_sample.shape[0]
    NCORE = 8
    NS = NSALL // NCORE
    LC = cache_win_k.shape[2]
    consts = host_consts(_BRANCHES)
    nc = build(T=T, branches=_BRANCHES, NS=NS, LC=LC, phases=_phases)
    shared = {
        "ln_mix": f(ln_mix).reshape(1, D), "w_in": f(w_in).reshape(D, IN_DIM), "conv_w": f(dn_conv_w).reshape(4, 1536),
        "a_log": f(dn_a_log).reshape(1, 4), "dt_bias": f(dn_dt_bias).reshape(1, 4), "dn_norm": f(dn_norm).reshape(1, 128),
        "w_out": f(w_out).reshape(D, D), "ln_ffn": f(ln_ffn).reshape(1, D), "w_up": f(w_up).reshape(D, DFF),
        "w_down": f(w_down).reshape(DFF, D), "ln_final": f(ln_final).reshape(1, D),
    }
    for k, v in consts.items():
        shared["c_" + k] = v
    sc = f(state_conv)[0]; ss = f(state_ssm)[0]
    ckk = f(cache_win_k)[0].reshape(NSALL, LC, 512); cvv = f(cache_win_v)[0].reshape(NSALL, LC, 512)
    in_maps = []
    for c in range(NCORE):
        m = dict(shared)
        sl = slice(c * NS, (c + 1) * NS)
        m["x"] = x_prompt[c]
        m["xs"] = np.ascontiguousarray(x_sample[sl, 0, :])
        m["sconv"] = np.ascontiguousarray(sc[sl]); m["sssm"] = np.ascontiguousarray(ss[sl])
        m["ck"] = np.ascontiguousarray(ckk[sl]); m["cv"] = np.ascontiguousarray(cvv[sl])
        in_maps.append(m)
    res = run_bass_kernel_spmd(nc, in_maps, core_ids=list(range(NCORE))).results
    KEEP = min(2048, T)
    cat = lambda name: np.stack([np.asarray(r[name]) for r in res], axis=0)
    y_prompt = cat("y")
    y_sample = cat("ys").reshape(NSALL, 1, D)
    p_conv = cat("p_conv")[None]
    p_ssm = cat("p_ssm")[None]
    p_k = cat("p_k").reshape(1, B, KEEP, 8, 64)
    p_v = cat("p_v").reshape(1, B, KEEP, 8, 64)
    s_conv = cat("s_conv").reshape(1, NSALL, 3, 1536)
    s_ssm = cat("s_ssm").reshape(1, NSALL, 4, 128, 128)
    s_k = cat("s_k").reshape(1, NSALL, LC, 8, 64)
    s_v = cat("s_v").reshape(1, NSALL, LC, 8, 64)
    return (y_prompt, y_sample, p_conv, p_ssm, p_k, p_v, s_conv, s_ssm, s_k, s_v)
```

```python
from contextlib import ExitStack
import numpy as np
import concourse.bass as bass
import concourse.mybir as mybir
from concourse.bass_utils import run_bass_kernel_spmd

F32 = mybir.dt.float32
BF16 = mybir.dt.bfloat16
ALU = mybir.AluOpType
AF = mybir.ActivationFunctionType
AX = mybir.AxisListType

D = 1024
KC = 8
IN_DIM = 3592
DFF = 4096
EPS = 1e-6
BIGM = 30000.0


class _Op:
    __slots__ = ("eng", "fn", "waits", "seq", "needed", "incval", "is_dma", "key", "dmaval", "clock", "idx", "seg",
                 "preds", "opreds", "dur", "lat", "start", "finish", "nsucc", "succs", "npend", "bundle", "unit")


def _free_elems(ap):
    n = 1
    for s_ in list(ap.shape)[1:]:
        n *= int(s_)
    return n


class Prog:
    def __init__(self, nc):
        self.nc = nc
        self.stack = ExitStack()
        self.all = []
        self.last_w = {}
        self.readers = {}
        self.seg = 0
        self.cur_bundle = None
        self.nbundle = 0

    def bundle(self):
        prog = self

        class _B:
            def __enter__(self_):
                prog.nbundle += 1
                prog.cur_bundle = prog.nbundle

            def __exit__(self_, *a):
                prog.cur_bundle = None
        return _B()

    def sb(self, name, shape, dtype):
        return self.stack.enter_context(self.nc.sbuf_tensor(name, list(shape), dtype))

    def ps(self, name, shape, dtype):
        return self.stack.enter_context(self.nc.psum_tensor(name, list(shape), dtype))

    def _record(self, op, reads, writes):
        pr = [r for r in reads if r.startswith("psb")]
        if pr:
            reads = [r for r in reads if not r.startswith("psb")]
            writes = list(writes) + pr
        preds = {}
        for r in reads:
            d = self.last_w.get(r)
            if d is not None and d is not op:
                preds[id(d)] = d
        for w in writes:
            d = self.last_w.get(w)
            if d is not None and d is not op:
                preds[id(d)] = d
            for d in self.readers.get(w, ()):
                if d is not op:
                    preds[id(d)] = d
        op.preds = list(preds.values())
        op.opreds = []
        op.idx = len(self.all)
        op.seg = self.seg
        op.bundle = self.cur_bundle
        for r in reads:
            self.readers.setdefault(r, []).append(op)
        for w in writes:
            self.last_w[w] = op
            self.readers[w] = []
        self.all.append(op)

    def op(self, eng, fn, reads=(), writes=(), dur=0.5):
        o = _Op()
        o.eng = eng; o.fn = fn; o.waits = []; o.needed = False; o.is_dma = False
        o.incval = None; o.key = None; o.dmaval = None; o.dur = dur; o.lat = 0.0
        self._record(o, list(reads), list(writes))
        return o

    def dma(self, queue, out, in_, key, reads=(), writes=(), **kw):
        o = _Op()
        o.eng = queue; o.fn = (lambda e: e.dma_start(out=out, in_=in_, **kw))
        o.waits = []; o.needed = True; o.is_dma = True
        o.incval = None; o.key = key; o.dmaval = None
        nbytes = 1
        for s_ in list(out.shape):
            nbytes *= int(s_)
        nbytes *= 4 if out.dtype == F32 else 2
        o.dur = 1.5 if queue == "pool" else 0.12
        o.lat = 2.0 + nbytes / 120e3
        self._record(o, list(reads), list(writes))
        return o

    def group(self, ops):
        last = ops[-1]
        members = set(id(o) for o in ops)
        for res, o in list(self.last_w.items()):
            if id(o) in members:
                self.last_w[res] = last
        for res, lst in self.readers.items():
            if any(id(o) in members for o in lst):
                self.readers[res] = [o for o in lst if id(o) not in members] + [last]
        for a_, b_ in zip(ops[:-1], ops[1:]):
            b_.opreds.append(a_)

    def barrier(self):
        self.seg += 1

    def mm(self, out, lhsT, rhs, start, stop, reads, writes):
        n = _free_elems(out)
        d = 0.035 + n * 0.00043
        if lhsT.dtype == F32:
            d *= 4
        return self.op("pe", lambda e: e.matmul(out, lhsT=lhsT, rhs=rhs, start=start, stop=stop), reads, writes, d)

    def tr(self, out, in_, ident, reads, writes):
        d = 0.035 + _free_elems(out) * 0.00043
        if in_.dtype == F32:
            d *= 4
        return self.op("pe", lambda e: e.transpose(out, in_, ident), reads, writes, d)

    def act(self, out, in_, func, reads, writes, scale=None, bias=None, accum_out=None, eng="act"):
        kw = {}
        if scale is not None:
            kw["scale"] = scale
        if bias is not None:
            kw["bias"] = bias
        if accum_out is not None:
            kw["accum_out"] = accum_out
        d = 0.2 + _free_elems(out) * 0.00075
        return self.op(eng, lambda e: e.activation(out=out, in_=in_, func=func, **kw), reads, writes, d)

    def _dur(self, eng, out):
        n = _free_elems(out)
        if eng == "pool":
            return 0.25 + n * 0.002
        if eng == "act":
            return 0.2 + n * 0.00075
        return 0.12 + n * 0.00105

    def tt(self, eng, out, in0, in1, op, reads, writes):
        return self.op(eng, lambda e: e.tensor_tensor(out=out, in0=in0, in1=in1, op=op), reads, writes, self._dur(eng, out))

    def ts(self, eng, out, in0, s1, op0, reads, writes, s2=None, op1=None, accum_out=None):
        kw = {}
        if op1 is not None:
            kw["op1"] = op1
        if accum_out is not None:
            kw["accum_out"] = accum_out
        return self.op(eng, lambda e: e.tensor_scalar(out=out, in0=in0, scalar1=s1, scalar2=s2, op0=op0, **kw), reads, writes,
                       self._dur(eng, out))

    def stt(self, out, in0, scalar, in1, op0, op1, reads, writes):
        return self.op("dve", lambda e: e.scalar_tensor_tensor(out=out, in0=in0, scalar=scalar, in1=in1, op0=op0, op1=op1),
                       reads, writes, self._dur("dve", out))

    def cp(self, eng, out, in_, reads, writes):
        if eng == "act":
            return self.op("act", lambda e: e.activation(out=out, in_=in_, func=AF.Copy), reads, writes, self._dur("act", out))
        return self.op(eng, lambda e: e.tensor_copy(out=out, in_=in_), reads, writes, self._dur(eng, out))

    def memset(self, eng, out, val, writes):
        return self.op(eng, lambda e: e.memset(out, val), (), writes, self._dur(eng, out))

    def redsum(self, out, in_, reads, writes):
        return self.op("dve", lambda e: e.tensor_reduce(out=out, in_=in_, axis=AX.X, op=ALU.add), reads, writes,
                       self._dur("dve", in_))

    def recip(self, out, in_, reads, writes):
        return self.op("dve", lambda e: e.reciprocal(out=out, in_=in_), reads, writes, 0.2 + _free_elems(out) * 0.004)

    def _schedule(self, ops, t0):
        import heapq
        engs = ("pe", "act", "dve", "pool", "sp")
        inseg = set(id(o) for o in ops)
        units = []
        bmap = {}
        for o in ops:
            if o.bundle is not None and not o.is_dma:
                k = (o.bundle, o.eng)
                u = bmap.get(k)
                if u is None:
                    u = {"ops": [], "eng": o.eng, "idx": o.idx, "npend": 0, "succs": [], "preds": {}}
                    bmap[k] = u
                    units.append(u)
            else:
                u = {"ops": [], "eng": o.eng, "idx": o.idx, "npend": 0, "succs": [], "preds": {}}
                units.append(u)
            u["ops"].append(o)
            o.unit = u
        for u in units:
            for o in u["ops"]:
                for p in o.preds + o.opreds:
                    if id(p) in inseg and p.unit is not u:
                        u["preds"][id(p.unit)] = p.unit
        for u in units:
            u["npend"] = len(u["preds"])
            for pu in u["preds"].values():
                pu["succs"].append(u)
        for u in reversed(units):
            own = sum(o.dur + o.lat for o in u["ops"])
            u["cp"] = own + max([su["cp"] for su in u["succs"]] + [0.0])
        cand = {e: [] for e in engs}
        cnt = [0]
        for u in units:
            if u["npend"] == 0:
                heapq.heappush(cand[u["eng"]], (u["idx"], id(u), u))
        te = {e: t0 for e in engs}
        order = {e: [] for e in engs}
        left = len(units)
        XLAT = 0.15
        CPW = 8

        def ready(u):
            r = t0
            for o in u["ops"]:
                for p in o.preds:
                    if id(p) in inseg and p.unit is not u:
                        f = p.finish + (XLAT if p.eng != o.eng else 0.05)
                        if f > r:
                            r = f
            return r

        while left:
            best = None
            for e in engs:
                if cand[e] and (best is None or te[e] < te[best]):
                    best = e
            e = best
            t = te[e]
            look = heapq.nsmallest(48, cand[e])
            pick = None; pick_r = None; rdy = []
            for (ix, _, u) in look:
                r = ready(u)
                if r <= t:
                    rdy.append((u, r))
                    if len(rdy) >= CPW:
                        break
                elif not rdy and (pick_r is None or r < pick_r):
                    pick = u; pick_r = r
            if rdy:
                base = rdy[0][0]["idx"]
                pick, pick_r = max(rdy, key=lambda ur: (ur[0]["cp"], -ur[0]["idx"]))
            u = pick
            cand[e] = [c_ for c_ in cand[e] if c_[2] is not u]
            heapq.heapify(cand[e])
            tcur = max(t, pick_r)
            for o in u["ops"]:
                o.start = tcur
                if o.is_dma:
                    o.finish = tcur + o.dur + o.lat
                    tcur = tcur + o.dur
                else:
                    o.finish = tcur + o.dur
                    tcur = o.finish
                order[e].append(o)
            te[e] = tcur
            left -= 1
            for su in u["succs"]:
                su["npend"] -= 1
                if su["npend"] == 0:
                    heapq.heappush(cand[su["eng"]], (su["idx"], id(su), su))
        tend = max([t0] + [o.finish for o in ops])
        return order, tend

    def emit(self, verbose=True):
        nc = self.nc
        st = self.stack
        engs = ("pe", "act", "dve", "pool", "sp")
        nseg = self.seg + 1
        final = {e: [] for e in engs}
        t0 = 0.0
        seg_first = []
        for sg in range(nseg):
            ops = [o for o in self.all if o.seg == sg]
            order, t1 = self._schedule(ops, t0)
            seg_first.append({e: (order[e][0] if order[e] else None) for e in engs})
            for e in engs:
                final[e].extend(order[e])
            if verbose:
                print("  segment %d: %d ops, est %.0f us" % (sg, len(ops), t1 - t0))
            t0 = t1
        if verbose:
            print("  estimated total %.0f us" % t0)
        dma_cnt = {}
        for e in engs:
            for i, o in enumerate(final[e]):
                o.seq = i
                if o.is_dma:
                    dma_cnt[o.key] = dma_cnt.get(o.key, 0) + 16
                    o.dmaval = dma_cnt[o.key]
        firsts = {}
        for sg in range(1, nseg):
            lasts = {}
            dl = {}
            for e in engs:
                for o in final[e]:
                    if o.seg >= sg:
                        break
                    if o.is_dma:
                        dl[o.key] = o
                    else:
                        lasts[e] = o
            bar = list(lasts.values()) + list(dl.values())
            for e in engs:
                f = seg_first[sg][e]
                if f is not None:
                    firsts[id(f)] = bar
        clock = {e: {} for e in engs}
        glob = sorted(self.all, key=lambda o: (o.seg, o.start, o.idx))
        for o in glob:
            clk = clock[o.eng]
            deps = list(o.preds) + firsts.get(id(o), [])
            for d in deps:
                if d.is_dma:
                    if clk.get(d.key, 0) >= d.dmaval:
                        continue
                else:
                    if d.eng == o.eng and o.eng == "pe":
                        continue
                    if clk.get(d.eng, -1) >= d.seq:
                        continue
                o.waits.append(d)
                d.needed = True
                for k, v in d.clock.items():
                    if clk.get(k, -1) < v:
                        clk[k] = v
            c = dict(clk)
            if o.is_dma:
                c[o.key] = o.dmaval
            else:
                c[o.eng] = o.seq
            o.clock = c
        esem = {e: st.enter_context(nc.semaphore("s_" + e)) for e in engs}
        dsem = {k: st.enter_context(nc.semaphore("d_%d" % i)) for i, k in enumerate(dma_cnt)}
        for e in engs:
            cum = 0
            for o in final[e]:
                if not o.is_dma and o.needed:
                    cum += 1
                    o.incval = cum
        engh = {"pe": "tensor", "act": "scalar", "dve": "vector", "pool": "gpsimd", "sp": "sync"}
        block = st.enter_context(nc.Block())

        def mk(ename):
            lst = final[ename]

            def body(e):
                fin = {}
                for o in lst:
                    for d in o.waits:
                        if d.is_dma:
                            e.wait_ge(dsem[d.key], d.dmaval)
                        else:
                            e.wait_ge(esem[d.eng], d.incval)
                    ins = o.fn(e)
                    if o.is_dma:
                        ins.then_inc(dsem[o.key], 16)
                        fin[o.key] = o.dmaval
                    elif o.needed:
                        ins.then_inc(esem[ename], 1)
                for k, v in fin.items():
                    e.wait_ge(dsem[k], v)
            return body

        for ename in engs:
            if final[ename]:
                getattr(block, engh[ename])(mk(ename))
        st.close()


class Arena:
    def __init__(self, P, nbytes):
        self.t = P.sb("arena", [128, nbytes // 4], F32)
        self.n = nbytes
        self.lo = 0
        self.hi = nbytes
        self.peak = 0

    def alloc(self, shape, dt, persistent=False):
        n = 1
        for s_ in shape[1:]:
            n *= s_
        nb = (n * (4 if dt == F32 else 2) + 31) // 32 * 32
        if persistent:
            off = self.lo
            self.lo += nb
        else:
            self.hi -= nb
            off = self.hi
        assert self.lo <= self.hi, "SBUF arena overflow: lo=%d hi=%d" % (self.lo, self.hi)
        self.peak = max(self.peak, self.lo + (self.n - self.hi))
        v = self.t[0:shape[0], off // 4:(off + nb) // 4]
        if dt != F32:
            v = v.bitcast(dt)
        v = v[:, 0:n]
        if len(shape) == 3:
            v = v.rearrange("p (a b) -> p a b", a=shape[1])
        elif len(shape) == 4:
            v = v.rearrange("p (a b c) -> p a b c", a=shape[1], b=shape[2])
        return v

    def phase_reset(self):
        self.hi = self.n


class Ring:
    def __init__(self, A, name, n, shape, dtype, persistent=False):
        self.t = [A.alloc(shape, dtype, persistent) for i in range(n)]
        self.k = ["%s%d" % (name, i) for i in range(n)]
        self.i = 0

    def next(self):
        j = self.i % len(self.t)
        self.i += 1
        return self.t[j], self.k[j]


class PsPool:
    def __init__(self, tens, keys):
        self.t = tens; self.k = keys; self.i = 0

    def next(self):
        j = self.i % len(self.t)
        self.i += 1
        return self.t[j], self.k[j]


def run_pipeline(gens, max_active=1000):
    active = []
    gens = list(gens)
    gi = 0
    while gi < len(gens) or active:
        if gi < len(gens) and len(active) < max_active:
            active.append(gens[gi]); gi += 1
        nxt = []
        for g in active:
            try:
                next(g)
                nxt.append(g)
            except StopIteration:
                pass
        active = nxt


def alibi_slope(h):
    return 2.0 ** (-8.0 * (h + 1) / 8)


def host_consts(branches):
    c = {}
    c["ident"] = np.eye(128, dtype=np.float32)
    p = np.arange(128)[:, None]
    f = np.arange(128)[None, :]
    c["umat"] = (p <= f).astype(np.float32)
    c["maskA"] = np.where(f >= p, BIGM, 0.0).astype(np.float32)
    c["maskC"] = np.where(f < p, -BIGM, 0.0).astype(np.float32)
    bd32 = ((p // 32) == (f // 32)).astype(np.float32)
    bd64 = ((p // 64) == (f // 64)).astype(np.float32)
    c["bmask"] = np.concatenate([bd32, bd64 - bd32, 1.0 - bd64], axis=1)
    NEG = -1.0e5
    j = p; i = f
    dprev = np.where(i <= j, (i - j + 128).astype(np.float32), np.nan)
    dcur = np.where(i >= j, (i - j).astype(np.float32), np.nan)
    tab = np.zeros((8, len(branches), 128, 256), np.float32)
    for h in range(8):
        for g, (win, dil) in enumerate(branches):
            cc = -alibi_slope(h) * dil * 8.0
            a = np.where(np.isnan(dprev), NEG, cc * np.nan_to_num(dprev))
            b = np.where(np.isnan(dcur), NEG, cc * np.nan_to_num(dcur))
            tab[h, g, :, 0:128] = a
            tab[h, g, :, 128:256] = b
    sm_ = np.zeros((8, 512), np.float32)
    for h in range(8):
        sm_[h, h * 64:(h + 1) * 64] = 1.0
    c["smask"] = sm_
    sb_ = np.zeros((128, len(branches), 8), np.float32)
    for g, (win, dil) in enumerate(branches):
        for h in range(8):
            sb_[:, g, h] = -alibi_slope(h) * dil * (128 - np.arange(128))
    c["sbias"] = sb_.reshape(128, len(branches) * 8)
    c["dist"] = np.ascontiguousarray(tab.transpose(2, 0, 1, 3).reshape(128, 8 * len(branches) * 256))
    return c


WIN_GROUPS = [(0, 512), (512, 512), (1024, 512), (1536, 512), (2048, 8), (2056, 512), (2568, 512), (3080, 512)]
G_QD, G_KD, G_VD, G_Z, G_BA, G_QA, G_KA, G_VA = range(8)
ARENA_BYTES = 196 * 1024


def build(T=4096, branches=((128, 1), (512, 4), (2048, 16)), NS=4, LC=2048, phases=("p1", "p2", "p3", "p4", "ps"), dbg=()):
    nc = bass.Bass("TRN2", target_bir_lowering=False)
    P = Prog(nc)
    A = Arena(P, ARENA_BYTES)
    NBR = len(branches)
    WINDOW = max(w for w, _ in branches)
    KEEP = min(WINDOW, T)
    NT = T // 512

    def din(name, shape, dt=F32):
        return nc.dram_tensor(name, list(shape), dt, kind="ExternalInput").ap()

    def dout(name, shape, dt=F32):
        return nc.dram_tensor(name, list(shape), dt, kind="ExternalOutput").ap()

    def dscr(name, shape, dt=BF16):
        return nc.dram_tensor(name, list(shape), dt, kind="Internal").ap()

    def pa(shape, dt):
        return A.alloc(shape, dt, True)

    def ta(shape, dt):
        return A.alloc(shape, dt, False)

    def new_phase():
        P.barrier()
        A.phase_reset()

    x = din("x", [T, D]); xs_in = din("xs", [NS, D])
    sconv = din("sconv", [NS, 3, 1536]); sssm = din("sssm", [NS, 4, 128, 128])
    ck = din("ck", [NS, LC, 512]); cv = din("cv", [NS, LC, 512])
    ln_mix = din("ln_mix", [1, D]); w_in = din("w_in", [D, IN_DIM]); conv_w = din("conv_w", [4, 1536])
    a_log = din("a_log", [1, 4]); dt_bias = din("dt_bias", [1, 4]); dn_norm = din("dn_norm", [1, 128])
    w_out = din("w_out", [D, D]); ln_ffn = din("ln_ffn", [1, D]); w_up = din("w_up", [D, DFF])
    w_down = din("w_down", [DFF, D]); ln_final = din("ln_final", [1, D])
    c_ident = din("c_ident", [128, 128]); c_umat = din("c_umat", [128, 128])
    c_maskA = din("c_maskA", [128, 128]); c_maskC = din("c_maskC", [128, 128])
    c_dist = din("c_dist", [128, 8 * NBR * 256])
    c_bmask = din("c_bmask", [128, 384])
    c_smask = din("c_smask", [8, 512]); c_sbias = din("c_sbias", [128, NBR * 8])

    y = dout("y", [T, D]); ys_out = dout("ys", [NS, D])
    p_conv = dout("p_conv", [3, 1536]); p_ssm = dout("p_ssm", [4, 128, 128])
    p_k = dout("p_k", [KEEP, 512]); p_v = dout("p_v", [KEEP, 512])
    s_conv = dout("s_conv", [NS, 3, 1536]); s_ssm = dout("s_ssm", [NS, 4, 128, 128])
    s_k = dout("s_k", [NS, LC, 512]); s_v = dout("s_v", [NS, LC, 512])
    dbgout = {}
    dbgin = {}
    for name, shape in dbg:
        if name.startswith("in_"):
            dbgin[name] = din("dbg_" + name, shape)
        else:
            dbgout[name] = dout("dbg_" + name, shape)

    win_s = dscr("win_s", [8, 128, KC, 512])
    wout_s = dscr("wout_s", [2, 128, KC, 512])
    wup_s = dscr("wup_s", [8, 128, KC, 512])
    wdn_s = dscr("wdn_s", [8, 128, KC, 512])
    TC = T + 128
    cat_s = dscr("cat_s", [KC, 128, TC])
    cat_smp = dscr("cat_smp", [128, KC, NS])

    identb = pa([128, 128], BF16)
    identf = pa([128, 128], F32)
    umat = pa([128, 128], F32)
    maskA = pa([128, 128], F32)
    maskC = pa([128, 128], F32)
    onesb = pa([128, 128], BF16)
    onesf = pa([128, 128], F32)
    negonesf = pa([128, 128], F32)
    negonesb = pa([128, 128], BF16)
    lnm_bc = pa([128, D], F32)
    epsc = pa([128, 4], F32)
    bmask = pa([128, 3, 128], BF16)

    pst = [P.ps("psb%d" % i, [128, 512], F32) for i in range(8)]
    psk = ["psb%d" % i for i in range(8)]

    wring = Ring(A, "wr", 4, [128, KC, 512], BF16, True)
    stat = Ring(A, "stat", 6, [128, 4], F32, True)
    trp = PsPool([pst[7]], [psk[7]])
    H = {}

    def phase_helpers():
        H["xring"] = Ring(A, "xt", 2, [128, D], F32)
        H["hbring"] = Ring(A, "hb", 2, [128, D], BF16)
        H["junk"] = ta([128, D], BF16)

    g0 = [P.dma("sp", identf, c_ident, "C0", writes=["identf"]),
          P.dma("sp", umat, c_umat, "C0", writes=["umat"]),
          P.dma("sp", maskA, c_maskA, "C0", writes=["maskA"]),
          P.dma("sp", maskC, c_maskC, "C0", writes=["maskC"]),
          P.dma("sp", lnm_bc, ln_mix.partition_broadcast(128), "C0", writes=["lnm_bc"])]
    P.group(g0)
    g0 = [P.dma("pool", identb, c_ident, "C1", writes=["identb"]),
          P.dma("pool", bmask, c_bmask.rearrange("p (a f) -> p a f", a=3), "C1", writes=["bmask"])]
    P.group(g0)
    P.memset("dve", onesb, 1.0, ["onesb"])
    P.memset("dve", onesf, 1.0, ["onesf"])
    P.memset("dve", negonesf, -1.0, ["negonesf"])
    P.memset("dve", negonesb, -1.0, ["negonesb"])
    P.memset("dve", epsc, EPS, ["epsc"])

    cast_jobs = []
    wres = {}

    def add_cast(pieces, name):
        wres[name] = "%s_%d" % (name, len(pieces) - 1)
        cast_jobs.append((name, pieces))

    wv = w_in.rearrange("(kc p) c -> p kc c", p=128)
    for t in (G_QA, G_KA, G_VA, G_QD, G_KD, G_VD, G_Z, G_BA):
        off, n = WIN_GROUPS[t]
        add_cast([(win_s[t, :, kc, 0:n], wv[:, kc, off:off + n]) for kc in range(KC)], "win_s%d" % t)
    wv = w_out.rearrange("(kc p) c -> p kc c", p=128)
    for t in range(2):
        add_cast([(wout_s[t, :, kc, :], wv[:, kc, t * 512:(t + 1) * 512]) for kc in range(KC)], "wout_s%d" % t)
    wv = w_up.rearrange("(kc p) c -> p kc c", p=128)
    for t in range(8):
        add_cast([(wup_s[t, :, kc, :], wv[:, kc, t * 512:(t + 1) * 512]) for kc in range(KC)], "wup_s%d" % t)
    wv = w_down.rearrange("(fc p) c -> p fc c", p=128)
    for t in range(8):
        half, g = t // 4, t % 4
        add_cast([(wdn_s[t, :, fl, :], wv[:, g * 8 + fl, half * 512:(half + 1) * 512]) for fl in range(8)], "wdn_s%d" % t)

    cast_prev = [None]

    def cast_some(n):
        for _ in range(n):
            if not cast_jobs:
                return
            name, pieces = cast_jobs.pop(0)
            ops = []
            for i, (dst, src) in enumerate(pieces):
                rd = [cast_prev[0]] if (i == 0 and cast_prev[0]) else []
                ops.append(P.dma("pool", dst, src, "CAST", reads=rd, writes=["%s_%d" % (name, i)], max_dma_last_dim=2048))
            P.group(ops)
            cast_prev[0] = "%s_%d" % (name, len(pieces) - 1)

    def wload(scr, t, name, ncols=512):
        while any(j[0] == name for j in cast_jobs):
            cast_some(1)
        wt, wk = wring.next()
        P.dma("sp", wt[:, :, 0:ncols], scr[t, :, :, 0:ncols], "L" + wk, reads=[wres[name]], writes=[wk])
        return wt, wk

    cast_some(3)

    def bulk_copies():
        nel = (LC - 1) * 512
        for n in range(NS):
            for (src, dst) in ((ck, s_k), (cv, s_v)):
                sv = src[n].rearrange("l c -> (l c)")[512:512 + nel].rearrange("(a b) -> a b", a=16)
                dv = dst[n].rearrange("l c -> (l c)")[0:nel].rearrange("(a b) -> a b", a=16)
                P.dma("act", dv, sv, "OUTps")
        P.dma("act", s_conv[:, 0:2, :], sconv[:, 1:3, :], "OUTps")

    def rstd_act(st, sk, nt, scale):
        P.act(st[:nt, 1:2], st[:nt, 0:1], AF.Ln, [sk, "epsc"], [sk], scale=scale, bias=epsc[:nt, 0:1])
        P.act(st[:nt, 2:3], st[:nt, 1:2], AF.Exp, [sk], [sk], scale=-0.5)

    def norm_T(xt, xk, nt, lnw, lnk, dst, dstk):
        st, sk = stat.next()
        hb, hk = H["hbring"].next()
        junk = H["junk"]
        P.act(junk[:nt], xt[:nt], AF.Square, [xk], ["junk", sk], accum_out=st[:nt, 0:1])
        rstd_act(st, sk, nt, 1.0 / D)
        P.stt(hb[:nt], xt[:nt], st[:nt, 2:3], lnw[:nt], ALU.mult, ALU.mult, [xk, sk, lnk], [hk])
        pt, pk = trp.next()
        pb = pt[:].bitcast(BF16).rearrange("p (k n) -> p k n", k=8)
        for kc in range(KC):
            P.tr(pb[:, kc, 0:nt], hb[:nt, kc * 128:(kc + 1) * 128], identb[:nt, :nt], [hk, "identb"], [pk])
        P.cp("act", dst, pb[:, :, 0:nt], [pk], [dstk])

    def dbg_store(name, src_ap, reskey, dst_ap=None, key=None):
        if name in dbgout:
            P.dma("pool", dbgout[name] if dst_ap is None else dst_ap, src_ap, key or ("DBG" + name), reads=[reskey],
                  max_dma_last_dim=2048)

    if "p1" in phases:
        QKVT = [ta([128, 4, T], BF16) for i in range(3)]
        distb = ta([128, 8 * NBR * 256], BF16)
        P.dma("pool", distb, c_dist, "c8", writes=["distb"], max_dma_last_dim=2048)
        a_mark = A.hi
        phase_helpers()
        xring = H["xring"]
        hTr = Ring(A, "hT", 2, [128, KC, 512], BF16)
        kvst = Ring(A, "kvst", 2, [128, 512], F32)
        mmp = PsPool(pst[0:4], psk[0:4])
        ev = [0]

        def p1_tile(j):
            hT, hTk = hTr.next()
            for sub in range(4):
                xt, xk = xring.next()
                r0 = j * 512 + sub * 128
                P.dma("sp", xt, x[r0:r0 + 128, :], "L" + xk, writes=[xk])
                norm_T(xt, xk, 128, lnm_bc, "lnm_bc", hT[:, :, sub * 128:(sub + 1) * 128], hTk)
            yield
            for which in range(3):
                wt, wk = wload(win_s, G_QA + which, "win_s%d" % (G_QA + which))
                for c4 in range(4):
                    pt, pk = mmp.next()
                    for kc in range(KC):
                        P.mm(pt[:, 0:512], wt[:, kc, c4 * 128:(c4 + 1) * 128], hT[:, kc, :], kc == 0, kc == KC - 1,
                             [wk, hTk], [pk])
                    dstT = QKVT[which][:, c4, j * 512:(j + 1) * 512]
                    P.cp(("act", "dve")[ev[0] % 2], dstT, pt[:, 0:512], [pk], ["qkvt%d" % which]); ev[0] += 1
                if which >= 1:
                    for sub in range(4):
                        tok0 = j * 512 + sub * 128
                        if tok0 < T - KEEP:
                            continue
                        pt, pk = mmp.next()
                        for kc in range(KC):
                            P.mm(pt[:, 0:512], hT[:, kc, sub * 128:(sub + 1) * 128], wt[:, kc, :], kc == 0, kc == KC - 1,
                                 [wk, hTk], [pk])
                        st_, stk = kvst.next()
                        P.cp(("act", "dve")[ev[0] % 2], st_, pt[:, 0:512], [pk], [stk]); ev[0] += 1
                        o0 = tok0 - (T - KEEP)
                        P.dma("sp", (p_k, p_v)[which - 1][o0:o0 + 128, :], st_, "S" + stk, reads=[stk])
                if which < 2:
                    yield
            cast_some(1)

        run_pipeline([p1_tile(j) for j in range(NT)], max_active=2)
        for i in range(3):
            dbg_store("qkvt%d" % i, QKVT[i], "qkvt%d" % i)

    if "p2" in phases:
        P.barrier()
        if "ps" in phases:
            bulk_copies()
        A.hi = a_mark
        QT, KT, VT = QKVT
        oacc = ta([128, 2, T], F32)
        oast = Ring(A, "oast", 2, [128, 1024], BF16)
        ptr = Ring(A, "ptr", 6, [128, 256], BF16)
        vtr = Ring(A, "vtok", 4, [128, 128], BF16)
        stA = PsPool([pst[0], pst[2]], [psk[0], psk[2]])
        stB = PsPool([pst[1], pst[3]], [psk[1], psk[3]])
        pvp = PsPool([pst[4], pst[5], pst[7]], [psk[4], psk[5], psk[7]])
        vtp = PsPool([pst[6]], [psk[6]])
        vslots = {}

        def blk(tens, hp, d, r, b):
            return tens[:, hp, :].rearrange("p (n d) -> p n d", d=d)[:, b * 128:(b + 1) * 128, r]

        def unit(hp, g, d, r, b):
            vt, vk = vtr.next()
            vslots[(hp, g, r, b)] = (vt, vk)
            pv_, pvk = vtp.next()
            pvb = pv_[:].bitcast(BF16)
            bctx = P.bundle(); bctx.__enter__()
            P.tr(pvb[:, 0:128], blk(VT, hp, d, r, b), identb, ["qkvt2", "identb"], [pvk])
            banks = []
            qb = blk(QT, hp, d, r, b)
            for h in range(2):
                pt, pk = (stA, stB)[h].next()
                banks.append((pt, pk))
                doff = ((hp * 2 + h) * NBR + g) * 256
                P.mm(pt[:, 0:256], identb, distb[:, doff:doff + 256], True, False, ["identb", "distb"], [pk])
                rows = slice(h * 64, (h + 1) * 64)
                if b > 0:
                    P.mm(pt[:, 0:128], blk(KT, hp, d, r, b - 1)[rows], qb[rows], False, False, ["qkvt0", "qkvt1"], [pk])
                P.mm(pt[:, 128:256], blk(KT, hp, d, r, b)[rows], qb[rows], False, True, ["qkvt0", "qkvt1"], [pk])
            bctx.__exit__()
            yield
            P.cp("dve", vt, pvb[:, 0:128], [pvk], [vk])
            pts = []
            for h in range(2):
                pt, pk = banks[h]
                e_, ek = ptr.next()
                P.act(e_, pt[:, 0:256], AF.Exp, [pk], [ek], scale=0.125)
                pts.append((e_, ek))
            yield
            pc, pck = pvp.next()
            halves = ([(0, vslots[(hp, g, r, b - 1)])] if b > 0 else []) + [(1, (vt, vk))]
            bctx = P.bundle(); bctx.__enter__()
            for h in range(2):
                rows = slice(h * 64, (h + 1) * 64)
                e_, ek = pts[h]
                for grp in range(2):
                    for i, (half, (vv, vvk)) in enumerate(halves):
                        lhs = vv[:, h * 64:(h + 1) * 64] if grp == 0 else onesb[:, 0:64]
                        P.mm(pc[rows, grp * 128:(grp + 1) * 128], lhs, e_[:, half * 128:(half + 1) * 128],
                             i == 0, i == len(halves) - 1, [vvk, ek, "onesb"], [pck])
            bctx.__exit__()
            yield
            ov = oacc.rearrange("p a (n d) -> p a n d", d=d)[:, :, b * 128:(b + 1) * 128, r]
            pcv = pc[:, 0:256].rearrange("p (a q) -> p a q", a=2)
            if g == 0:
                P.cp("dve", ov, pcv, [pck], ["oacc"])
            else:
                P.tt("dve", ov, pcv, ov, ALU.add, [pck, "oacc"], ["oacc"])

        for hp in range(4):
            gens = []
            for g, (win, d) in enumerate(branches):
                nb = T // (128 * d)
                for r in range(d):
                    for b in range(nb):
                        gens.append(unit(hp, g, d, r, b))
            run_pipeline(gens)
            for c0 in range(0, T, 1024):
                n = min(1024, T - c0)
                P.recip(oacc[:, 1, c0:c0 + n], oacc[:, 1, c0:c0 + n], ["oacc"], ["oacc"])
                o_, ok_ = oast.next()
                P.tt("dve", o_[:, 0:n], oacc[:, 0, c0:c0 + n], oacc[:, 1, c0:c0 + n], ALU.mult, ["oacc"], [ok_])
                P.dma("sp", cat_s[4 + hp, :, c0:c0 + n], o_[:, 0:n], "S" + ok_, reads=[ok_],
                      writes=["cat%d_%d" % (4 + hp, tt_) for tt_ in range(c0 // 512, (c0 + n) // 512)])
                dbg_store("oaT", o_[:, 0:n], ok_, dbgout.get("oaT", [None] * 8)[hp, :, c0:c0 + n] if "oaT" in dbgout else None,
                          "DBGoaT%d_%d" % (hp, c0))
            cast_some(2)
        if "ps" in phases:
            for grp in range(4):
                off_, n_ = WIN_GROUPS[grp]
                H["ps_pref%d" % grp] = wload(win_s, grp, "win_s%d" % grp, ncols=n_)

    if "ps" in phases:
        new_phase()
        if "p2" not in phases:
            bulk_copies()
        phase_helpers()
        xring = H["xring"]
        pp = PsPool(pst[0:6], psk[0:6])
        trp.t = [pst[5]]; trp.k = [psk[5]]
        psNum, kNum = pst[6], psk[6]
        psDen, kDen = pst[7], psk[7]
        n4 = NS
        S_all = ta([128, NS * 4, 128], F32)
        Snew = ta([128, NS * 4, 128], F32)
        g0 = [P.dma("sp", S_all[:, n * 4:(n + 1) * 4, :], sssm[n].rearrange("h k v -> k h v"), "Lsall", writes=["S_all"])
              for n in range(NS)]
        P.group(g0)
        proj4 = ta([n4, IN_DIM], F32)
        hTs = ta([128, KC, n4], BF16)
        acc = ta([n4, 1536], F32)
        tmpc = ta([n4, 1536], F32)
        scr_ = Ring(A, "scr", 2, [n4, 1536], F32)
        cwr = Ring(A, "cwr", 2, [n4, 1536], F32)
        I4bc = ta([128, 4, 4], F32)
        a4 = ta([n4, 4], F32); dtb4 = ta([n4, 4], F32); nrm4 = ta([n4, 128], F32)
        sml = ta([n4, 16, 8], F32)
        smask = ta([8, 512], F32); sbias = ta([128, NBR * 8], F32)
        QKcol = ta([128, 8, 4], F32)
        Sel = ta([128, 8, 4, 4], F32)
        qS4 = ta([n4, 4, 128], F32); kS4 = ta([n4, 4, 128], F32)
        vnew4 = ta([n4, 4, 128], F32); o4 = ta([n4, 4, 128], F32); t4a = ta([n4, 4, 128], F32)
        ksel = Ring(A, "ksel", 2, [n4, 4, 128], F32)
        EgSel = ta([n4, 4, 4], F32); EgB = ta([128, 16], F32)
        RowSel = ta([n4, 4, 128], F32)
        z4 = ta([n4, 512], F32)
        cat4 = ta([n4, 1024], F32)
        kgr = Ring(A, "kg", 2, [128, 512], F32); vgr = Ring(A, "vg", 2, [128, 512], F32)
        Qbs = ta([128, 512], F32); prod = ta([128, 512], F32)
        sc8 = Ring(A, "sc8", 2, [128, 8], F32)
        Xm = Ring(A, "Xm", 2, [8, 512], F32)
        num4 = ta([n4, 512], F32); den4 = ta([n4, 8], F32)
        catb = ta([128, KC, n4], BF16)
        I4 = identf[0:n4, 0:n4]

        g0 = [P.dma("sp", a4, a_log.partition_broadcast(n4), "C2", writes=["a4"]),
              P.dma("sp", dtb4, dt_bias.partition_broadcast(n4), "C2", writes=["dtb4"]),
              P.dma("sp", nrm4, dn_norm.partition_broadcast(n4), "C2", writes=["nrm4"]),
              P.dma("sp", smask, c_smask, "C2", writes=["smask"]),
              P.dma("sp", sbias, c_sbias, "C2", writes=["sbias"])]
        P.group(g0)
        P.act(a4, a4, AF.Exp, ["a4"], ["a4"])
        P.ts("dve", a4, a4, -1.0, ALU.mult, ["a4"], ["a4"])
        P.memset("dve", I4bc, 0.0, ["I4bc"])
        for n in range(4):
            P.memset("dve", I4bc[:, n, n:n + 1], 1.0, ["I4bc"])
        P.tt("dve", RowSel, onesf[0:n4, :].unsqueeze(1).to_broadcast([n4, 4, 128]), I4.unsqueeze(2).to_broadcast([n4, 4, 128]),
             ALU.mult, ["onesf", "identf"], ["RowSel"])

        xt, xk = xring.next()
        P.dma("sp", xt[:n4], xs_in, "L" + xk, writes=[xk])
        norm_T(xt, xk, n4, lnm_bc, "lnm_bc", hTs, "hTs")
        for grp in range(8):
            off, n = WIN_GROUPS[grp]
            if ("ps_pref%d" % grp) in H:
                wt, wk = H["ps_pref%d" % grp]
            else:
                wt, wk = wload(win_s, grp, "win_s%d" % grp, ncols=n)
            pt, pk = pp.next()
            for kc in range(KC):
                P.mm(pt[0:n4, 0:n], hTs[:, kc, :], wt[:, kc, 0:n], kc == 0, kc == KC - 1, [wk, "hTs"], [pk])
            P.cp("act", proj4[:, off:off + n], pt[0:n4, 0:n], [pk], ["proj4"])
        g0 = [P.dma("sp", s_conv[:, 2, :], proj4[:, 0:1536], "OUTp2", reads=["proj4"]),
              P.dma("sp", s_k[:, LC - 1, :], proj4[:, 2568:3080], "OUTp2", reads=["proj4"]),
              P.dma("sp", s_v[:, LC - 1, :], proj4[:, 3080:3592], "OUTp2", reads=["proj4"])]
        P.group(g0)
        q4a = proj4[:, 2056:2568]; k4a = proj4[:, 2568:3080]; v4a = proj4[:, 3080:3592]

        cw_, cwk = cwr.next()
        P.dma("sp", cw_, conv_w[3:4, :].partition_broadcast(n4), "L" + cwk, writes=[cwk])
        P.tt("dve", acc, proj4[:, 0:1536], cw_, ALU.mult, ["proj4", cwk], ["acc"])
        for i in range(3):
            cw_, cwk = cwr.next(); sc_, sck = scr_.next()
            P.dma("sp", cw_, conv_w[i:i + 1, :].partition_broadcast(n4), "L" + cwk, writes=[cwk])
            P.dma("sp", sc_, sconv[:, i, :], "L" + sck, writes=[sck])
            P.tt("dve", tmpc, sc_, cw_, ALU.mult, [sck, cwk], ["tmpc"])
            P.tt("dve", acc, acc, tmpc, ALU.add, ["acc", "tmpc"], ["acc"])
        P.act(acc, acc, AF.Silu, ["acc"], ["acc"])
        P.act(z4, proj4[:, 1536:2048], AF.Silu, ["proj4"], ["z4"])
        qk = acc[:, 0:1024].rearrange("p (i d) -> p i d", i=8)
        v4 = acc[:, 1024:1536].rearrange("p (h d) -> p h d", h=4)
        P.tt("dve", tmpc[:, 0:1024], acc[:, 0:1024], acc[:, 0:1024], ALU.mult, ["acc"], ["tmpc"])
        P.redsum(sml[:, 0, :], tmpc[:, 0:1024].rearrange("p (i d) -> p i d", i=8), ["tmpc"], ["sml"])
        P.act(sml[:, 1, :], sml[:, 0, :], AF.Ln, ["sml", "epsc"], ["sml"], bias=epsc[0:n4, 0:1])
        P.act(sml[:, 1, :], sml[:, 1, :], AF.Exp, ["sml"], ["sml"], scale=-0.5)
        P.ts("dve", sml[:, 1, 0:4], sml[:, 1, 0:4], float(128.0 ** -0.5), ALU.mult, ["sml"], ["sml"])
        P.tt("dve", qk, qk, sml[:, 1, :].unsqueeze(2).to_broadcast([n4, 8, 128]), ALU.mult, ["acc", "sml"], ["acc"])
        beta4 = sml[:, 2, 0:4]; g4 = sml[:, 3, 0:4]; eg4 = sml[:, 4, 0:4]
        P.act(sml[:, 5, 0:4], proj4[:, 2048:2052], AF.Exp, ["proj4"], ["sml"], scale=-1.0)
        P.ts("dve", sml[:, 5, 0:4], sml[:, 5, 0:4], 1.0, ALU.add, ["sml"], ["sml"])
        P.recip(beta4, sml[:, 5, 0:4], ["sml"], ["sml"])
        P.tt("dve", g4, proj4[:, 2052:2056], dtb4, ALU.add, ["proj4", "dtb4"], ["sml"])
        P.act(g4, g4, AF.Exp, ["sml"], ["sml"])
        P.act(g4, g4, AF.Ln, ["sml", "onesf"], ["sml"], bias=onesf[0:n4, 0:1])
        P.tt("dve", g4, g4, a4, ALU.mult, ["sml", "a4"], ["sml"])
        P.act(eg4, g4, AF.Exp, ["sml"], ["sml"])

        def b4(ap):
            return ap.unsqueeze(2).to_broadcast([n4, 4, 128])

        pt, pk = pp.next()
        ptv = pt[:, 0:32].rearrange("p (i m) -> p i m", i=8)
        for i in range(8):
            P.tr(ptv[:, i, :], qk[:, i, :], I4, ["acc", "identf"], [pk])
        P.cp("dve", QKcol, ptv, [pk], ["QKcol"])
        P.tt("dve", Sel, QKcol.unsqueeze(2).to_broadcast([128, 8, 4, 4]), I4bc.unsqueeze(1).to_broadcast([128, 8, 4, 4]), ALU.mult,
             ["QKcol", "I4bc"], ["Sel"])
        for (i0, dstt, dk_) in ((0, qS4, "qS4"), (4, kS4, "kS4")):
            pt, pk = pp.next()
            for h in range(4):
                for n in range(NS):
                    P.mm(pt[0:n4, h * 128:(h + 1) * 128], Sel[:, i0 + h, n, :], S_all[:, n * 4 + h, :], n == 0, n == NS - 1,
                         ["Sel", "S_all"], [pk])
            P.cp("act", dstt, pt[0:n4, 0:512].rearrange("p (h d) -> p h d", h=4), [pk], [dk_])
        q4 = qk[:, 0:4, :]; k4 = qk[:, 4:8, :]
        P.tt("dve", t4a, kS4, b4(eg4), ALU.mult, ["kS4", "sml"], ["t4a"])
        P.tt("dve", t4a, v4, t4a, ALU.subtract, ["acc", "t4a"], ["t4a"])
        P.tt("dve", vnew4, t4a, b4(beta4), ALU.mult, ["t4a", "sml"], ["vnew4"])
        P.tt("dve", t4a, q4, k4, ALU.mult, ["acc"], ["t4a"])
        P.redsum(sml[:, 6, 0:4], t4a, ["t4a"], ["sml"])
        P.tt("dve", o4, qS4, b4(eg4), ALU.mult, ["qS4", "sml"], ["o4"])
        P.tt("dve", t4a, vnew4, b4(sml[:, 6, 0:4]), ALU.mult, ["vnew4", "sml"], ["t4a"])
        P.tt("dve", o4, o4, t4a, ALU.add, ["o4", "t4a"], ["o4"])
        P.tt("dve", EgSel, eg4.unsqueeze(1).to_broadcast([n4, 4, 4]), I4.unsqueeze(2).to_broadcast([n4, 4, 4]), ALU.mult,
             ["sml", "identf"], ["EgSel"])
        pt, pk = pp.next()
        P.mm(pt[:, 0:16], onesf[0:n4, :], EgSel.rearrange("p n h -> p (n h)"), True, True, ["onesf", "EgSel"], [pk])
        P.cp("dve", EgB, pt[:, 0:16], [pk], ["EgB"])
        for n in range(NS):
            ks_, ksk = ksel.next()
            P.ts("dve", ks_, k4, I4[:, n:n + 1], ALU.mult, ["acc", "identf"], [ksk])
            pt, pk = pp.next()
            for h in range(4):
                P.mm(pt[:, h * 128:(h + 1) * 128], ks_[:, h, :], vnew4[:, h, :], True, True, [ksk, "vnew4"], [pk])
            for h in range(4):
                nh = n * 4 + h
                P.stt(Snew[:, nh, :], S_all[:, nh, :], EgB[:, nh:nh + 1], pt[:, h * 128:(h + 1) * 128], ALU.mult, ALU.add,
                      ["S_all", "EgB", pk], ["Snew"])
        g0 = [P.dma("sp", s_ssm[n].rearrange("h k v -> k h v"), Snew[:, n * 4:(n + 1) * 4, :], "Sssm", reads=["Snew"])
              for n in range(NS)]
        P.group(g0)
        P.tt("dve", t4a, o4, o4, ALU.mult, ["o4"], ["t4a"])
        P.redsum(sml[:, 7, 0:4], t4a, ["t4a"], ["sml"])
        P.act(sml[:, 8, 0:4], sml[:, 7, 0:4], AF.Ln, ["sml", "epsc"], ["sml"], scale=1.0 / 128, bias=epsc[0:n4, 0:1])
        P.act(sml[:, 8, 0:4], sml[:, 8, 0:4], AF.Exp, ["sml"], ["sml"], scale=-0.5)
        od4 = cat4[:, 0:512].rearrange("p (h d) -> p h d", h=4)
        P.tt("dve", o4, o4, b4(sml[:, 8, 0:4]), ALU.mult, ["o4", "sml"], ["o4"])
        P.tt("dve", o4, o4, nrm4.unsqueeze(1).to_broadcast([n4, 4, 128]), ALU.mult, ["o4", "nrm4"], ["o4"])
        P.tt("dve", od4, o4, z4.rearrange("p (h d) -> p h d", h=4), ALU.mult, ["o4", "z4"], ["cat4"])

        P.tt("dve", tmpc[:, 0:512], q4a, k4a, ALU.mult, ["proj4"], ["tmpc"])
        P.redsum(sml[:, 9, :], tmpc[:, 0:512].rearrange("p (h d) -> p h d", h=8), ["tmpc"], ["sml"])
        P.act(sml[:, 10, :], sml[:, 9, :], AF.Exp, ["sml"], ["sml"], scale=0.125)
        P.ts("dve", sml[:, 10, :], sml[:, 10, :], float(NBR), ALU.mult, ["sml"], ["sml"])
        first = True
        for n in range(NS):
            pt, pk = pp.next()
            P.mm(pt[:, 0:512], RowSel[:, n, :], q4a, True, True, ["RowSel", "proj4"], [pk])
            P.cp("act", Qbs, pt[:, 0:512], [pk], ["Qbs"])
            for g, (win, d) in enumerate(branches):
                kg, kgk = kgr.next(); vg, vgk = vgr.next()
                rows = ck[n].rearrange("(a d) c -> a d c", d=d)[LC // d - 128:LC // d, 0, :]
                rows_v = cv[n].rearrange("(a d) c -> a d c", d=d)[LC // d - 128:LC // d, 0, :]
                P.dma("sp", kg, rows, "L" + kgk, writes=[kgk])
                P.dma("sp", vg, rows_v, "L" + vgk, writes=[vgk])
                P.tt("dve", prod, kg, Qbs, ALU.mult, [kgk, "Qbs"], ["prod"])
                s8, s8k = sc8.next()
                P.redsum(s8, prod.rearrange("p (h d) -> p h d", h=8), ["prod"], [s8k])
                P.stt(s8, s8, 0.125, sbias[:, g * 8:(g + 1) * 8], ALU.mult, ALU.add, [s8k, "sbias"], [s8k])
                P.act(s8, s8, AF.Exp, [s8k], [s8k])
                pt, pk = pp.next()
                P.mm(pt[0:8, 0:512], s8, vg, True, True, [s8k, vgk], [pk])
                xm, xmk = Xm.next()
                P.tt("dve", xm, pt[0:8, 0:512], smask, ALU.mult, [pk, "smask"], [xmk])
                last = (n == NS - 1 and g == NBR - 1)
                P.mm(psNum[0:n4, 0:512], I4bc[0:8, n, :], xm, first, last, ["I4bc", xmk], [kNum])
                P.mm(psDen[0:n4, 0:8], I4bc[:, n, :], s8, first, last, ["I4bc", s8k], [kDen])
                first = False
        P.cp("dve", num4, psNum[0:n4, 0:512], [kNum], ["num4"])
        P.cp("dve", den4, psDen[0:n4, 0:8], [kDen], ["den4"])
        e0b = sml[:, 10, :].unsqueeze(2).to_broadcast([n4, 8, 64])
        P.tt("dve", tmpc[:, 0:512].rearrange("p (h d) -> p h d", h=8), v4a.rearrange("p (h d) -> p h d", h=8), e0b, ALU.mult,
             ["proj4", "sml"], ["tmpc"])
        P.tt("dve", num4, num4, tmpc[:, 0:512], ALU.add, ["num4", "tmpc"], ["num4"])
        P.tt("dve", den4, den4, sml[:, 10, :], ALU.add, ["den4", "sml"], ["den4"])
        P.recip(den4, den4, ["den4"], ["den4"])
        P.tt("dve", cat4[:, 512:1024].rearrange("p (h d) -> p h d", h=8), num4.rearrange("p (h d) -> p h d", h=8),
             den4.unsqueeze(2).to_broadcast([n4, 8, 64]), ALU.mult, ["num4", "den4"], ["cat4"])
        pt, pk = pp.next()
        ptc = pt[:, 0:32].rearrange("p (c m) -> p c m", c=8)
        for c in range(8):
            P.tr(ptc[:, c, :], cat4[:, c * 128:(c + 1) * 128], I4, ["cat4", "identf"], [pk])
        P.cp("dve", catb, ptc, [pk], ["catb"])
        P.dma("sp", cat_smp, catb, "Scats", reads=["catb"], writes=["cats%d" % c for c in range(8)])
        dbg_store("cat4", cat4, "cat4")

    if "p3" in phases:
        new_phase()
        phase_helpers()
        xring = H["xring"]
        hTr = Ring(A, "hT3", 2, [128, KC, 512], BF16)
        stg = Ring(A, "cstg", 2, [128, 516], F32)
        halo = ta([128, 12, 4], F32)
        ybr = Ring(A, "yb", 2, [128, 512], F32)
        ysil = ta([128, 8, 512], F32)
        sqr = Ring(A, "sq", 2, [128, 512], BF16)
        lrs = Ring(A, "lrs", 2, [128, 512], F32)
        QKr = Ring(A, "QKn", 2, [128, 8, 512], BF16)
        VTr = Ring(A, "VTn", 2, [128, 4, 512], BF16)
        nwzr = Ring(A, "nwz", 2, [128, 4, 512], BF16)
        zsr = Ring(A, "zs", 2, [128, 512], F32)
        bar_ = Ring(A, "ba", 2, [128, 4, 8], F32)
        bgr = Ring(A, "bg", 2, [128, 3, 4, 4], F32)
        odTr = Ring(A, "odT", 2, [128, 4, 512], BF16)
        smr = Ring(A, "sm", 8, [128, 8, 4], F32)
        S32 = ta([128, 4, 128], F32)
        Sb = ta([128, 4, 128], BF16)
        cw = ta([128, 12, 4], F32)
        cwt = ysil[0:4, 0:3, :].rearrange("p a b -> p (a b)")
        nrm_bc = ta([128, 128], F32)
        nA_bc = ta([128, 4], F32)
        dtb_bc = ta([128, 4], F32)
        qsc = ta([128, 1], F32)
        wba = ta([128, KC, 8], BF16)
        pp = PsPool(pst[6:8], psk[6:8])
        trp.t = pp.t; trp.k = pp.k

        g0 = [P.dma("sp", cwt, conv_w, "C3", writes=["ysil"]),
              P.dma("sp", nrm_bc, dn_norm.partition_broadcast(128), "C3", writes=["nrm_bc"]),
              P.dma("sp", nA_bc, a_log.partition_broadcast(128), "C3", writes=["nA_bc"]),
              P.dma("sp", dtb_bc, dt_bias.partition_broadcast(128), "C3", writes=["dtb_bc"])]
        P.group(g0)
        P.act(nA_bc, nA_bc, AF.Exp, ["nA_bc"], ["nA_bc"])
        P.ts("dve", nA_bc, nA_bc, -1.0, ALU.mult, ["nA_bc"], ["nA_bc"])
        P.memset("dve", qsc, float(np.log(128.0 ** -0.5)), ["qsc"])
        P.memset("dve", halo, 0.0, ["halo"])
        P.memset("dve", S32, 0.0, ["S32"])
        P.memset("dve", Sb, 0.0, ["Sb"])
        pt, pk = pp.next()
        ptv = pt[:, 0:48].rearrange("p (c i) -> p c i", i=4)
        for cc in range(12):
            P.tr(ptv[:, cc, :], cwt[:, cc * 128:(cc + 1) * 128], identf[0:4, 0:4], ["ysil", "identf"], [pk])
        P.cp("dve", cw, ptv, [pk], ["cw"])
        while any(j_[0] == "win_s%d" % G_BA for j_ in cast_jobs):
            cast_some(1)
        P.dma("sp", wba, win_s[G_BA, :, :, 0:8], "c14", reads=[wres["win_s%d" % G_BA]], writes=["wba"])

        def bc_h(ap):
            return ap.unsqueeze(2).to_broadcast([128, 4, 128])

        def bc_m(ap):
            return ap.unsqueeze(1).to_broadcast([128, 4, 128])

        tiles = {}

        def prep(j):
            hT, hTk = hTr.next()
            QK, QKk = QKr.next(); VTn, VTk = VTr.next(); nwz, nwzk = nwzr.next()
            ba, bak = bar_.next(); bg, bgk = bgr.next()
            tiles[j] = (QK, QKk, VTn, VTk, nwz, nwzk, bg, bgk)
            for sub in range(4):
                xt, xk = xring.next()
                r0 = j * 512 + sub * 128
                P.dma("sp", xt, x[r0:r0 + 128, :], "L" + xk, writes=[xk])
                norm_T(xt, xk, 128, lnm_bc, "lnm_bc", hT[:, :, sub * 128:(sub + 1) * 128], hTk)
            yield
            for grp in range(3):
                wt, wk = wload(win_s, G_QD + grp, "win_s%d" % (G_QD + grp))
                for c4 in range(4):
                    cc = grp * 4 + c4
                    pt, pk = pp.next()
                    for kc in range(KC):
                        P.mm(pt[:, 0:512], wt[:, kc, c4 * 128:(c4 + 1) * 128], hT[:, kc, :], kc == 0, kc == KC - 1, [wk, hTk], [pk])
                    sg, sgk = stg.next()
                    P.cp("dve", sg[:, 0:3], halo[:, cc, 0:3], ["halo"], [sgk])
                    P.cp("act", sg[:, 3:515], pt[:, 0:512], [pk], [sgk])
                    P.cp("dve", halo[:, cc, 0:3], sg[:, 512:515], [sgk], ["halo"])
                    yb, ybk = ybr.next()
                    P.act(yb, sg[:, 0:512], AF.Identity, [sgk, "cw"], [ybk], scale=cw[:, cc, 0:1])
                    for i in range(1, 4):
                        P.stt(yb, sg[:, i:i + 512], cw[:, cc, i:i + 1], yb, ALU.mult, ALU.add, [sgk, "cw", ybk], [ybk])
                    if grp == 2:
                        P.act(VTn[:, c4, :], yb, AF.Silu, [ybk], [VTk])
                    else:
                        P.act(ysil[:, cc, :], yb, AF.Silu, [ybk], ["ysil"])
                yield
            if j == NT - 1:
                for grp in range(3):
                    pt, pk = pp.next()
                    for c4 in range(4):
                        P.tr(pt[0:3, c4 * 128:(c4 + 1) * 128], halo[:, grp * 4 + c4, 0:3], identf, ["halo", "identf"], [pk])
                    pz, pzk = zsr.next()
                    P.cp("dve", pz[0:3, :], pt[0:3, 0:512], [pk], [pzk])
                    P.dma("sp", p_conv[:, grp * 512:(grp + 1) * 512], pz[0:3, :], "S" + pzk, reads=[pzk])
            wt, wk = wload(win_s, G_Z, "win_s%d" % G_Z)
            for sub in range(4):
                pt, pk = pp.next()
                ts_ = slice(sub * 128, (sub + 1) * 128)
                for kc in range(KC):
                    P.mm(pt[:, 0:512], hT[:, kc, ts_], wt[:, kc, :], kc == 0, kc == KC - 1, [wk, hTk], [pk])
                zs, zsk = zsr.next()
                P.act(zs, pt[:, 0:512], AF.Silu, [pk], [zsk])
                P.tt("pool", nwz[:, :, ts_].rearrange("p h t -> p h t"), zs.rearrange("p (h v) -> p h v", h=4), bc_m(nrm_bc), ALU.mult,
                     [zsk, "nrm_bc"], [nwzk])
                pt2, pk2 = pp.next()
                for kc in range(KC):
                    P.mm(pt2[:, 0:8], hT[:, kc, ts_], wba[:, kc, :], kc == 0, kc == KC - 1, ["wba", hTk], [pk2])
                P.cp("dve", ba[:, sub, :], pt2[:, 0:8], [pk2], [bak])
            yield
            for cc in range(8):
                sq, sqk = sqr.next()
                P.tt("pool", sq, ysil[:, cc, :], ysil[:, cc, :], ALU.mult, ["ysil"], [sqk])
                pt, pk = pp.next()
                P.mm(pt[:, 0:512], onesb, sq, True, True, ["onesb", sqk], [pk])
                lr, lrk = lrs.next()
                P.act(lr, pt[:, 0:512], AF.Ln, [pk, "epsc"], [lrk], bias=epsc[:, 0:1])
                if cc < 4:
                    P.act(lr, lr, AF.Exp, [lrk, "qsc"], [lrk], scale=-0.5, bias=qsc[:, 0:1])
                else:
                    P.act(lr, lr, AF.Exp, [lrk], [lrk], scale=-0.5)
                P.tt("dve", QK[:, cc, :], ysil[:, cc, :], lr, ALU.mult, ["ysil", lrk], [QKk])
                if cc == 3:
                    yield
            yield
            P.act(bg[:, 2], ba[:, :, 0:4], AF.Exp, [bak], [bgk], scale=-1.0)
            P.ts("dve", bg[:, 2], bg[:, 2], 1.0, ALU.add, [bgk], [bgk])
            P.recip(bg[:, 0], bg[:, 2], [bgk], [bgk])
            P.tt("dve", bg[:, 1], ba[:, :, 4:8], dtb_bc.unsqueeze(1).to_broadcast([128, 4, 4]), ALU.add, [bak, "dtb_bc"], [bgk])
            P.act(bg[:, 1], bg[:, 1], AF.Exp, [bgk], [bgk])
            P.act(bg[:, 1], bg[:, 1], AF.Ln, [bgk, "onesf"], [bgk], bias=onesf[:, 0:1])
            P.tt("dve", bg[:, 1], bg[:, 1], nA_bc.unsqueeze(1).to_broadcast([128, 4, 4]), ALU.mult, [bgk, "nA_bc"], [bgk])

        KCTX = 2
        ctxs = []
        for ci in range(KCTX):
            cx = {}
            for nm in ("Mm", "Mo64", "Mo128", "Nn", "Qa", "Pa", "R0", "R1", "QKm", "QsT", "Kt", "rK", "vb"):
                cx[nm] = (ta([128, 4, 128], BF16), "cx%d_%s" % (ci, nm))
            for nm in ("gU", "aL", "aU", "eg"):
                cx[nm] = (ta([128, 4, 128], F32), "cx%d_%s" % (ci, nm))
            ctxs.append(cx)

        def v4(pt):
            return pt[:, 0:512].rearrange("p (h f) -> p h f", h=4)

        seq_done = {}

        def chunk(c):
            j, sub = c // 4, c % 4
            cs = slice(sub * 128, (sub + 1) * 128)
            ci = c % KCTX
            cx = ctxs[ci]
            if False:
                yield
            bA, bB, bC = pst[3 * ci], pst[3 * ci + 1], pst[3 * ci + 2]
            kA, kB, kC = psk[3 * ci], psk[3 * ci + 1], psk[3 * ci + 2]
            QK, QKk, VTn, VTk, nwz, nwzk, bg, bgk = tiles[j]
            qT = QK[:, 0:4, cs]; kT = QK[:, 4:8, cs]; vT = VTn[:, :, cs]
            beta = bg[:, 0, sub, :]; g = bg[:, 1, sub, :]
            sm, smk = smr.next()
            CbK = bC[:].bitcast(BF16)[:, 0:512].rearrange("p (h f) -> p h f", h=4)
            Gc = bC[:, 256:260]
            gU, gUk = cx["gU"]
            ghi, ghik = cx["Nn"]; glo, glok = cx["Pa"]
            P.tt("pool", gU, bc_m(umat), bc_h(g), ALU.mult, ["umat", bgk], [gUk])
            P.cp("pool", ghi, gU, [gUk], [ghik])
            P.tt("pool", glo, gU, ghi, ALU.subtract, [gUk, ghik], [glok])
            G2 = v4(bA); KKp = v4(bB)
            for h in range(4):
                P.mm(G2[:, h, :], onesb, ghi[:, h, :], True, False, ["onesb", ghik], [kA])
                P.mm(G2[:, h, :], onesb, glo[:, h, :], False, False, ["onesb", glok], [kA])
                P.mm(G2[:, h, :], ghi[:, h, :], negonesb, False, False, ["negonesb", ghik], [kA])
                P.mm(G2[:, h, :], glo[:, h, :], negonesb, False, True, ["negonesb", glok], [kA])
            for h in range(4):
                P.mm(KKp[:, h, :], kT[:, h, :], kT[:, h, :], True, True, [QKk], [kB])
            for h in range(4):
                P.tr(CbK[:, h, :], kT[:, h, :], identb, [QKk, "identb"], [kC])
            P.mm(Gc, umat, g, True, True, ["umat", bgk], [kC])
            yield
            P.cp("dve", sm[:, 0, :], Gc, [kC], [smk])
            aL, aLk = cx["aL"]; aU, aUk = cx["aU"]; eg, egk = cx["eg"]
            P.tt("dve", aL, G2, bc_m(maskA), ALU.add, [kA, "maskA"], [aLk])
            P.tt("dve", aU, G2, bc_m(maskC), ALU.add, [kA, "maskC"], [aUk])
            P.cp("dve", sm[:, 1, :], G2[:, :, 127], [kA], [smk])
            for h in range(4):
                P.act(eg[:, h, :], G2[:, h, :], AF.Exp, [kA, smk], [egk], bias=sm[:, 0, h:h + 1])
            P.act(aL, aL, AF.Exp, [aLk], [aLk], scale=-1.0)
            P.act(aU, aU, AF.Exp, [aUk], [aUk])
            P.act(sm[:, 2, :], sm[:, 1, :], AF.Exp, [smk], [smk])
            P.act(sm[:, 3, :], sm[:, 0, :], AF.Exp, [smk], [smk])
            P.tt("dve", sm[:, 4, :], sm[:, 3, :], beta, ALU.mult, [smk, bgk], [smk])
            P.tt("dve", sm[:, 5, :], sm[:, 1, :], sm[:, 0, :], ALU.add, [smk], [smk])
            P.act(sm[:, 5, :], sm[:, 5, :], AF.Exp, [smk], [smk])
            yield
            P.tt("pool", aL, aL, bc_h(beta), ALU.mult, [aLk, bgk], [aLk])
            Mm, Mk = cx["Mm"]; QKm, QKmk = cx["QKm"]; QsT, QsTk = cx["QsT"]
            rK, rKk = cx["rK"]; Kt, Ktk = cx["Kt"]; vb, vbk = cx["vb"]
            Mf, Mfk = cx["Qa"]
            Mo64, Mo64k = cx["Mo64"]; Mo128, Mo128k = cx["Mo128"]
            P.tt("dve", Mf, KKp, aL, ALU.mult, [kB, aLk], [Mfk])
            P.tt("pool", Mm, Mf, bc_m(bmask[:, 0, :]), ALU.mult, [Mfk, "bmask"], [Mk])
            P.tt("pool", Mo64, Mf, bc_m(bmask[:, 1, :]), ALU.mult, [Mfk, "bmask"], [Mo64k])
            P.tt("pool", Mo128, Mf, bc_m(bmask[:, 2, :]), ALU.mult, [Mfk, "bmask"], [Mo128k])
            ktk_, ktkk = cx["Pa"]
            P.cp("act", ktk_, CbK, [kC], [ktkk])
            P.tt("pool", rK, ktk_, bc_h(sm[:, 4, :]), ALU.mult, [ktkk, smk], [rKk])
            P.tt("pool", Kt, ktk_, bc_h(sm[:, 2, :]), ALU.mult, [ktkk, smk], [Ktk])
            P.tt("pool", QsT, qT, eg, ALU.mult, [QKk, egk], [QsTk])
            if c == 0:
                dbg_store("M", Mf, Mfk)
            yield
            Nb = bA[:].bitcast(BF16)[:, 0:512].rearrange("p (h f) -> p h f", h=4)
            QKp = v4(bB)
            for h in range(4):
                P.tr(Nb[:, h, :], Mm[:, h, :], identb, [Mk, "identb"], [kA])
            for h in range(4):
                P.mm(QKp[:, h, :], kT[:, h, :], qT[:, h, :], True, True, [QKk], [kB])
            for h in range(4):
                P.tr(CbK[:, h, :], vT[:, h, :], identb, [VTk, "identb"], [kC])
            yield
            Nn, Nk = cx["Nn"]
            Rbuf = [cx["R0"], cx["R1"]]
            Rr, Rk = Rbuf[0]
            P.cp("act", Nn, Nb, [kA], [Nk])
            P.tt("pool", Rr, bc_m(identb), Nn, ALU.subtract, ["identb", Nk], [Rk])
            P.tt("dve", QKm, QKp, aU, ALU.mult, [kB, aUk], [QKmk])
            P.tt("dve", vb, CbK, bc_h(beta), ALU.mult, [kC, bgk], [vbk])
            if c == 0:
                dbg_store("aL", aL, aLk); dbg_store("aU", aU, aUk); dbg_store("eg", eg, egk)
                dbg_store("QKm", QKm, QKmk); dbg_store("rK", rK, rKk); dbg_store("Kt", Kt, Ktk); dbg_store("vb", vb, vbk)
                dbg_store("qT", qT, QKk); dbg_store("kT", kT, QKk)
                dbg_store("sm", sm[:, 0:6, :], smk, dbgout["sm"][:, 0:6, :] if "sm" in dbgout else None); dbg_store("bg", bg, bgk)
            yield
            Pbuf = [cx["Nn"], cx["Pa"]]; Qbuf = [cx["Mm"], cx["Qa"]]
            Pc, Pck = Pbuf[0]; Qc, Qck = Qbuf[0]
            Qp = v4(bA); Pp = v4(bB); Rp = v4(bC)
            pend = None
            NLEV = 4
            for lev in range(1, NLEV + 2):
                if lev <= NLEV:
                    if lev < NLEV:
                        for h in range(4):
                            P.mm(Pp[:, h, :], Qc[:, h, :], Pc[:, h, :], True, True, [Qck, Pck], [kB])
                    for h in range(4):
                        P.mm(Qp[:, h, :], Pc[:, h, :], Qc[:, h, :], True, True, [Qck, Pck], [kA])
                if pend is not None:
                    for h in range(4):
                        P.mm(Rp[:, h, :], pend[0][:, h, :], Rr[:, h, :], True, True, [pend[1], Rk], [kC])
                yield
                if pend is not None:
                    Rn, Rnk = Rbuf[(lev - 1) % 2]
                    P.tt("dve", Rn, Rp, Rr, ALU.add, [kC, Rk], [Rnk])
                    Rr, Rk = Rn, Rnk
                    pend = None
                if lev <= NLEV:
                    Qn, Qnk = Qbuf[lev % 2]
                    P.cp("dve", Qn, Qp, [kA], [Qnk])
                    if lev < NLEV:
                        Pn, Pnk = Pbuf[lev % 2]
                        P.cp("act", Pn, Pp, [kB], [Pnk])
                        Pc, Pck = Pn, Pnk
                    Qc, Qck = Qn, Qnk
                    pend = (Qn, Qnk)
                yield
            RTb = bA[:].bitcast(BF16)[:, 0:512].rearrange("p (h f) -> p h f", h=4)
            Yp = v4(bB); Xp = v4(bC)
            for (Mo, Mok) in ((Mo64, Mo64k), (Mo128, Mo128k)):
                for h in range(4):
                    P.tr(RTb[:, h, :], Rr[:, h, :], identb, [Rk, "identb"], [kA])
                for h in range(4):
                    P.mm(Yp[:, h, :], Mo[:, h, :], Rr[:, h, :], True, True, [Mok, Rk], [kB])
                yield
                Rm, Rmk = cx["Nn"]; Yb, Ybk = cx["Pa"]
                P.cp("act", Rm, RTb, [kA], [Rmk])
                P.cp("act", Yb, Yp, [kB], [Ybk])
                yield
                for h in range(4):
                    P.mm(Xp[:, h, :], Rm[:, h, :], Yb[:, h, :], True, True, [Rmk, Ybk], [kC])
                yield
                Rn, Rnk = Rbuf[1] if Rr is Rbuf[0][0] else Rbuf[0]
                P.stt(Rn, Xp, -1.0, Rr, ALU.mult, ALU.add, [kC, Rk], [Rnk])
                Rr, Rk = Rn, Rnk
                yield
            Up = v4(bA); Wp = v4(bB)
            for h in range(4):
                P.mm(Up[:, h, :], Rr[:, h, :], vb[:, h, :], True, True, [Rk, vbk], [kA])
            for h in range(4):
                P.mm(Wp[:, h, :], rK[:, h, :], Rr[:, h, :], True, True, [Rk, rKk], [kB])
            yield
            us, usk = cx["gU"]; WT, WTk = cx["Qa"]
            P.cp("act", us, Up, [kA], [usk])
            P.cp("dve", WT, Wp, [kB], [WTk])
            if c == 0:
                dbg_store("R", Rr, Rk); dbg_store("us", us, usk); dbg_store("WT", WT, WTk)
            yield
            while c > 0 and not seq_done.get(c - 1):
                yield
            WSp = v4(bC)
            for h in range(4):
                P.mm(WSp[:, h, :], WT[:, h, :], Sb[:, h, :], True, True, [WTk, "Sb"], [kC])
            yield
            vn, vnk = cx["Nn"]
            P.stt(vn, WSp, -1.0, us, ALU.mult, ALU.add, [kC, usk], [vnk])
            yield
            Op = v4(bA); Snp = v4(bB)
            for h in range(4):
                P.mm(Op[:, h, :], QsT[:, h, :], Sb[:, h, :], True, False, [QsTk, "Sb"], [kA])
                P.mm(Op[:, h, :], QKm[:, h, :], vn[:, h, :], False, True, [QKmk, vnk], [kA])
            for h in range(4):
                P.mm(Snp[:, h, :], Kt[:, h, :], vn[:, h, :], True, True, [Ktk, vnk], [kB])
            yield
            P.tt("pool", S32, S32, bc_h(sm[:, 5, :]), ALU.mult, ["S32", smk], ["S32"])
            P.tt("dve", S32, Snp, S32, ALU.add, [kB, "S32"], ["S32"])
            P.cp("act", Sb, S32, ["S32"], ["Sb"])
            seq_done[c] = True
            junk = H["junk"]
            for h in range(4):
                P.act(junk[:, h * 128:(h + 1) * 128], Op[:, h, :], AF.Square, [kA], ["junk", smk], accum_out=sm[:, 6, h:h + 1])
            P.act(sm[:, 7, :], sm[:, 6, :], AF.Ln, [smk, "epsc"], [smk], scale=1.0 / 128, bias=epsc[:, 0:1])
            P.act(sm[:, 7, :], sm[:, 7, :], AF.Exp, [smk], [smk], scale=-0.5)
            t1, t1k = cx["aL"]
            P.tt("dve", t1, Op, bc_h(sm[:, 7, :]), ALU.mult, [kA, smk], [t1k])
            od, odk = cx["Pa"]
            P.tt("pool", od, t1, nwz[:, :, cs], ALU.mult, [t1k, nwzk], [odk])
            yield
            Tb = bC[:].bitcast(BF16)[:, 0:512].rearrange("p (h f) -> p h f", h=4)
            for h in range(4):
                P.tr(Tb[:, h, :], od[:, h, :], identb, [odk, "identb"], [kC])
            yield
            if sub == 0:
                tiles[("odT", j)] = odTr.next()
            oT, oTk = tiles[("odT", j)]
            P.cp("act", oT[:, :, cs], Tb, [kC], [oTk])
            if sub == 3:
                P.dma("sp", cat_s.rearrange("c p t -> p c t")[:, 0:4, j * 512:(j + 1) * 512], oT, "S" + oTk, reads=[oTk],
                      writes=["cat%d_%d" % (h, j) for h in range(4)])
                for h in range(4):
                    dbg_store("odT", oT[:, h, :], oTk, dbgout["odT"][h, :, j * 512:(j + 1) * 512] if "odT" in dbgout else None,
                              "DBGodT%d_%d" % (h, j))
            if c == T // 128 - 1:
                P.dma("sp", p_ssm.rearrange("h k v -> k h v"), S32, "Spssm", reads=["S32"])

        NCH = T // 128
        PSTEP = 5
        next_prep = 0; prep_gen = None; cur_prep = -1; prep_done = set()
        chunk_next = 0; active = []; done_chunks = set(); step = 0
        while len(done_chunks) < NCH:
            if prep_gen is None and next_prep < NT and all(cc_ in done_chunks for cc_ in range(0, 4 * (next_prep - 1))):
                prep_gen = prep(next_prep); cur_prep = next_prep; next_prep += 1
            while (len(active) < KCTX and chunk_next < NCH and (chunk_next // 4) in prep_done
                   and (chunk_next < KCTX or (chunk_next - KCTX) in done_chunks)):
                active.append((chunk_next, chunk(chunk_next))); chunk_next += 1
            nxt = []
            for (cid, g_) in active:
                try:
                    next(g_)
                    nxt.append((cid, g_))
                except StopIteration:
                    done_chunks.add(cid)
            active = nxt
            if prep_gen is not None and (not active or step % PSTEP == 0):
                try:
                    next(prep_gen)
                except StopIteration:
                    prep_done.add(cur_prep); prep_gen = None
            step += 1

    if "p4" in phases:
        new_phase()
        phase_helpers()
        xring = H["xring"]; junk = H["junk"]
        lnf_bc = ta([128, D], F32)
        lnz_bc = ta([128, D], F32)
        g0 = [P.dma("sp", lnf_bc, ln_ffn.partition_broadcast(128), "C4", writes=["lnf_bc"]),
              P.dma("sp", lnz_bc, ln_final.partition_broadcast(128), "C4", writes=["lnz_bc"])]
        P.group(g0)
        if "in_cat" in dbgin:
            for c in range(8):
                P.dma("pool", cat_s[c], dbgin["in_cat"][c], "DBGcat%d" % c,
                      writes=["cat%d_%d" % (c, j) for j in range(NT)], max_dma_last_dim=2048)
                P.dma("pool", cat_smp[:, c, :], dbgin["in_cat"][c, :, T:T + NS], "DBGcats%d" % c, writes=["cats%d" % c])
        for i_ in range(2):
            wring.t.append(ta([128, KC, 512], BF16)); wring.k.append("wrx%d" % i_)
        catr = Ring(A, "cat", 2, [128, KC, 512], BF16)
        x1r = Ring(A, "x1", 2, [128, 4, D], F32)
        h2r = Ring(A, "h2T", 2, [128, KC, 512], BF16)
        rr = Ring(A, "relu", 3, [128, 512], F32)
        aT = ta([128, 32, 512], BF16)
        mixp = PsPool(pst[0:3], psk[0:3])
        cat_v = cat_s.rearrange("c p t -> p c t")

        def p4_tile(tok0, ntok, xsrc, ydst, catkeys):
            subs = [(s0, min(128, ntok - s0)) for s0 in range(0, ntok, 128)]
            ct, ctk = catr.next(); x1, x1k = x1r.next(); h2, h2k = h2r.next()
            if tok0 >= T:
                P.dma("sp", ct[:, :, 0:ntok], cat_smp, "L" + ctk, reads=catkeys, writes=[ctk])
            else:
                P.dma("sp", ct[:, :, 0:ntok], cat_v[:, :, tok0:tok0 + ntok], "L" + ctk, reads=catkeys, writes=[ctk])
            wos = [wload(wout_s, half, "wout_s%d" % half) for half in range(2)]
            for si, (s0, nt) in enumerate(subs):
                xt, xk = xring.next()
                P.dma("sp", xt[:nt], xsrc[s0:s0 + nt, :], "L" + xk, writes=[xk])
                for half in range(2):
                    wt, wk = wos[half]
                    pt, pk = mixp.next()
                    for kc in range(KC):
                        P.mm(pt[:nt, 0:512], ct[:, kc, s0:s0 + nt], wt[:, kc, :], kc == 0, kc == KC - 1, [ctk, wk], [pk])
                    cs = slice(half * 512, (half + 1) * 512)
                    P.tt("dve", x1[:nt, si, cs], pt[:nt, 0:512], xt[:nt, cs], ALU.add, [pk, xk], [x1k + "_%d" % si])
                norm_T(x1[:, si, :], x1k + "_%d" % si, nt, lnf_bc, "lnf_bc", h2[:, :, s0:s0 + nt], h2k)
            yield
            for g in range(8):
                wt, wk = wload(wup_s, g, "wup_s%d" % g)
                for fl in range(4):
                    pt, pk = mixp.next()
                    for kc in range(KC):
                        P.mm(pt[:, 0:ntok], wt[:, kc, fl * 128:(fl + 1) * 128], h2[:, kc, 0:ntok], kc == 0, kc == KC - 1,
                             [wk, h2k], [pk])
                    r_, rk = rr.next()
                    P.act(r_[:, 0:ntok], pt[:, 0:ntok], AF.Relu, [pk], [rk])
                    P.tt("pool", aT[:, g * 4 + fl, 0:ntok], r_[:, 0:ntok], r_[:, 0:ntok], ALU.mult, [rk], ["aT"])
            for half in range(2):
                cs = slice(half * 512, (half + 1) * 512)
                for g in range(4):
                    wt, wk = wload(wdn_s, half * 4 + g, "wdn_s%d" % (half * 4 + g))
                    for fl in range(8):
                        fc = g * 8 + fl
                        for si, (s0, nt) in enumerate(subs):
                            P.mm(pst[3 + si][:nt, 0:512], aT[:, fc, s0:s0 + nt], wt[:, fl, :], fc == 0, fc == 31,
                                 [wk, "aT"], [psk[3 + si]])
                for si, (s0, nt) in enumerate(subs):
                    P.tt("dve", x1[:nt, si, cs], pst[3 + si][:nt, 0:512], x1[:nt, si, cs], ALU.add,
                         [psk[3 + si], x1k + "_%d" % si], [x1k + "_%d" % si])
            for si, (s0, nt) in enumerate(subs):
                st, sk = stat.next()
                k1 = x1k + "_%d" % si
                P.act(junk[:nt], x1[:nt, si, :], AF.Square, [k1], ["junk", sk], accum_out=st[:nt, 0:1])
                rstd_act(st, sk, nt, 1.0 / D)
                P.stt(x1[:nt, si, :], x1[:nt, si, :], st[:nt, 2:3], lnz_bc[:nt], ALU.mult, ALU.mult, [k1, sk, "lnz_bc"], [k1])
            yo = [P.dma("sp", ydst[s0:s0 + nt, :], x1[:nt, si, :], "S" + x1k, reads=[x1k + "_%d" % si])
                  for si, (s0, nt) in enumerate(subs)]
            P.group(yo)

        gens = []
        for j in range(NT):
            gens.append(p4_tile(j * 512, 512, x[j * 512:(j + 1) * 512, :], y[j * 512:(j + 1) * 512, :],
                                ["cat%d_%d" % (c, j) for c in range(8)]))
        if "ps" in phases or "p4s" in phases:
            gens.append(p4_tile(T, NS, xs_in, ys_out, ["cats%d" % c for c in range(8)]))
        run_pipeline(gens)

    print("SBUF peak bytes/partition:", A.peak, "ops:", len(P.all))
    P.emit()
    return nc


_BRANCHES = ((128, 1), (512, 4), (2048, 16))
_NC_CACHE = {}


def kernel(x_prompt, x_sample, state_conv, state_ssm, cache_win_k, cache_win_v, ln_mix, w_in, dn_conv_w, dn_a_log,
           dn_dt_bias, dn_norm, w_out, ln_ffn, w_up=None, w_down=None, ln_final=None, w_ffn_up=None, w_ffn_down=None,
           _phases=("p1", "p2", "p3", "p4", "ps")):
    if w_up is None:
        w_up = w_ffn_up
    if w_down is None:
        w_down = w_ffn_down
    f = lambda a: np.ascontiguousarray(np.asarray(a, dtype=np.float32))
    x_prompt = f(x_prompt); x_sample = f(x_sample)
    B, T, _ = x_prompt.shape
    NSALL = x_sample.shape[0]
    NCORE = 8
    NS = NSALL // NCORE
    LC = cache_win_k.shape[2]
    consts = host_consts(_BRANCHES)
    nc = build(T=T, branches=_BRANCHES, NS=NS, LC=LC, phases=_phases)
    shared = {
        "ln_mix": f(ln_mix).reshape(1, D), "w_in": f(w_in).reshape(D, IN_DIM), "conv_w": f(dn_conv_w).reshape(4, 1536),
        "a_log": f(dn_a_log).reshape(1, 4), "dt_bias": f(dn_dt_bias).reshape(1, 4), "dn_norm": f(dn_norm).reshape(1, 128),
        "w_out": f(w_out).reshape(D, D), "ln_ffn": f(ln_ffn).reshape(1, D), "w_up": f(w_up).reshape(D, DFF),
        "w_down": f(w_down).reshape(DFF, D), "ln_final": f(ln_final).reshape(1, D),
    }
    for k, v in consts.items():
        shared["c_" + k] = v
    sc = f(state_conv)[0]; ss = f(state_ssm)[0]
    ckk = f(cache_win_k)[0].reshape(NSALL, LC, 512); cvv = f(cache_win_v)[0].reshape(NSALL, LC, 512)
    in_maps = []
    for c in range(NCORE):
        m = dict(shared)
        sl = slice(c * NS, (c + 1) * NS)
        m["x"] = x_prompt[c]
        m["xs"] = np.ascontiguousarray(x_sample[sl, 0, :])
        m["sconv"] = np.ascontiguousarray(sc[sl]); m["sssm"] = np.ascontiguousarray(ss[sl])
        m["ck"] = np.ascontiguousarray(ckk[sl]); m["cv"] = np.ascontiguousarray(cvv[sl])
        in_maps.append(m)
    res = run_bass_kernel_spmd(nc, in_maps, core_ids=list(range(NCORE))).results
    KEEP = min(2048, T)
    cat = lambda name: np.stack([np.asarray(r[name]) for r in res], axis=0)
    y_prompt = cat("y")
    y_sample = cat("ys").reshape(NSALL, 1, D)
    p_conv = cat("p_conv")[None]
    p_ssm = cat("p_ssm")[None]
    p_k = cat("p_k").reshape(1, B, KEEP, 8, 64)
    p_v = cat("p_v").reshape(1, B, KEEP, 8, 64)
    s_conv = cat("s_conv").reshape(1, NSALL, 3, 1536)
    s_ssm = cat("s_ssm").reshape(1, NSALL, 4, 128, 128)
    s_k = cat("s_k").reshape(1, NSALL, LC, 8, 64)
    s_v = cat("s_v").reshape(1, NSALL, LC, 8, 64)
    return (y_prompt, y_sample, p_conv, p_ssm, p_k, p_v, s_conv, s_ssm, s_k, s_v)
```

```python
from contextlib import ExitStack
import numpy as np
import concourse.bass as bass
import concourse.mybir as mybir
from concourse.bass_utils import run_bass_kernel_spmd

F32 = mybir.dt.float32
BF16 = mybir.dt.bfloat16
ALU = mybir.AluOpType
AF = mybir.ActivationFunctionType
AX = mybir.AxisListType

D = 1024
KC = 8
IN_DIM = 3592
DFF = 4096
EPS = 1e-6
BIGM = 30000.0


class _Op:
    __slots__ = ("eng", "fn", "waits", "seq", "needed", "incval", "is_dma", "key", "dmaval", "clock", "idx", "seg",
                 "preds", "opreds", "dur", "lat", "start", "finish", "nsucc", "succs", "npend", "bundle", "unit")


def _free_elems(ap):
    n = 1
    for s_ in list(ap.shape)[1:]:
        n *= int(s_)
    return n


class Prog:
    def __init__(self, nc):
        self.nc = nc
        self.stack = ExitStack()
        self.all = []
        self.last_w = {}
        self.readers = {}
        self.seg = 0
        self.cur_bundle = None
        self.nbundle = 0

    def bundle(self):
        prog = self

        class _B:
            def __enter__(self_):
                prog.nbundle += 1
                prog.cur_bundle = prog.nbundle

            def __exit__(self_, *a):
                prog.cur_bundle = None
        return _B()

    def sb(self, name, shape, dtype):
        return self.stack.enter_context(self.nc.sbuf_tensor(name, list(shape), dtype))

    def ps(self, name, shape, dtype):
        return self.stack.enter_context(self.nc.psum_tensor(name, list(shape), dtype))

    def _record(self, op, reads, writes):
        pr = [r for r in reads if r.startswith("psb")]
        if pr:
            reads = [r for r in reads if not r.startswith("psb")]
            writes = list(writes) + pr
        preds = {}
        for r in reads:
            d = self.last_w.get(r)
            if d is not None and d is not op:
                preds[id(d)] = d
        for w in writes:
            d = self.last_w.get(w)
            if d is not None and d is not op:
                preds[id(d)] = d
            for d in self.readers.get(w, ()):
                if d is not op:
                    preds[id(d)] = d
        op.preds = list(preds.values())
        op.opreds = []
        op.idx = len(self.all)
        op.seg = self.seg
        op.bundle = self.cur_bundle
        for r in reads:
            self.readers.setdefault(r, []).append(op)
        for w in writes:
            self.last_w[w] = op
            self.readers[w] = []
        self.all.append(op)

    def op(self, eng, fn, reads=(), writes=(), dur=0.5):
        o = _Op()
        o.eng = eng; o.fn = fn; o.waits = []; o.needed = False; o.is_dma = False
        o.incval = None; o.key = None; o.dmaval = None; o.dur = dur; o.lat = 0.0
        self._record(o, list(reads), list(writes))
        return o

    def dma(self, queue, out, in_, key, reads=(), writes=(), **kw):
        o = _Op()
        o.eng = queue; o.fn = (lambda e: e.dma_start(out=out, in_=in_, **kw))
        o.waits = []; o.needed = True; o.is_dma = True
        o.incval = None; o.key = key; o.dmaval = None
        nbytes = 1
        for s_ in list(out.shape):
            nbytes *= int(s_)
        nbytes *= 4 if out.dtype == F32 else 2
        o.dur = 1.5 if queue == "pool" else 0.12
        o.lat = 2.0 + nbytes / 120e3
        self._record(o, list(reads), list(writes))
        return o

    def group(self, ops):
        last = ops[-1]
        members = set(id(o) for o in ops)
        for res, o in list(self.last_w.items()):
            if id(o) in members:
                self.last_w[res] = last
        for res, lst in self.readers.items():
            if any(id(o) in members for o in lst):
                self.readers[res] = [o for o in lst if id(o) not in members] + [last]
        for a_, b_ in zip(ops[:-1], ops[1:]):
            b_.opreds.append(a_)

    def barrier(self):
        self.seg += 1

    def mm(self, out, lhsT, rhs, start, stop, reads, writes):
        n = _free_elems(out)
        d = 0.035 + n * 0.00043
        if lhsT.dtype == F32:
            d *= 4
        return self.op("pe", lambda e: e.matmul(out, lhsT=lhsT, rhs=rhs, start=start, stop=stop), reads, writes, d)

    def tr(self, out, in_, ident, reads, writes):
        d = 0.035 + _free_elems(out) * 0.00043
        if in_.dtype == F32:
            d *= 4
        return self.op("pe", lambda e: e.transpose(out, in_, ident), reads, writes, d)

    def act(self, out, in_, func, reads, writes, scale=None, bias=None, accum_out=None, eng="act"):
        kw = {}
        if scale is not None:
            kw["scale"] = scale
        if bias is not None:
            kw["bias"] = bias
        if accum_out is not None:
            kw["accum_out"] = accum_out
        d = 0.2 + _free_elems(out) * 0.00075
        return self.op(eng, lambda e: e.activation(out=out, in_=in_, func=func, **kw), reads, writes, d)

    def _dur(self, eng, out):
        n = _free_elems(out)
        if eng == "pool":
            return 0.25 + n * 0.002
        if eng == "act":
            return 0.2 + n * 0.00075
        return 0.12 + n * 0.00105

    def tt(self, eng, out, in0, in1, op, reads, writes):
        return self.op(eng, lambda e: e.tensor_tensor(out=out, in0=in0, in1=in1, op=op), reads, writes, self._dur(eng, out))

    def ts(self, eng, out, in0, s1, op0, reads, writes, s2=None, op1=None, accum_out=None):
        kw = {}
        if op1 is not None:
            kw["op1"] = op1
        if accum_out is not None:
            kw["accum_out"] = accum_out
        return self.op(eng, lambda e: e.tensor_scalar(out=out, in0=in0, scalar1=s1, scalar2=s2, op0=op0, **kw), reads, writes,
                       self._dur(eng, out))

    def stt(self, out, in0, scalar, in1, op0, op1, reads, writes):
        return self.op("dve", lambda e: e.scalar_tensor_tensor(out=out, in0=in0, scalar=scalar, in1=in1, op0=op0, op1=op1),
                       reads, writes, self._dur("dve", out))

    def cp(self, eng, out, in_, reads, writes):
        if eng == "act":
            return self.op("act", lambda e: e.activation(out=out, in_=in_, func=AF.Copy), reads, writes, self._dur("act", out))
        return self.op(eng, lambda e: e.tensor_copy(out=out, in_=in_), reads, writes, self._dur(eng, out))

    def memset(self, eng, out, val, writes):
        return self.op(eng, lambda e: e.memset(out, val), (), writes, self._dur(eng, out))

    def redsum(self, out, in_, reads, writes):
        return self.op("dve", lambda e: e.tensor_reduce(out=out, in_=in_, axis=AX.X, op=ALU.add), reads, writes,
                       self._dur("dve", in_))

    def recip(self, out, in_, reads, writes):
        return self.op("dve", lambda e: e.reciprocal(out=out, in_=in_), reads, writes, 0.2 + _free_elems(out) * 0.004)

    def _schedule(self, ops, t0):
        import heapq
        engs = ("pe", "act", "dve", "pool", "sp")
        inseg = set(id(o) for o in ops)
        units = []
        bmap = {}
        for o in ops:
            if o.bundle is not None and not o.is_dma:
                k = (o.bundle, o.eng)
                u = bmap.get(k)
                if u is None:
                    u = {"ops": [], "eng": o.eng, "idx": o.idx, "npend": 0, "succs": [], "preds": {}}
                    bmap[k] = u
                    units.append(u)
            else:
                u = {"ops": [], "eng": o.eng, "idx": o.idx, "npend": 0, "succs": [], "preds": {}}
                units.append(u)
            u["ops"].append(o)
            o.unit = u
        for u in units:
            for o in u["ops"]:
                for p in o.preds + o.opreds:
                    if id(p) in inseg and p.unit is not u:
                        u["preds"][id(p.unit)] = p.unit
        for u in units:
            u["npend"] = len(u["preds"])
            for pu in u["preds"].values():
                pu["succs"].append(u)
        for u in reversed(units):
            own = sum(o.dur + o.lat for o in u["ops"])
            u["cp"] = own + max([su["cp"] for su in u["succs"]] + [0.0])
        cand = {e: [] for e in engs}
        cnt = [0]
        for u in units:
            if u["npend"] == 0:
                heapq.heappush(cand[u["eng"]], (u["idx"], id(u), u))
        te = {e: t0 for e in engs}
        order = {e: [] for e in engs}
        left = len(units)
        XLAT = 0.15
        CPW = 8

        def ready(u):
            r = t0
            for o in u["ops"]:
                for p in o.preds:
                    if id(p) in inseg and p.unit is not u:
                        f = p.finish + (XLAT if p.eng != o.eng else 0.05)
                        if f > r:
                            r = f
            return r

        while left:
            best = None
            for e in engs:
                if cand[e] and (best is None or te[e] < te[best]):
                    best = e
            e = best
            t = te[e]
            look = heapq.nsmallest(48, cand[e])
            pick = None; pick_r = None; rdy = []
            for (ix, _, u) in look:
                r = ready(u)
                if r <= t:
                    rdy.append((u, r))
                    if len(rdy) >= CPW:
                        break
                elif not rdy and (pick_r is None or r < pick_r):
                    pick = u; pick_r = r
            if rdy:
                base = rdy[0][0]["idx"]
                pick, pick_r = max(rdy, key=lambda ur: (ur[0]["cp"], -ur[0]["idx"]))
            u = pick
            cand[e] = [c_ for c_ in cand[e] if c_[2] is not u]
            heapq.heapify(cand[e])
            tcur = max(t, pick_r)
            for o in u["ops"]:
                o.start = tcur
                if o.is_dma:
                    o.finish = tcur + o.dur + o.lat
                    tcur = tcur + o.dur
                else:
                    o.finish = tcur + o.dur
                    tcur = o.finish
                order[e].append(o)
            te[e] = tcur
            left -= 1
            for su in u["succs"]:
                su["npend"] -= 1
                if su["npend"] == 0:
                    heapq.heappush(cand[su["eng"]], (su["idx"], id(su), su))
        tend = max([t0] + [o.finish for o in ops])
        return order, tend

    def emit(self, verbose=True):
        nc = self.nc
        st = self.stack
        engs = ("pe", "act", "dve", "pool", "sp")
        nseg = self.seg + 1
        final = {e: [] for e in engs}
        t0 = 0.0
        seg_first = []
        for sg in range(nseg):
            ops = [o for o in self.all if o.seg == sg]
            order, t1 = self._schedule(ops, t0)
            seg_first.append({e: (order[e][0] if order[e] else None) for e in engs})
            for e in engs:
                final[e].extend(order[e])
            if verbose:
                print("  segment %d: %d ops, est %.0f us" % (sg, len(ops), t1 - t0))
            t0 = t1
        if verbose:
            print("  estimated total %.0f us" % t0)
        dma_cnt = {}
        for e in engs:
            for i, o in enumerate(final[e]):
                o.seq = i
                if o.is_dma:
                    dma_cnt[o.key] = dma_cnt.get(o.key, 0) + 16
                    o.dmaval = dma_cnt[o.key]
        firsts = {}
        for sg in range(1, nseg):
            lasts = {}
            dl = {}
            for e in engs:
                for o in final[e]:
                    if o.seg >= sg:
                        break
                    if o.is_dma:
                        dl[o.key] = o
                    else:
                        lasts[e] = o
            bar = list(lasts.values()) + list(dl.values())
            for e in engs:
                f = seg_first[sg][e]
                if f is not None:
                    firsts[id(f)] = bar
        clock = {e: {} for e in engs}
        glob = sorted(self.all, key=lambda o: (o.seg, o.start, o.idx))
        for o in glob:
            clk = clock[o.eng]
            deps = list(o.preds) + firsts.get(id(o), [])
            for d in deps:
                if d.is_dma:
                    if clk.get(d.key, 0) >= d.dmaval:
                        continue
                else:
                    if d.eng == o.eng and o.eng == "pe":
                        continue
                    if clk.get(d.eng, -1) >= d.seq:
                        continue
                o.waits.append(d)
                d.needed = True
                for k, v in d.clock.items():
                    if clk.get(k, -1) < v:
                        clk[k] = v
            c = dict(clk)
            if o.is_dma:
                c[o.key] = o.dmaval
            else:
                c[o.eng] = o.seq
            o.clock = c
        esem = {e: st.enter_context(nc.semaphore("s_" + e)) for e in engs}
        dsem = {k: st.enter_context(nc.semaphore("d_%d" % i)) for i, k in enumerate(dma_cnt)}
        for e in engs:
            cum = 0
            for o in final[e]:
                if not o.is_dma and o.needed:
                    cum += 1
                    o.incval = cum
        engh = {"pe": "tensor", "act": "scalar", "dve": "vector", "pool": "gpsimd", "sp": "sync"}
        block = st.enter_context(nc.Block())

        def mk(ename):
            lst = final[ename]

            def body(e):
                fin = {}
                for o in lst:
                    for d in o.waits:
                        if d.is_dma:
                            e.wait_ge(dsem[d.key], d.dmaval)
                        else:
                            e.wait_ge(esem[d.eng], d.incval)
                    ins = o.fn(e)
                    if o.is_dma:
                        ins.then_inc(dsem[o.key], 16)
                        fin[o.key] = o.dmaval
                    elif o.needed:
                        ins.then_inc(esem[ename], 1)
                for k, v in fin.items():
                    e.wait_ge(dsem[k], v)
            return body

        for ename in engs:
            if final[ename]:
                getattr(block, engh[ename])(mk(ename))
        st.close()


class Arena:
    def __init__(self, P, nbytes):
        self.t = P.sb("arena", [128, nbytes // 4], F32)
        self.n = nbytes
        self.lo = 0
        self.hi = nbytes
        self.peak = 0

    def alloc(self, shape, dt, persistent=False):
        n = 1
        for s_ in shape[1:]:
            n *= s_
        nb = (n * (4 if dt == F32 else 2) + 31) // 32 * 32
        if persistent:
            off = self.lo
            self.lo += nb
        else:
            self.hi -= nb
            off = self.hi
        assert self.lo <= self.hi, "SBUF arena overflow: lo=%d hi=%d" % (self.lo, self.hi)
        self.peak = max(self.peak, self.lo + (self.n - self.hi))
        v = self.t[0:shape[0], off // 4:(off + nb) // 4]
        if dt != F32:
            v = v.bitcast(dt)
        v = v[:, 0:n]
        if len(shape) == 3:
            v = v.rearrange("p (a b) -> p a b", a=shape[1])
        elif len(shape) == 4:
            v = v.rearrange("p (a b c) -> p a b c", a=shape[1], b=shape[2])
        return v

    def phase_reset(self):
        self.hi = self.n


class Ring:
    def __init__(self, A, name, n, shape, dtype, persistent=False):
        self.t = [A.alloc(shape, dtype, persistent) for i in range(n)]
        self.k = ["%s%d" % (name, i) for i in range(n)]
        self.i = 0

    def next(self):
        j = self.i % len(self.t)
        self.i += 1
        return self.t[j], self.k[j]


class PsPool:
    def __init__(self, tens, keys):
        self.t = tens; self.k = keys; self.i = 0

    def next(self):
        j = self.i % len(self.t)
        self.i += 1
        return self.t[j], self.k[j]


def run_pipeline(gens, max_active=1000):
    active = []
    gens = list(gens)
    gi = 0
    while gi < len(gens) or active:
        if gi < len(gens) and len(active) < max_active:
            active.append(gens[gi]); gi += 1
        nxt = []
        for g in active:
            try:
                next(g)
                nxt.append(g)
            except StopIteration:
                pass
        active = nxt


def alibi_slope(h):
    return 2.0 ** (-8.0 * (h + 1) / 8)


def host_consts(branches):
    c = {}
    c["ident"] = np.eye(128, dtype=np.float32)
    p = np.arange(128)[:, None]
    f = np.arange(128)[None, :]
    c["umat"] = (p <= f).astype(np.float32)
    c["maskA"] = np.where(f >= p, BIGM, 0.0).astype(np.float32)
    c["maskC"] = np.where(f < p, -BIGM, 0.0).astype(np.float32)
    bd32 = ((p // 32) == (f // 32)).astype(np.float32)
    bd64 = ((p // 64) == (f // 64)).astype(np.float32)
    c["bmask"] = np.concatenate([bd32, bd64 - bd32, 1.0 - bd64], axis=1)
    NEG = -1.0e5
    j = p; i = f
    dprev = np.where(i <= j, (i - j + 128).astype(np.float32), np.nan)
    dcur = np.where(i >= j, (i - j).astype(np.float32), np.nan)
    tab = np.zeros((8, len(branches), 128, 256), np.float32)
    for h in range(8):
        for g, (win, dil) in enumerate(branches):
            cc = -alibi_slope(h) * dil * 8.0
            a = np.where(np.isnan(dprev), NEG, cc * np.nan_to_num(dprev))
            b = np.where(np.isnan(dcur), NEG, cc * np.nan_to_num(dcur))
            tab[h, g, :, 0:128] = a
            tab[h, g, :, 128:256] = b
    sm_ = np.zeros((8, 512), np.float32)
    for h in range(8):
        sm_[h, h * 64:(h + 1) * 64] = 1.0
    c["smask"] = sm_
    sb_ = np.zeros((128, len(branches), 8), np.float32)
    for g, (win, dil) in enumerate(branches):
        for h in range(8):
            sb_[:, g, h] = -alibi_slope(h) * dil * (128 - np.arange(128))
    c["sbias"] = sb_.reshape(128, len(branches) * 8)
    c["dist"] = np.ascontiguousarray(tab.transpose(2, 0, 1, 3).reshape(128, 8 * len(branches) * 256))
    return c


WIN_GROUPS = [(0, 512), (512, 512), (1024, 512), (1536, 512), (2048, 8), (2056, 512), (2568, 512), (3080, 512)]
G_QD, G_KD, G_VD, G_Z, G_BA, G_QA, G_KA, G_VA = range(8)
ARENA_BYTES = 196 * 1024


def build(T=4096, branches=((128, 1), (512, 4), (2048, 16)), NS=4, LC=2048, phases=("p1", "p2", "p3", "p4", "ps"), dbg=()):
    nc = bass.Bass("TRN2", target_bir_lowering=False)
    P = Prog(nc)
    A = Arena(P, ARENA_BYTES)
    NBR = len(branches)
    WINDOW = max(w for w, _ in branches)
    KEEP = min(WINDOW, T)
    NT = T // 512

    def din(name, shape, dt=F32):
        return nc.dram_tensor(name, list(shape), dt, kind="ExternalInput").ap()

    def dout(name, shape, dt=F32):
        return nc.dram_tensor(name, list(shape), dt, kind="ExternalOutput").ap()

    def dscr(name, shape, dt=BF16):
        return nc.dram_tensor(name, list(shape), dt, kind="Internal").ap()

    def pa(shape, dt):
        return A.alloc(shape, dt, True)

    def ta(shape, dt):
        return A.alloc(shape, dt, False)

    def new_phase():
        P.barrier()
        A.phase_reset()

    x = din("x", [T, D]); xs_in = din("xs", [NS, D])
    sconv = din("sconv", [NS, 3, 1536]); sssm = din("sssm", [NS, 4, 128, 128])
    ck = din("ck", [NS, LC, 512]); cv = din("cv", [NS, LC, 512])
    ln_mix = din("ln_mix", [1, D]); w_in = din("w_in", [D, IN_DIM]); conv_w = din("conv_w", [4, 1536])
    a_log = din("a_log", [1, 4]); dt_bias = din("dt_bias", [1, 4]); dn_norm = din("dn_norm", [1, 128])
    w_out = din("w_out", [D, D]); ln_ffn = din("ln_ffn", [1, D]); w_up = din("w_up", [D, DFF])
    w_down = din("w_down", [DFF, D]); ln_final = din("ln_final", [1, D])
    c_ident = din("c_ident", [128, 128]); c_umat = din("c_umat", [128, 128])
    c_maskA = din("c_maskA", [128, 128]); c_maskC = din("c_maskC", [128, 128])
    c_dist = din("c_dist", [128, 8 * NBR * 256])
    c_bmask = din("c_bmask", [128, 384])
    c_smask = din("c_smask", [8, 512]); c_sbias = din("c_sbias", [128, NBR * 8])

    y = dout("y", [T, D]); ys_out = dout("ys", [NS, D])
    p_conv = dout("p_conv", [3, 1536]); p_ssm = dout("p_ssm", [4, 128, 128])
    p_k = dout("p_k", [KEEP, 512]); p_v = dout("p_v", [KEEP, 512])
    s_conv = dout("s_conv", [NS, 3, 1536]); s_ssm = dout("s_ssm", [NS, 4, 128, 128])
    s_k = dout("s_k", [NS, LC, 512]); s_v = dout("s_v", [NS, LC, 512])
    dbgout = {}
    dbgin = {}
    for name, shape in dbg:
        if name.startswith("in_"):
            dbgin[name] = din("dbg_" + name, shape)
        else:
            dbgout[name] = dout("dbg_" + name, shape)

    win_s = dscr("win_s", [8, 128, KC, 512])
    wout_s = dscr("wout_s", [2, 128, KC, 512])
    wup_s = dscr("wup_s", [8, 128, KC, 512])
    wdn_s = dscr("wdn_s", [8, 128, KC, 512])
    TC = T + 128
    cat_s = dscr("cat_s", [KC, 128, TC])
    cat_smp = dscr("cat_smp", [128, KC, NS])

    identb = pa([128, 128], BF16)
    identf = pa([128, 128], F32)
    umat = pa([128, 128], F32)
    maskA = pa([128, 128], F32)
    maskC = pa([128, 128], F32)
    onesb = pa([128, 128], BF16)
    onesf = pa([128, 128], F32)
    negonesf = pa([128, 128], F32)
    negonesb = pa([128, 128], BF16)
    lnm_bc = pa([128, D], F32)
    epsc = pa([128, 4], F32)
    bmask = pa([128, 3, 128], BF16)

    pst = [P.ps("psb%d" % i, [128, 512], F32) for i in range(8)]
    psk = ["psb%d" % i for i in range(8)]

    wring = Ring(A, "wr", 4, [128, KC, 512], BF16, True)
    stat = Ring(A, "stat", 6, [128, 4], F32, True)
    trp = PsPool([pst[7]], [psk[7]])
    H = {}

    def phase_helpers():
        H["xring"] = Ring(A, "xt", 2, [128, D], F32)
        H["hbring"] = Ring(A, "hb", 2, [128, D], BF16)
        H["junk"] = ta([128, D], BF16)

    g0 = [P.dma("sp", identf, c_ident, "C0", writes=["identf"]),
          P.dma("sp", umat, c_umat, "C0", writes=["umat"]),
          P.dma("sp", maskA, c_maskA, "C0", writes=["maskA"]),
          P.dma("sp", maskC, c_maskC, "C0", writes=["maskC"]),
          P.dma("sp", lnm_bc, ln_mix.partition_broadcast(128), "C0", writes=["lnm_bc"])]
    P.group(g0)
    g0 = [P.dma("pool", identb, c_ident, "C1", writes=["identb"]),
          P.dma("pool", bmask, c_bmask.rearrange("p (a f) -> p a f", a=3), "C1", writes=["bmask"])]
    P.group(g0)
    P.memset("dve", onesb, 1.0, ["onesb"])
    P.memset("dve", onesf, 1.0, ["onesf"])
    P.memset("dve", negonesf, -1.0, ["negonesf"])
    P.memset("dve", negonesb, -1.0, ["negonesb"])
    P.memset("dve", epsc, EPS, ["epsc"])

    cast_jobs = []
    wres = {}

    def add_cast(pieces, name):
        wres[name] = "%s_%d" % (name, len(pieces) - 1)
        cast_jobs.append((name, pieces))

    wv = w_in.rearrange("(kc p) c -> p kc c", p=128)
    for t in (G_QA, G_KA, G_VA, G_QD, G_KD, G_VD, G_Z, G_BA):
        off, n = WIN_GROUPS[t]
        add_cast([(win_s[t, :, kc, 0:n], wv[:, kc, off:off + n]) for kc in range(KC)], "win_s%d" % t)
    wv = w_out.rearrange("(kc p) c -> p kc c", p=128)
    for t in range(2):
        add_cast([(wout_s[t, :, kc, :], wv[:, kc, t * 512:(t + 1) * 512]) for kc in range(KC)], "wout_s%d" % t)
    wv = w_up.rearrange("(kc p) c -> p kc c", p=128)
    for t in range(8):
        add_cast([(wup_s[t, :, kc, :], wv[:, kc, t * 512:(t + 1) * 512]) for kc in range(KC)], "wup_s%d" % t)
    wv = w_down.rearrange("(fc p) c -> p fc c", p=128)
    for t in range(8):
        half, g = t // 4, t % 4
        add_cast([(wdn_s[t, :, fl, :], wv[:, g * 8 + fl, half * 512:(half + 1) * 512]) for fl in range(8)], "wdn_s%d" % t)

    cast_prev = [None]

    def cast_some(n):
        for _ in range(n):
            if not cast_jobs:
                return
            name, pieces = cast_jobs.pop(0)
            ops = []
            for i, (dst, src) in enumerate(pieces):
                rd = [cast_prev[0]] if (i == 0 and cast_prev[0]) else []
                ops.append(P.dma("pool", dst, src, "CAST", reads=rd, writes=["%s_%d" % (name, i)], max_dma_last_dim=2048))
            P.group(ops)
            cast_prev[0] = "%s_%d" % (name, len(pieces) - 1)

    def wload(scr, t, name, ncols=512):
        while any(j[0] == name for j in cast_jobs):
            cast_some(1)
        wt, wk = wring.next()
        P.dma("sp", wt[:, :, 0:ncols], scr[t, :, :, 0:ncols], "L" + wk, reads=[wres[name]], writes=[wk])
        return wt, wk

    cast_some(3)

    def bulk_copies():
        nel = (LC - 1) * 512
        for n in range(NS):
            for (src, dst) in ((ck, s_k), (cv, s_v)):
                sv = src[n].rearrange("l c -> (l c)")[512:512 + nel].rearrange("(a b) -> a b", a=16)
                dv = dst[n].rearrange("l c -> (l c)")[0:nel].rearrange("(a b) -> a b", a=16)
                P.dma("act", dv, sv, "OUTps")
        P.dma("act", s_conv[:, 0:2, :], sconv[:, 1:3, :], "OUTps")

    def rstd_act(st, sk, nt, scale):
        P.act(st[:nt, 1:2], st[:nt, 0:1], AF.Ln, [sk, "epsc"], [sk], scale=scale, bias=epsc[:nt, 0:1])
        P.act(st[:nt, 2:3], st[:nt, 1:2], AF.Exp, [sk], [sk], scale=-0.5)

    def norm_T(xt, xk, nt, lnw, lnk, dst, dstk):
        st, sk = stat.next()
        hb, hk = H["hbring"].next()
        junk = H["junk"]
        P.act(junk[:nt], xt[:nt], AF.Square, [xk], ["junk", sk], accum_out=st[:nt, 0:1])
        rstd_act(st, sk, nt, 1.0 / D)
        P.stt(hb[:nt], xt[:nt], st[:nt, 2:3], lnw[:nt], ALU.mult, ALU.mult, [xk, sk, lnk], [hk])
        pt, pk = trp.next()
        pb = pt[:].bitcast(BF16).rearrange("p (k n) -> p k n", k=8)
        for kc in range(KC):
            P.tr(pb[:, kc, 0:nt], hb[:nt, kc * 128:(kc + 1) * 128], identb[:nt, :nt], [hk, "identb"], [pk])
        P.cp("act", dst, pb[:, :, 0:nt], [pk], [dstk])

    def dbg_store(name, src_ap, reskey, dst_ap=None, key=None):
        if name in dbgout:
            P.dma("pool", dbgout[name] if dst_ap is None else dst_ap, src_ap, key or ("DBG" + name), reads=[reskey],
                  max_dma_last_dim=2048)

    if "p1" in phases:
        QKVT = [ta([128, 4, T], BF16) for i in range(3)]
        distb = ta([128, 8 * NBR * 256], BF16)
        P.dma("pool", distb, c_dist, "c8", writes=["distb"], max_dma_last_dim=2048)
        a_mark = A.hi
        phase_helpers()
        xring = H["xring"]
        hTr = Ring(A, "hT", 2, [128, KC, 512], BF16)
        kvst = Ring(A, "kvst", 2, [128, 512], F32)
        mmp = PsPool(pst[0:4], psk[0:4])
        ev = [0]

        def p1_tile(j):
            hT, hTk = hTr.next()
            for sub in range(4):
                xt, xk = xring.next()
                r0 = j * 512 + sub * 128
                P.dma("sp", xt, x[r0:r0 + 128, :], "L" + xk, writes=[xk])
                norm_T(xt, xk, 128, lnm_bc, "lnm_bc", hT[:, :, sub * 128:(sub + 1) * 128], hTk)
            yield
            for which in range(3):
                wt, wk = wload(win_s, G_QA + which, "win_s%d" % (G_QA + which))
                for c4 in range(4):
                    pt, pk = mmp.next()
                    for kc in range(KC):
                        P.mm(pt[:, 0:512], wt[:, kc, c4 * 128:(c4 + 1) * 128], hT[:, kc, :], kc == 0, kc == KC - 1,
                             [wk, hTk], [pk])
                    dstT = QKVT[which][:, c4, j * 512:(j + 1) * 512]
                    P.cp(("act", "dve")[ev[0] % 2], dstT, pt[:, 0:512], [pk], ["qkvt%d" % which]); ev[0] += 1
                if which >= 1:
                    for sub in range(4):
                        tok0 = j * 512 + sub * 128
                        if tok0 < T - KEEP:
                            continue
                        pt, pk = mmp.next()
                        for kc in range(KC):
                            P.mm(pt[:, 0:512], hT[:, kc, sub * 128:(sub + 1) * 128], wt[:, kc, :], kc == 0, kc == KC - 1,
                                 [wk, hTk], [pk])
                        st_, stk = kvst.next()
                        P.cp(("act", "dve")[ev[0] % 2], st_, pt[:, 0:512], [pk], [stk]); ev[0] += 1
                        o0 = tok0 - (T - KEEP)
                        P.dma("sp", (p_k, p_v)[which - 1][o0:o0 + 128, :], st_, "S" + stk, reads=[stk])
                if which < 2:
                    yield
            cast_some(1)

        run_pipeline([p1_tile(j) for j in range(NT)], max_active=2)
        for i in range(3):
            dbg_store("qkvt%d" % i, QKVT[i], "qkvt%d" % i)

    if "p2" in phases:
        P.barrier()
        if "ps" in phases:
            bulk_copies()
        A.hi = a_mark
        QT, KT, VT = QKVT
        oacc = ta([128, 2, T], F32)
        oast = Ring(A, "oast", 2, [128, 1024], BF16)
        ptr = Ring(A, "ptr", 6, [128, 256], BF16)
        vtr = Ring(A, "vtok", 4, [128, 128], BF16)
        stA = PsPool([pst[0], pst[2]], [psk[0], psk[2]])
        stB = PsPool([pst[1], pst[3]], [psk[1], psk[3]])
        pvp = PsPool([pst[4], pst[5], pst[7]], [psk[4], psk[5], psk[7]])
        vtp = PsPool([pst[6]], [psk[6]])
        vslots = {}

        def blk(tens, hp, d, r, b):
            return tens[:, hp, :].rearrange("p (n d) -> p n d", d=d)[:, b * 128:(b + 1) * 128, r]

        def unit(hp, g, d, r, b):
            vt, vk = vtr.next()
            vslots[(hp, g, r, b)] = (vt, vk)
            pv_, pvk = vtp.next()
            pvb = pv_[:].bitcast(BF16)
            bctx = P.bundle(); bctx.__enter__()
            P.tr(pvb[:, 0:128], blk(VT, hp, d, r, b), identb, ["qkvt2", "identb"], [pvk])
            banks = []
            qb = blk(QT, hp, d, r, b)
            for h in range(2):
                pt, pk = (stA, stB)[h].next()
                banks.append((pt, pk))
                doff = ((hp * 2 + h) * NBR + g) * 256
                P.mm(pt[:, 0:256], identb, distb[:, doff:doff + 256], True, False, ["identb", "distb"], [pk])
            for half in ((0, 1) if b > 0 else (1,)):
                for h in range(2):
                    pt, pk = banks[h]
                    rows = slice(h * 64, (h + 1) * 64)
                    kb_ = blk(KT, hp, d, r, b - 1 + half)
                    P.mm(pt[:, half * 128:(half + 1) * 128], kb_[rows], qb[rows], False, half == 1, ["qkvt0", "qkvt1"], [pk])
            bctx.__exit__()
            yield
            P.cp("dve", vt, pvb[:, 0:128], [pvk], [vk])
            pts = []
            for h in range(2):
                pt, pk = banks[h]
                e_, ek = ptr.next()
                P.act(e_, pt[:, 0:256], AF.Exp, [pk], [ek], scale=0.125)
                pts.append((e_, ek))
            yield
            pc, pck = pvp.next()
            halves = ([(0, vslots[(hp, g, r, b - 1)])] if b > 0 else []) + [(1, (vt, vk))]
            bctx = P.bundle(); bctx.__enter__()
            for grp in range(2):
                for i, (half, (vv, vvk)) in enumerate(halves):
                    for h in range(2):
                        rows = slice(h * 64, (h + 1) * 64)
                        e_, ek = pts[h]
                        lhs = vv[:, h * 64:(h + 1) * 64] if grp == 0 else onesb[:, 0:64]
                        P.mm(pc[rows, grp * 128:(grp + 1) * 128], lhs, e_[:, half * 128:(half + 1) * 128],
                             i == 0, i == len(halves) - 1, [vvk, ek, "onesb"], [pck])
            bctx.__exit__()
            yield
            ov = oacc.rearrange("p a (n d) -> p a n d", d=d)[:, :, b * 128:(b + 1) * 128, r]
            pcv = pc[:, 0:256].rearrange("p (a q) -> p a q", a=2)
            if g == 0:
                P.cp("dve", ov, pcv, [pck], ["oacc"])
            else:
                P.tt("dve", ov, pcv, ov, ALU.add, [pck, "oacc"], ["oacc"])

        for hp in range(4):
            gens = []
            for g, (win, d) in enumerate(branches):
                nb = T // (128 * d)
                for r in range(d):
                    for b in range(nb):
                        gens.append(unit(hp, g, d, r, b))
            run_pipeline(gens)
            for c0 in range(0, T, 1024):
                n = min(1024, T - c0)
                P.recip(oacc[:, 1, c0:c0 + n], oacc[:, 1, c0:c0 + n], ["oacc"], ["oacc"])
                o_, ok_ = oast.next()
                P.tt("dve", o_[:, 0:n], oacc[:, 0, c0:c0 + n], oacc[:, 1, c0:c0 + n], ALU.mult, ["oacc"], [ok_])
                P.dma("sp", cat_s[4 + hp, :, c0:c0 + n], o_[:, 0:n], "S" + ok_, reads=[ok_],
                      writes=["cat%d_%d" % (4 + hp, tt_) for tt_ in range(c0 // 512, (c0 + n) // 512)])
                dbg_store("oaT", o_[:, 0:n], ok_, dbgout.get("oaT", [None] * 8)[hp, :, c0:c0 + n] if "oaT" in dbgout else None,
                          "DBGoaT%d_%d" % (hp, c0))
            cast_some(2)

    if "ps" in phases:
        new_phase()
        if "p2" not in phases:
            bulk_copies()
        phase_helpers()
        xring = H["xring"]
        pp = PsPool(pst[0:6], psk[0:6])
        trp.t = [pst[5]]; trp.k = [psk[5]]
        psNum, kNum = pst[6], psk[6]
        psDen, kDen = pst[7], psk[7]
        n4 = NS
        S_all = ta([128, NS * 4, 128], F32)
        Snew = ta([128, NS * 4, 128], F32)
        g0 = [P.dma("sp", S_all[:, n * 4:(n + 1) * 4, :], sssm[n].rearrange("h k v -> k h v"), "Lsall", writes=["S_all"])
              for n in range(NS)]
        P.group(g0)
        proj4 = ta([n4, IN_DIM], F32)
        hTs = ta([128, KC, n4], BF16)
        acc = ta([n4, 1536], F32)
        tmpc = ta([n4, 1536], F32)
        scr_ = Ring(A, "scr", 2, [n4, 1536], F32)
        cwr = Ring(A, "cwr", 2, [n4, 1536], F32)
        I4bc = ta([128, 4, 4], F32)
        a4 = ta([n4, 4], F32); dtb4 = ta([n4, 4], F32); nrm4 = ta([n4, 128], F32)
        sml = ta([n4, 16, 8], F32)
        smask = ta([8, 512], F32); sbias = ta([128, NBR * 8], F32)
        QKcol = ta([128, 8, 4], F32)
        Sel = ta([128, 8, 4, 4], F32)
        qS4 = ta([n4, 4, 128], F32); kS4 = ta([n4, 4, 128], F32)
        vnew4 = ta([n4, 4, 128], F32); o4 = ta([n4, 4, 128], F32); t4a = ta([n4, 4, 128], F32)
        ksel = Ring(A, "ksel", 2, [n4, 4, 128], F32)
        EgSel = ta([n4, 4, 4], F32); EgB = ta([128, 16], F32)
        RowSel = ta([n4, 4, 128], F32)
        z4 = ta([n4, 512], F32)
        cat4 = ta([n4, 1024], F32)
        kgr = Ring(A, "kg", 2, [128, 512], F32); vgr = Ring(A, "vg", 2, [128, 512], F32)
        Qbs = ta([128, 512], F32); prod = ta([128, 512], F32)
        sc8 = Ring(A, "sc8", 2, [128, 8], F32)
        Xm = Ring(A, "Xm", 2, [8, 512], F32)
        num4 = ta([n4, 512], F32); den4 = ta([n4, 8], F32)
        catb = ta([128, KC, n4], BF16)
        I4 = identf[0:n4, 0:n4]

        g0 = [P.dma("sp", a4, a_log.partition_broadcast(n4), "C2", writes=["a4"]),
              P.dma("sp", dtb4, dt_bias.partition_broadcast(n4), "C2", writes=["dtb4"]),
              P.dma("sp", nrm4, dn_norm.partition_broadcast(n4), "C2", writes=["nrm4"]),
              P.dma("sp", smask, c_smask, "C2", writes=["smask"]),
              P.dma("sp", sbias, c_sbias, "C2", writes=["sbias"])]
        P.group(g0)
        P.act(a4, a4, AF.Exp, ["a4"], ["a4"])
        P.ts("dve", a4, a4, -1.0, ALU.mult, ["a4"], ["a4"])
        P.memset("dve", I4bc, 0.0, ["I4bc"])
        for n in range(4):
            P.memset("dve", I4bc[:, n, n:n + 1], 1.0, ["I4bc"])
        P.tt("dve", RowSel, onesf[0:n4, :].unsqueeze(1).to_broadcast([n4, 4, 128]), I4.unsqueeze(2).to_broadcast([n4, 4, 128]),
             ALU.mult, ["onesf", "identf"], ["RowSel"])

        xt, xk = xring.next()
        P.dma("sp", xt[:n4], xs_in, "L" + xk, writes=[xk])
        norm_T(xt, xk, n4, lnm_bc, "lnm_bc", hTs, "hTs")
        for grp in range(8):
            off, n = WIN_GROUPS[grp]
            wt, wk = wload(win_s, grp, "win_s%d" % grp, ncols=n)
            pt, pk = pp.next()
            for kc in range(KC):
                P.mm(pt[0:n4, 0:n], hTs[:, kc, :], wt[:, kc, 0:n], kc == 0, kc == KC - 1, [wk, "hTs"], [pk])
            P.cp("act", proj4[:, off:off + n], pt[0:n4, 0:n], [pk], ["proj4"])
        g0 = [P.dma("sp", s_conv[:, 2, :], proj4[:, 0:1536], "OUTp2", reads=["proj4"]),
              P.dma("sp", s_k[:, LC - 1, :], proj4[:, 2568:3080], "OUTp2", reads=["proj4"]),
              P.dma("sp", s_v[:, LC - 1, :], proj4[:, 3080:3592], "OUTp2", reads=["proj4"])]
        P.group(g0)
        q4a = proj4[:, 2056:2568]; k4a = proj4[:, 2568:3080]; v4a = proj4[:, 3080:3592]

        cw_, cwk = cwr.next()
        P.dma("sp", cw_, conv_w[3:4, :].partition_broadcast(n4), "L" + cwk, writes=[cwk])
        P.tt("dve", acc, proj4[:, 0:1536], cw_, ALU.mult, ["proj4", cwk], ["acc"])
        for i in range(3):
            cw_, cwk = cwr.next(); sc_, sck = scr_.next()
            P.dma("sp", cw_, conv_w[i:i + 1, :].partition_broadcast(n4), "L" + cwk, writes=[cwk])
            P.dma("sp", sc_, sconv[:, i, :], "L" + sck, writes=[sck])
            P.tt("dve", tmpc, sc_, cw_, ALU.mult, [sck, cwk], ["tmpc"])
            P.tt("dve", acc, acc, tmpc, ALU.add, ["acc", "tmpc"], ["acc"])
        P.act(acc, acc, AF.Silu, ["acc"], ["acc"])
        P.act(z4, proj4[:, 1536:2048], AF.Silu, ["proj4"], ["z4"])
        qk = acc[:, 0:1024].rearrange("p (i d) -> p i d", i=8)
        v4 = acc[:, 1024:1536].rearrange("p (h d) -> p h d", h=4)
        P.tt("dve", tmpc[:, 0:1024], acc[:, 0:1024], acc[:, 0:1024], ALU.mult, ["acc"], ["tmpc"])
        P.redsum(sml[:, 0, :], tmpc[:, 0:1024].rearrange("p (i d) -> p i d", i=8), ["tmpc"], ["sml"])
        P.act(sml[:, 1, :], sml[:, 0, :], AF.Ln, ["sml", "epsc"], ["sml"], bias=epsc[0:n4, 0:1])
        P.act(sml[:, 1, :], sml[:, 1, :], AF.Exp, ["sml"], ["sml"], scale=-0.5)
        P.ts("dve", sml[:, 1, 0:4], sml[:, 1, 0:4], float(128.0 ** -0.5), ALU.mult, ["sml"], ["sml"])
        P.tt("dve", qk, qk, sml[:, 1, :].unsqueeze(2).to_broadcast([n4, 8, 128]), ALU.mult, ["acc", "sml"], ["acc"])
        beta4 = sml[:, 2, 0:4]; g4 = sml[:, 3, 0:4]; eg4 = sml[:, 4, 0:4]
        P.act(sml[:, 5, 0:4], proj4[:, 2048:2052], AF.Exp, ["proj4"], ["sml"], scale=-1.0)
        P.ts("dve", sml[:, 5, 0:4], sml[:, 5, 0:4], 1.0, ALU.add, ["sml"], ["sml"])
        P.recip(beta4, sml[:, 5, 0:4], ["sml"], ["sml"])
        P.tt("dve", g4, proj4[:, 2052:2056], dtb4, ALU.add, ["proj4", "dtb4"], ["sml"])
        P.act(g4, g4, AF.Exp, ["sml"], ["sml"])
        P.act(g4, g4, AF.Ln, ["sml", "onesf"], ["sml"], bias=onesf[0:n4, 0:1])
        P.tt("dve", g4, g4, a4, ALU.mult, ["sml", "a4"], ["sml"])
        P.act(eg4, g4, AF.Exp, ["sml"], ["sml"])

        def b4(ap):
            return ap.unsqueeze(2).to_broadcast([n4, 4, 128])

        pt, pk = pp.next()
        ptv = pt[:, 0:32].rearrange("p (i m) -> p i m", i=8)
        for i in range(8):
            P.tr(ptv[:, i, :], qk[:, i, :], I4, ["acc", "identf"], [pk])
        P.cp("dve", QKcol, ptv, [pk], ["QKcol"])
        P.tt("dve", Sel, QKcol.unsqueeze(2).to_broadcast([128, 8, 4, 4]), I4bc.unsqueeze(1).to_broadcast([128, 8, 4, 4]), ALU.mult,
             ["QKcol", "I4bc"], ["Sel"])
        for (i0, dstt, dk_) in ((0, qS4, "qS4"), (4, kS4, "kS4")):
            pt, pk = pp.next()
            for h in range(4):
                for n in range(NS):
                    P.mm(pt[0:n4, h * 128:(h + 1) * 128], Sel[:, i0 + h, n, :], S_all[:, n * 4 + h, :], n == 0, n == NS - 1,
                         ["Sel", "S_all"], [pk])
            P.cp("act", dstt, pt[0:n4, 0:512].rearrange("p (h d) -> p h d", h=4), [pk], [dk_])
        q4 = qk[:, 0:4, :]; k4 = qk[:, 4:8, :]
        P.tt("dve", t4a, kS4, b4(eg4), ALU.mult, ["kS4", "sml"], ["t4a"])
        P.tt("dve", t4a, v4, t4a, ALU.subtract, ["acc", "t4a"], ["t4a"])
        P.tt("dve", vnew4, t4a, b4(beta4), ALU.mult, ["t4a", "sml"], ["vnew4"])
        P.tt("dve", t4a, q4, k4, ALU.mult, ["acc"], ["t4a"])
        P.redsum(sml[:, 6, 0:4], t4a, ["t4a"], ["sml"])
        P.tt("dve", o4, qS4, b4(eg4), ALU.mult, ["qS4", "sml"], ["o4"])
        P.tt("dve", t4a, vnew4, b4(sml[:, 6, 0:4]), ALU.mult, ["vnew4", "sml"], ["t4a"])
        P.tt("dve", o4, o4, t4a, ALU.add, ["o4", "t4a"], ["o4"])
        P.tt("dve", EgSel, eg4.unsqueeze(1).to_broadcast([n4, 4, 4]), I4.unsqueeze(2).to_broadcast([n4, 4, 4]), ALU.mult,
             ["sml", "identf"], ["EgSel"])
        pt, pk = pp.next()
        P.mm(pt[:, 0:16], onesf[0:n4, :], EgSel.rearrange("p n h -> p (n h)"), True, True, ["onesf", "EgSel"], [pk])
        P.cp("dve", EgB, pt[:, 0:16], [pk], ["EgB"])
        for n in range(NS):
            ks_, ksk = ksel.next()
            P.ts("dve", ks_, k4, I4[:, n:n + 1], ALU.mult, ["acc", "identf"], [ksk])
            pt, pk = pp.next()
            for h in range(4):
                P.mm(pt[:, h * 128:(h + 1) * 128], ks_[:, h, :], vnew4[:, h, :], True, True, [ksk, "vnew4"], [pk])
            for h in range(4):
                nh = n * 4 + h
                P.stt(Snew[:, nh, :], S_all[:, nh, :], EgB[:, nh:nh + 1], pt[:, h * 128:(h + 1) * 128], ALU.mult, ALU.add,
                      ["S_all", "EgB", pk], ["Snew"])
        g0 = [P.dma("sp", s_ssm[n].rearrange("h k v -> k h v"), Snew[:, n * 4:(n + 1) * 4, :], "Sssm", reads=["Snew"])
              for n in range(NS)]
        P.group(g0)
        P.tt("dve", t4a, o4, o4, ALU.mult, ["o4"], ["t4a"])
        P.redsum(sml[:, 7, 0:4], t4a, ["t4a"], ["sml"])
        P.act(sml[:, 8, 0:4], sml[:, 7, 0:4], AF.Ln, ["sml", "epsc"], ["sml"], scale=1.0 / 128, bias=epsc[0:n4, 0:1])
        P.act(sml[:, 8, 0:4], sml[:, 8, 0:4], AF.Exp, ["sml"], ["sml"], scale=-0.5)
        od4 = cat4[:, 0:512].rearrange("p (h d) -> p h d", h=4)
        P.tt("dve", o4, o4, b4(sml[:, 8, 0:4]), ALU.mult, ["o4", "sml"], ["o4"])
        P.tt("dve", o4, o4, nrm4.unsqueeze(1).to_broadcast([n4, 4, 128]), ALU.mult, ["o4", "nrm4"], ["o4"])
        P.tt("dve", od4, o4, z4.rearrange("p (h d) -> p h d", h=4), ALU.mult, ["o4", "z4"], ["cat4"])

        P.tt("dve", tmpc[:, 0:512], q4a, k4a, ALU.mult, ["proj4"], ["tmpc"])
        P.redsum(sml[:, 9, :], tmpc[:, 0:512].rearrange("p (h d) -> p h d", h=8), ["tmpc"], ["sml"])
        P.act(sml[:, 10, :], sml[:, 9, :], AF.Exp, ["sml"], ["sml"], scale=0.125)
        P.ts("dve", sml[:, 10, :], sml[:, 10, :], float(NBR), ALU.mult, ["sml"], ["sml"])
        first = True
        for n in range(NS):
            pt, pk = pp.next()
            P.mm(pt[:, 0:512], RowSel[:, n, :], q4a, True, True, ["RowSel", "proj4"], [pk])
            P.cp("act", Qbs, pt[:, 0:512], [pk], ["Qbs"])
            for g, (win, d) in enumerate(branches):
                kg, kgk = kgr.next(); vg, vgk = vgr.next()
                rows = ck[n].rearrange("(a d) c -> a d c", d=d)[LC // d - 128:LC // d, 0, :]
                rows_v = cv[n].rearrange("(a d) c -> a d c", d=d)[LC // d - 128:LC // d, 0, :]
                P.dma("sp", kg, rows, "L" + kgk, writes=[kgk])
                P.dma("sp", vg, rows_v, "L" + vgk, writes=[vgk])
                P.tt("dve", prod, kg, Qbs, ALU.mult, [kgk, "Qbs"], ["prod"])
                s8, s8k = sc8.next()
                P.redsum(s8, prod.rearrange("p (h d) -> p h d", h=8), ["prod"], [s8k])
                P.stt(s8, s8, 0.125, sbias[:, g * 8:(g + 1) * 8], ALU.mult, ALU.add, [s8k, "sbias"], [s8k])
                P.act(s8, s8, AF.Exp, [s8k], [s8k])
                pt, pk = pp.next()
                P.mm(pt[0:8, 0:512], s8, vg, True, True, [s8k, vgk], [pk])
                xm, xmk = Xm.next()
                P.tt("dve", xm, pt[0:8, 0:512], smask, ALU.mult, [pk, "smask"], [xmk])
                last = (n == NS - 1 and g == NBR - 1)
                P.mm(psNum[0:n4, 0:512], I4bc[0:8, n, :], xm, first, last, ["I4bc", xmk], [kNum])
                P.mm(psDen[0:n4, 0:8], I4bc[:, n, :], s8, first, last, ["I4bc", s8k], [kDen])
                first = False
        P.cp("dve", num4, psNum[0:n4, 0:512], [kNum], ["num4"])
        P.cp("dve", den4, psDen[0:n4, 0:8], [kDen], ["den4"])
        e0b = sml[:, 10, :].unsqueeze(2).to_broadcast([n4, 8, 64])
        P.tt("dve", tmpc[:, 0:512].rearrange("p (h d) -> p h d", h=8), v4a.rearrange("p (h d) -> p h d", h=8), e0b, ALU.mult,
             ["proj4", "sml"], ["tmpc"])
        P.tt("dve", num4, num4, tmpc[:, 0:512], ALU.add, ["num4", "tmpc"], ["num4"])
        P.tt("dve", den4, den4, sml[:, 10, :], ALU.add, ["den4", "sml"], ["den4"])
        P.recip(den4, den4, ["den4"], ["den4"])
        P.tt("dve", cat4[:, 512:1024].rearrange("p (h d) -> p h d", h=8), num4.rearrange("p (h d) -> p h d", h=8),
             den4.unsqueeze(2).to_broadcast([n4, 8, 64]), ALU.mult, ["num4", "den4"], ["cat4"])
        pt, pk = pp.next()
        ptc = pt[:, 0:32].rearrange("p (c m) -> p c m", c=8)
        for c in range(8):
            P.tr(ptc[:, c, :], cat4[:, c * 128:(c + 1) * 128], I4, ["cat4", "identf"], [pk])
        P.cp("dve", catb, ptc, [pk], ["catb"])
        P.dma("sp", cat_smp, catb, "Scats", reads=["catb"], writes=["cats%d" % c for c in range(8)])
        dbg_store("cat4", cat4, "cat4")

    if "p3" in phases:
        new_phase()
        phase_helpers()
        xring = H["xring"]
        hTr = Ring(A, "hT3", 2, [128, KC, 512], BF16)
        stg = Ring(A, "cstg", 2, [128, 516], F32)
        halo = ta([128, 12, 4], F32)
        ybr = Ring(A, "yb", 2, [128, 512], F32)
        ysil = ta([128, 8, 512], F32)
        sqr = Ring(A, "sq", 2, [128, 512], BF16)
        lrs = Ring(A, "lrs", 2, [128, 512], F32)
        QKr = Ring(A, "QKn", 2, [128, 8, 512], BF16)
        VTr = Ring(A, "VTn", 2, [128, 4, 512], BF16)
        nwzr = Ring(A, "nwz", 2, [128, 4, 512], BF16)
        zsr = Ring(A, "zs", 2, [128, 512], F32)
        bar_ = Ring(A, "ba", 2, [128, 4, 8], F32)
        bgr = Ring(A, "bg", 2, [128, 3, 4, 4], F32)
        odTr = Ring(A, "odT", 2, [128, 4, 512], BF16)
        smr = Ring(A, "sm", 8, [128, 8, 4], F32)
        S32 = ta([128, 4, 128], F32)
        Sb = ta([128, 4, 128], BF16)
        cw = ta([128, 12, 4], F32)
        cwt = ysil[0:4, 0:3, :].rearrange("p a b -> p (a b)")
        nrm_bc = ta([128, 128], F32)
        nA_bc = ta([128, 4], F32)
        dtb_bc = ta([128, 4], F32)
        qsc = ta([128, 1], F32)
        wba = ta([128, KC, 8], BF16)
        pp = PsPool(pst[6:8], psk[6:8])
        trp.t = pp.t; trp.k = pp.k

        g0 = [P.dma("sp", cwt, conv_w, "C3", writes=["ysil"]),
              P.dma("sp", nrm_bc, dn_norm.partition_broadcast(128), "C3", writes=["nrm_bc"]),
              P.dma("sp", nA_bc, a_log.partition_broadcast(128), "C3", writes=["nA_bc"]),
              P.dma("sp", dtb_bc, dt_bias.partition_broadcast(128), "C3", writes=["dtb_bc"])]
        P.group(g0)
        P.act(nA_bc, nA_bc, AF.Exp, ["nA_bc"], ["nA_bc"])
        P.ts("dve", nA_bc, nA_bc, -1.0, ALU.mult, ["nA_bc"], ["nA_bc"])
        P.memset("dve", qsc, float(np.log(128.0 ** -0.5)), ["qsc"])
        P.memset("dve", halo, 0.0, ["halo"])
        P.memset("dve", S32, 0.0, ["S32"])
        P.memset("dve", Sb, 0.0, ["Sb"])
        pt, pk = pp.next()
        ptv = pt[:, 0:48].rearrange("p (c i) -> p c i", i=4)
        for cc in range(12):
            P.tr(ptv[:, cc, :], cwt[:, cc * 128:(cc + 1) * 128], identf[0:4, 0:4], ["ysil", "identf"], [pk])
        P.cp("dve", cw, ptv, [pk], ["cw"])
        while any(j_[0] == "win_s%d" % G_BA for j_ in cast_jobs):
            cast_some(1)
        P.dma("sp", wba, win_s[G_BA, :, :, 0:8], "c14", reads=[wres["win_s%d" % G_BA]], writes=["wba"])

        def bc_h(ap):
            return ap.unsqueeze(2).to_broadcast([128, 4, 128])

        def bc_m(ap):
            return ap.unsqueeze(1).to_broadcast([128, 4, 128])

        tiles = {}

        def prep(j):
            hT, hTk = hTr.next()
            QK, QKk = QKr.next(); VTn, VTk = VTr.next(); nwz, nwzk = nwzr.next()
            ba, bak = bar_.next(); bg, bgk = bgr.next()
            tiles[j] = (QK, QKk, VTn, VTk, nwz, nwzk, bg, bgk)
            for sub in range(4):
                xt, xk = xring.next()
                r0 = j * 512 + sub * 128
                P.dma("sp", xt, x[r0:r0 + 128, :], "L" + xk, writes=[xk])
                norm_T(xt, xk, 128, lnm_bc, "lnm_bc", hT[:, :, sub * 128:(sub + 1) * 128], hTk)
            yield
            for grp in range(3):
                wt, wk = wload(win_s, G_QD + grp, "win_s%d" % (G_QD + grp))
                for c4 in range(4):
                    cc = grp * 4 + c4
                    pt, pk = pp.next()
                    for kc in range(KC):
                        P.mm(pt[:, 0:512], wt[:, kc, c4 * 128:(c4 + 1) * 128], hT[:, kc, :], kc == 0, kc == KC - 1, [wk, hTk], [pk])
                    sg, sgk = stg.next()
                    P.cp("dve", sg[:, 0:3], halo[:, cc, 0:3], ["halo"], [sgk])
                    P.cp("act", sg[:, 3:515], pt[:, 0:512], [pk], [sgk])
                    P.cp("dve", halo[:, cc, 0:3], sg[:, 512:515], [sgk], ["halo"])
                    yb, ybk = ybr.next()
                    P.act(yb, sg[:, 0:512], AF.Identity, [sgk, "cw"], [ybk], scale=cw[:, cc, 0:1])
                    for i in range(1, 4):
                        P.stt(yb, sg[:, i:i + 512], cw[:, cc, i:i + 1], yb, ALU.mult, ALU.add, [sgk, "cw", ybk], [ybk])
                    if grp == 2:
                        P.act(VTn[:, c4, :], yb, AF.Silu, [ybk], [VTk])
                    else:
                        P.act(ysil[:, cc, :], yb, AF.Silu, [ybk], ["ysil"])
                yield
            if j == NT - 1:
                for grp in range(3):
                    pt, pk = pp.next()
                    for c4 in range(4):
                        P.tr(pt[0:3, c4 * 128:(c4 + 1) * 128], halo[:, grp * 4 + c4, 0:3], identf, ["halo", "identf"], [pk])
                    pz, pzk = zsr.next()
                    P.cp("dve", pz[0:3, :], pt[0:3, 0:512], [pk], [pzk])
                    P.dma("sp", p_conv[:, grp * 512:(grp + 1) * 512], pz[0:3, :], "S" + pzk, reads=[pzk])
            wt, wk = wload(win_s, G_Z, "win_s%d" % G_Z)
            for sub in range(4):
                pt, pk = pp.next()
                ts_ = slice(sub * 128, (sub + 1) * 128)
                for kc in range(KC):
                    P.mm(pt[:, 0:512], hT[:, kc, ts_], wt[:, kc, :], kc == 0, kc == KC - 1, [wk, hTk], [pk])
                zs, zsk = zsr.next()
                P.act(zs, pt[:, 0:512], AF.Silu, [pk], [zsk])
                P.tt("pool", nwz[:, :, ts_].rearrange("p h t -> p h t"), zs.rearrange("p (h v) -> p h v", h=4), bc_m(nrm_bc), ALU.mult,
                     [zsk, "nrm_bc"], [nwzk])
                pt2, pk2 = pp.next()
                for kc in range(KC):
                    P.mm(pt2[:, 0:8], hT[:, kc, ts_], wba[:, kc, :], kc == 0, kc == KC - 1, ["wba", hTk], [pk2])
                P.cp("dve", ba[:, sub, :], pt2[:, 0:8], [pk2], [bak])
            yield
            for cc in range(8):
                sq, sqk = sqr.next()
                P.tt("pool", sq, ysil[:, cc, :], ysil[:, cc, :], ALU.mult, ["ysil"], [sqk])
                pt, pk = pp.next()
                P.mm(pt[:, 0:512], onesb, sq, True, True, ["onesb", sqk], [pk])
                lr, lrk = lrs.next()
                P.act(lr, pt[:, 0:512], AF.Ln, [pk, "epsc"], [lrk], bias=epsc[:, 0:1])
                if cc < 4:
                    P.act(lr, lr, AF.Exp, [lrk, "qsc"], [lrk], scale=-0.5, bias=qsc[:, 0:1])
                else:
                    P.act(lr, lr, AF.Exp, [lrk], [lrk], scale=-0.5)
                P.tt("dve", QK[:, cc, :], ysil[:, cc, :], lr, ALU.mult, ["ysil", lrk], [QKk])
                if cc == 3:
                    yield
            yield
            P.act(bg[:, 2], ba[:, :, 0:4], AF.Exp, [bak], [bgk], scale=-1.0)
            P.ts("dve", bg[:, 2], bg[:, 2], 1.0, ALU.add, [bgk], [bgk])
            P.recip(bg[:, 0], bg[:, 2], [bgk], [bgk])
            P.tt("dve", bg[:, 1], ba[:, :, 4:8], dtb_bc.unsqueeze(1).to_broadcast([128, 4, 4]), ALU.add, [bak, "dtb_bc"], [bgk])
            P.act(bg[:, 1], bg[:, 1], AF.Exp, [bgk], [bgk])
            P.act(bg[:, 1], bg[:, 1], AF.Ln, [bgk, "onesf"], [bgk], bias=onesf[:, 0:1])
            P.tt("dve", bg[:, 1], bg[:, 1], nA_bc.unsqueeze(1).to_broadcast([128, 4, 4]), ALU.mult, [bgk, "nA_bc"], [bgk])

        KCTX = 2
        ctxs = []
        for ci in range(KCTX):
            cx = {}
            for nm in ("Mm", "Mo64", "Mo128", "Nn", "Qa", "Pa", "R0", "R1", "QKm", "QsT", "Kt", "rK", "vb"):
                cx[nm] = (ta([128, 4, 128], BF16), "cx%d_%s" % (ci, nm))
            for nm in ("gU", "aL", "aU", "eg"):
                cx[nm] = (ta([128, 4, 128], F32), "cx%d_%s" % (ci, nm))
            ctxs.append(cx)

        def v4(pt):
            return pt[:, 0:512].rearrange("p (h f) -> p h f", h=4)

        seq_done = {}

        def chunk(c):
            j, sub = c // 4, c % 4
            cs = slice(sub * 128, (sub + 1) * 128)
            ci = c % KCTX
            cx = ctxs[ci]
            if False:
                yield
            bA, bB, bC = pst[3 * ci], pst[3 * ci + 1], pst[3 * ci + 2]
            kA, kB, kC = psk[3 * ci], psk[3 * ci + 1], psk[3 * ci + 2]
            QK, QKk, VTn, VTk, nwz, nwzk, bg, bgk = tiles[j]
            qT = QK[:, 0:4, cs]; kT = QK[:, 4:8, cs]; vT = VTn[:, :, cs]
            beta = bg[:, 0, sub, :]; g = bg[:, 1, sub, :]
            sm, smk = smr.next()
            CbK = bC[:].bitcast(BF16)[:, 0:512].rearrange("p (h f) -> p h f", h=4)
            Gc = bC[:, 256:260]
            gU, gUk = cx["gU"]
            ghi, ghik = cx["Nn"]; glo, glok = cx["Pa"]
            P.tt("pool", gU, bc_m(umat), bc_h(g), ALU.mult, ["umat", bgk], [gUk])
            P.cp("pool", ghi, gU, [gUk], [ghik])
            P.tt("pool", glo, gU, ghi, ALU.subtract, [gUk, ghik], [glok])
            G2 = v4(bA); KKp = v4(bB)
            for h in range(4):
                P.mm(G2[:, h, :], onesb, ghi[:, h, :], True, False, ["onesb", ghik], [kA])
                P.mm(G2[:, h, :], onesb, glo[:, h, :], False, False, ["onesb", glok], [kA])
                P.mm(G2[:, h, :], ghi[:, h, :], negonesb, False, False, ["negonesb", ghik], [kA])
                P.mm(G2[:, h, :], glo[:, h, :], negonesb, False, True, ["negonesb", glok], [kA])
            for h in range(4):
                P.mm(KKp[:, h, :], kT[:, h, :], kT[:, h, :], True, True, [QKk], [kB])
            for h in range(4):
                P.tr(CbK[:, h, :], kT[:, h, :], identb, [QKk, "identb"], [kC])
            P.mm(Gc, umat, g, True, True, ["umat", bgk], [kC])
            yield
            P.cp("dve", sm[:, 0, :], Gc, [kC], [smk])
            aL, aLk = cx["aL"]; aU, aUk = cx["aU"]; eg, egk = cx["eg"]
            P.tt("dve", aL, G2, bc_m(maskA), ALU.add, [kA, "maskA"], [aLk])
            P.tt("dve", aU, G2, bc_m(maskC), ALU.add, [kA, "maskC"], [aUk])
            P.cp("dve", sm[:, 1, :], G2[:, :, 127], [kA], [smk])
            for h in range(4):
                P.act(eg[:, h, :], G2[:, h, :], AF.Exp, [kA, smk], [egk], bias=sm[:, 0, h:h + 1])
            P.act(aL, aL, AF.Exp, [aLk], [aLk], scale=-1.0)
            P.act(aU, aU, AF.Exp, [aUk], [aUk])
            P.act(sm[:, 2, :], sm[:, 1, :], AF.Exp, [smk], [smk])
            P.act(sm[:, 3, :], sm[:, 0, :], AF.Exp, [smk], [smk])
            P.tt("dve", sm[:, 4, :], sm[:, 3, :], beta, ALU.mult, [smk, bgk], [smk])
            P.tt("dve", sm[:, 5, :], sm[:, 1, :], sm[:, 0, :], ALU.add, [smk], [smk])
            P.act(sm[:, 5, :], sm[:, 5, :], AF.Exp, [smk], [smk])
            yield
            P.tt("pool", aL, aL, bc_h(beta), ALU.mult, [aLk, bgk], [aLk])
            Mm, Mk = cx["Mm"]; QKm, QKmk = cx["QKm"]; QsT, QsTk = cx["QsT"]
            rK, rKk = cx["rK"]; Kt, Ktk = cx["Kt"]; vb, vbk = cx["vb"]
            Mf, Mfk = cx["Qa"]
            Mo64, Mo64k = cx["Mo64"]; Mo128, Mo128k = cx["Mo128"]
            P.tt("dve", Mf, KKp, aL, ALU.mult, [kB, aLk], [Mfk])
            P.tt("pool", Mm, Mf, bc_m(bmask[:, 0, :]), ALU.mult, [Mfk, "bmask"], [Mk])
            P.tt("pool", Mo64, Mf, bc_m(bmask[:, 1, :]), ALU.mult, [Mfk, "bmask"], [Mo64k])
            P.tt("pool", Mo128, Mf, bc_m(bmask[:, 2, :]), ALU.mult, [Mfk, "bmask"], [Mo128k])
            ktk_, ktkk = cx["Pa"]
            P.cp("act", ktk_, CbK, [kC], [ktkk])
            P.tt("pool", rK, ktk_, bc_h(sm[:, 4, :]), ALU.mult, [ktkk, smk], [rKk])
            P.tt("pool", Kt, ktk_, bc_h(sm[:, 2, :]), ALU.mult, [ktkk, smk], [Ktk])
            P.tt("pool", QsT, qT, eg, ALU.mult, [QKk, egk], [QsTk])
            if c == 0:
                dbg_store("M", Mf, Mfk)
            yield
            Nb = bA[:].bitcast(BF16)[:, 0:512].rearrange("p (h f) -> p h f", h=4)
            QKp = v4(bB)
            for h in range(4):
                P.tr(Nb[:, h, :], Mm[:, h, :], identb, [Mk, "identb"], [kA])
            for h in range(4):
                P.mm(QKp[:, h, :], kT[:, h, :], qT[:, h, :], True, True, [QKk], [kB])
            for h in range(4):
                P.tr(CbK[:, h, :], vT[:, h, :], identb, [VTk, "identb"], [kC])
            yield
            Nn, Nk = cx["Nn"]
            Rbuf = [cx["R0"], cx["R1"]]
            Rr, Rk = Rbuf[0]
            P.cp("act", Nn, Nb, [kA], [Nk])
            P.tt("pool", Rr, bc_m(identb), Nn, ALU.subtract, ["identb", Nk], [Rk])
            P.tt("dve", QKm, QKp, aU, ALU.mult, [kB, aUk], [QKmk])
            P.tt("dve", vb, CbK, bc_h(beta), ALU.mult, [kC, bgk], [vbk])
            if c == 0:
                dbg_store("aL", aL, aLk); dbg_store("aU", aU, aUk); dbg_store("eg", eg, egk)
                dbg_store("QKm", QKm, QKmk); dbg_store("rK", rK, rKk); dbg_store("Kt", Kt, Ktk); dbg_store("vb", vb, vbk)
                dbg_store("qT", qT, QKk); dbg_store("kT", kT, QKk)
                dbg_store("sm", sm[:, 0:6, :], smk, dbgout["sm"][:, 0:6, :] if "sm" in dbgout else None); dbg_store("bg", bg, bgk)
            yield
            Pbuf = [cx["Nn"], cx["Pa"]]; Qbuf = [cx["Mm"], cx["Qa"]]
            Pc, Pck = Pbuf[0]; Qc, Qck = Qbuf[0]
            Qp = v4(bA); Pp = v4(bB); Rp = v4(bC)
            pend = None
            NLEV = 4
            for lev in range(1, NLEV + 2):
                if lev <= NLEV:
                    if lev < NLEV:
                        for h in range(4):
                            P.mm(Pp[:, h, :], Qc[:, h, :], Pc[:, h, :], True, True, [Qck, Pck], [kB])
                    for h in range(4):
                        P.mm(Qp[:, h, :], Pc[:, h, :], Qc[:, h, :], True, True, [Qck, Pck], [kA])
                if pend is not None:
                    for h in range(4):
                        P.mm(Rp[:, h, :], pend[0][:, h, :], Rr[:, h, :], True, True, [pend[1], Rk], [kC])
                yield
                if pend is not None:
                    Rn, Rnk = Rbuf[(lev - 1) % 2]
                    P.tt("dve", Rn, Rp, Rr, ALU.add, [kC, Rk], [Rnk])
                    Rr, Rk = Rn, Rnk
                    pend = None
                if lev <= NLEV:
                    Qn, Qnk = Qbuf[lev % 2]
                    P.cp("dve", Qn, Qp, [kA], [Qnk])
                    if lev < NLEV:
                        Pn, Pnk = Pbuf[lev % 2]
                        P.cp("act", Pn, Pp, [kB], [Pnk])
                        Pc, Pck = Pn, Pnk
                    Qc, Qck = Qn, Qnk
                    pend = (Qn, Qnk)
                yield
            RTb = bA[:].bitcast(BF16)[:, 0:512].rearrange("p (h f) -> p h f", h=4)
            Yp = v4(bB); Xp = v4(bC)
            for (Mo, Mok) in ((Mo64, Mo64k), (Mo128, Mo128k)):
                for h in range(4):
                    P.tr(RTb[:, h, :], Rr[:, h, :], identb, [Rk, "identb"], [kA])
                for h in range(4):
                    P.mm(Yp[:, h, :], Mo[:, h, :], Rr[:, h, :], True, True, [Mok, Rk], [kB])
                yield
                Rm, Rmk = cx["Nn"]; Yb, Ybk = cx["Pa"]
                P.cp("act", Rm, RTb, [kA], [Rmk])
                P.cp("act", Yb, Yp, [kB], [Ybk])
                yield
                for h in range(4):
                    P.mm(Xp[:, h, :], Rm[:, h, :], Yb[:, h, :], True, True, [Rmk, Ybk], [kC])
                yield
                Rn, Rnk = Rbuf[1] if Rr is Rbuf[0][0] else Rbuf[0]
                P.stt(Rn, Xp, -1.0, Rr, ALU.mult, ALU.add, [kC, Rk], [Rnk])
                Rr, Rk = Rn, Rnk
                yield
            Up = v4(bA); Wp = v4(bB)
            for h in range(4):
                P.mm(Up[:, h, :], Rr[:, h, :], vb[:, h, :], True, True, [Rk, vbk], [kA])
            for h in range(4):
                P.mm(Wp[:, h, :], rK[:, h, :], Rr[:, h, :], True, True, [Rk, rKk], [kB])
            yield
            us, usk = cx["gU"]; WT, WTk = cx["Qa"]
            P.cp("act", us, Up, [kA], [usk])
            P.cp("dve", WT, Wp, [kB], [WTk])
            if c == 0:
                dbg_store("R", Rr, Rk); dbg_store("us", us, usk); dbg_store("WT", WT, WTk)
            yield
            while c > 0 and not seq_done.get(c - 1):
                yield
            WSp = v4(bC)
            for h in range(4):
                P.mm(WSp[:, h, :], WT[:, h, :], Sb[:, h, :], True, True, [WTk, "Sb"], [kC])
            yield
            vn, vnk = cx["Nn"]
            P.stt(vn, WSp, -1.0, us, ALU.mult, ALU.add, [kC, usk], [vnk])
            yield
            Op = v4(bA); Snp = v4(bB)
            for h in range(4):
                P.mm(Op[:, h, :], QsT[:, h, :], Sb[:, h, :], True, False, [QsTk, "Sb"], [kA])
                P.mm(Op[:, h, :], QKm[:, h, :], vn[:, h, :], False, True, [QKmk, vnk], [kA])
            for h in range(4):
                P.mm(Snp[:, h, :], Kt[:, h, :], vn[:, h, :], True, True, [Ktk, vnk], [kB])
            yield
            P.tt("pool", S32, S32, bc_h(sm[:, 5, :]), ALU.mult, ["S32", smk], ["S32"])
            P.tt("dve", S32, Snp, S32, ALU.add, [kB, "S32"], ["S32"])
            P.cp("act", Sb, S32, ["S32"], ["Sb"])
            seq_done[c] = True
            junk = H["junk"]
            for h in range(4):
                P.act(junk[:, h * 128:(h + 1) * 128], Op[:, h, :], AF.Square, [kA], ["junk", smk], accum_out=sm[:, 6, h:h + 1])
            P.act(sm[:, 7, :], sm[:, 6, :], AF.Ln, [smk, "epsc"], [smk], scale=1.0 / 128, bias=epsc[:, 0:1])
            P.act(sm[:, 7, :], sm[:, 7, :], AF.Exp, [smk], [smk], scale=-0.5)
            t1, t1k = cx["aL"]
            P.tt("dve", t1, Op, bc_h(sm[:, 7, :]), ALU.mult, [kA, smk], [t1k])
            od, odk = cx["Pa"]
            P.tt("pool", od, t1, nwz[:, :, cs], ALU.mult, [t1k, nwzk], [odk])
            yield
            Tb = bC[:].bitcast(BF16)[:, 0:512].rearrange("p (h f) -> p h f", h=4)
            for h in range(4):
                P.tr(Tb[:, h, :], od[:, h, :], identb, [odk, "identb"], [kC])
            yield
            if sub == 0:
                tiles[("odT", j)] = odTr.next()
            oT, oTk = tiles[("odT", j)]
            P.cp("act", oT[:, :, cs], Tb, [kC], [oTk])
            if sub == 3:
                P.dma("sp", cat_s.rearrange("c p t -> p c t")[:, 0:4, j * 512:(j + 1) * 512], oT, "S" + oTk, reads=[oTk],
                      writes=["cat%d_%d" % (h, j) for h in range(4)])
                for h in range(4):
                    dbg_store("odT", oT[:, h, :], oTk, dbgout["odT"][h, :, j * 512:(j + 1) * 512] if "odT" in dbgout else None,
                              "DBGodT%d_%d" % (h, j))
            if c == T // 128 - 1:
                P.dma("sp", p_ssm.rearrange("h k v -> k h v"), S32, "Spssm", reads=["S32"])

        NCH = T // 128
        PSTEP = 5
        next_prep = 0; prep_gen = None; cur_prep = -1; prep_done = set()
        chunk_next = 0; active = []; done_chunks = set(); step = 0
        while len(done_chunks) < NCH:
            if prep_gen is None and next_prep < NT and all(cc_ in done_chunks for cc_ in range(0, 4 * (next_prep - 1))):
                prep_gen = prep(next_prep); cur_prep = next_prep; next_prep += 1
            while (len(active) < KCTX and chunk_next < NCH and (chunk_next // 4) in prep_done
                   and (chunk_next < KCTX or (chunk_next - KCTX) in done_chunks)):
                active.append((chunk_next, chunk(chunk_next))); chunk_next += 1
            nxt = []
            for (cid, g_) in active:
                try:
                    next(g_)
                    nxt.append((cid, g_))
                except StopIteration:
                    done_chunks.add(cid)
            active = nxt
            if prep_gen is not None and (not active or step % PSTEP == 0):
                try:
                    next(prep_gen)
                except StopIteration:
                    prep_done.add(cur_prep); prep_gen = None
            step += 1

    if "p4" in phases:
        new_phase()
        phase_helpers()
        xring = H["xring"]; junk = H["junk"]
        lnf_bc = ta([128, D], F32)
        lnz_bc = ta([128, D], F32)
        g0 = [P.dma("sp", lnf_bc, ln_ffn.partition_broadcast(128), "C4", writes=["lnf_bc"]),
              P.dma("sp", lnz_bc, ln_final.partition_broadcast(128), "C4", writes=["lnz_bc"])]
        P.group(g0)
        if "in_cat" in dbgin:
            for c in range(8):
                P.dma("pool", cat_s[c], dbgin["in_cat"][c], "DBGcat%d" % c,
                      writes=["cat%d_%d" % (c, j) for j in range(NT)], max_dma_last_dim=2048)
                P.dma("pool", cat_smp[:, c, :], dbgin["in_cat"][c, :, T:T + NS], "DBGcats%d" % c, writes=["cats%d" % c])
        for i_ in range(2):
            wring.t.append(ta([128, KC, 512], BF16)); wring.k.append("wrx%d" % i_)
        catr = Ring(A, "cat", 2, [128, KC, 512], BF16)
        x1r = Ring(A, "x1", 2, [128, 4, D], F32)
        h2r = Ring(A, "h2T", 2, [128, KC, 512], BF16)
        rr = Ring(A, "relu", 3, [128, 512], F32)
        aT = ta([128, 32, 512], BF16)
        mixp = PsPool(pst[0:3], psk[0:3])
        cat_v = cat_s.rearrange("c p t -> p c t")

        def p4_tile(tok0, ntok, xsrc, ydst, catkeys):
            subs = [(s0, min(128, ntok - s0)) for s0 in range(0, ntok, 128)]
            ct, ctk = catr.next(); x1, x1k = x1r.next(); h2, h2k = h2r.next()
            if tok0 >= T:
                P.dma("sp", ct[:, :, 0:ntok], cat_smp, "L" + ctk, reads=catkeys, writes=[ctk])
            else:
                P.dma("sp", ct[:, :, 0:ntok], cat_v[:, :, tok0:tok0 + ntok], "L" + ctk, reads=catkeys, writes=[ctk])
            wos = [wload(wout_s, half, "wout_s%d" % half) for half in range(2)]
            for si, (s0, nt) in enumerate(subs):
                xt, xk = xring.next()
                P.dma("sp", xt[:nt], xsrc[s0:s0 + nt, :], "L" + xk, writes=[xk])
                for half in range(2):
                    wt, wk = wos[half]
                    pt, pk = mixp.next()
                    for kc in range(KC):
                        P.mm(pt[:nt, 0:512], ct[:, kc, s0:s0 + nt], wt[:, kc, :], kc == 0, kc == KC - 1, [ctk, wk], [pk])
                    cs = slice(half * 512, (half + 1) * 512)
                    P.tt("dve", x1[:nt, si, cs], pt[:nt, 0:512], xt[:nt, cs], ALU.add, [pk, xk], [x1k + "_%d" % si])
                norm_T(x1[:, si, :], x1k + "_%d" % si, nt, lnf_bc, "lnf_bc", h2[:, :, s0:s0 + nt], h2k)
            yield
            for g in range(8):
                wt, wk = wload(wup_s, g, "wup_s%d" % g)
                for fl in range(4):
                    pt, pk = mixp.next()
                    for kc in range(KC):
                        P.mm(pt[:, 0:ntok], wt[:, kc, fl * 128:(fl + 1) * 128], h2[:, kc, 0:ntok], kc == 0, kc == KC - 1,
                             [wk, h2k], [pk])
                    r_, rk = rr.next()
                    P.act(r_[:, 0:ntok], pt[:, 0:ntok], AF.Relu, [pk], [rk])
                    P.tt("pool", aT[:, g * 4 + fl, 0:ntok], r_[:, 0:ntok], r_[:, 0:ntok], ALU.mult, [rk], ["aT"])
            for half in range(2):
                cs = slice(half * 512, (half + 1) * 512)
                for g in range(4):
                    wt, wk = wload(wdn_s, half * 4 + g, "wdn_s%d" % (half * 4 + g))
                    for fl in range(8):
                        fc = g * 8 + fl
                        for si, (s0, nt) in enumerate(subs):
                            P.mm(pst[3 + si][:nt, 0:512], aT[:, fc, s0:s0 + nt], wt[:, fl, :], fc == 0, fc == 31,
                                 [wk, "aT"], [psk[3 + si]])
                for si, (s0, nt) in enumerate(subs):
                    P.tt("dve", x1[:nt, si, cs], pst[3 + si][:nt, 0:512], x1[:nt, si, cs], ALU.add,
                         [psk[3 + si], x1k + "_%d" % si], [x1k + "_%d" % si])
            for si, (s0, nt) in enumerate(subs):
                st, sk = stat.next()
                k1 = x1k + "_%d" % si
                P.act(junk[:nt], x1[:nt, si, :], AF.Square, [k1], ["junk", sk], accum_out=st[:nt, 0:1])
                rstd_act(st, sk, nt, 1.0 / D)
                P.stt(x1[:nt, si, :], x1[:nt, si, :], st[:nt, 2:3], lnz_bc[:nt], ALU.mult, ALU.mult, [k1, sk, "lnz_bc"], [k1])
            yo = [P.dma("sp", ydst[s0:s0 + nt, :], x1[:nt, si, :], "S" + x1k, reads=[x1k + "_%d" % si])
                  for si, (s0, nt) in enumerate(subs)]
            P.group(yo)

        gens = []
        for j in range(NT):
            gens.append(p4_tile(j * 512, 512, x[j * 512:(j + 1) * 512, :], y[j * 512:(j + 1) * 512, :],
                                ["cat%d_%d" % (c, j) for c in range(8)]))
        if "ps" in phases or "p4s" in phases:
            gens.append(p4_tile(T, NS, xs_in, ys_out, ["cats%d" % c for c in range(8)]))
        run_pipeline(gens)

    print("SBUF peak bytes/partition:", A.peak, "ops:", len(P.all))
    P.emit()
    return nc


_BRANCHES = ((128, 1), (512, 4), (2048, 16))
_NC_CACHE = {}


def kernel(x_prompt, x_sample, state_conv, state_ssm, cache_win_k, cache_win_v, ln_mix, w_in, dn_conv_w, dn_a_log,
           dn_dt_bias, dn_norm, w_out, ln_ffn, w_up=None, w_down=None, ln_final=None, w_ffn_up=None, w_ffn_down=None,
           _phases=("p1", "p2", "p3", "p4", "ps")):
    if w_up is None:
        w_up = w_ffn_up
    if w_down is None:
        w_down = w_ffn_down
    f = lambda a: np.ascontiguousarray(np.asarray(a, dtype=np.float32))
    x_prompt = f(x_prompt); x_sample = f(x_sample)
    B, T, _ = x_prompt.shape
    NSALL = x_sample.shape[0]
    NCORE = 8
    NS = NSALL // NCORE
    LC = cache_win_k.shape[2]
    consts = host_consts(_BRANCHES)
    nc = build(T=T, branches=_BRANCHES, NS=NS, LC=LC, phases=_phases)
    shared = {
        "ln_mix": f(ln_mix).reshape(1, D), "w_in": f(w_in).reshape(D, IN_DIM), "conv_w": f(dn_conv_w).reshape(4, 1536),
        "a_log": f(dn_a_log).reshape(1, 4), "dt_bias": f(dn_dt_bias).reshape(1, 4), "dn_norm": f(dn_norm).reshape(1, 128),
        "w_out": f(w_out).reshape(D, D), "ln_ffn": f(ln_ffn).reshape(1, D), "w_up": f(w_up).reshape(D, DFF),
        "w_down": f(w_down).reshape(DFF, D), "ln_final": f(ln_final).reshape(1, D),
    }
    for k, v in consts.items():
        shared["c_" + k] = v
    sc = f(state_conv)[0]; ss = f(state_ssm)[0]
    ckk = f(cache_win_k)[0].reshape(NSALL, LC, 512); cvv = f(cache_win_v)[0].reshape(NSALL, LC, 512)
    in_maps = []
    for c in range(NCORE):
        m = dict(shared)
        sl = slice(c * NS, (c + 1) * NS)
        m["x"] = x_prompt[c]
        m["xs"] = np.ascontiguousarray(x_sample[sl, 0, :])
        m["sconv"] = np.ascontiguousarray(sc[sl]); m["sssm"] = np.ascontiguousarray(ss[sl])
        m["ck"] = np.ascontiguousarray(ckk[sl]); m["cv"] = np.ascontiguousarray(cvv[sl])
        in_maps.append(m)
    res = run_bass_kernel_spmd(nc, in_maps, core_ids=list(range(NCORE))).results
    KEEP = min(2048, T)
    cat = lambda name: np.stack([np.asarray(r[name]) for r in res], axis=0)
    y_prompt = cat("y")
    y_sample = cat("ys").reshape(NSALL, 1, D)
    p_conv = cat("p_conv")[None]
    p_ssm = cat("p_ssm")[None]
    p_k = cat("p_k").reshape(1, B, KEEP, 8, 64)
    p_v = cat("p_v").reshape(1, B, KEEP, 8, 64)
    s_conv = cat("s_conv").reshape(1, NSALL, 3, 1536)
    s_ssm = cat("s_ssm").reshape(1, NSALL, 4, 128, 128)
    s_k = cat("s_k").reshape(1, NSALL, LC, 8, 64)
    s_v = cat("s_v").reshape(1, NSALL, LC, 8, 64)
    return (y_prompt, y_sample, p_conv, p_ssm, p_k, p_v, s_conv, s_ssm, s_k, s_v)
```
